# Optimizing a Trainium2 kernel written in Bass

```python
import math
import jax, jax.numpy as jnp
from jax import lax
import numpy as np

D_MODEL = 2048
BATCH = 4
SEQ = 4096
DEPTH = 2

D_FF = 5632
HEAD_DIM = 128
CONV_CH = 1024
CONV_WIDTH = 31
ATT_HEADS = 8
ATT_DIM = ATT_HEADS * HEAD_DIM
DIL_PATTERNS = ((128, 1), (512, 4), (2048, 16))
Q_BLOCK = 128
ROPE_THETA = 10000.0
HGRN_HEADS = 16
HGRN_KDIM = 128
HGRN_VDIM = 128
HGRN_WIDTH = HGRN_HEADS * HGRN_KDIM
CHUNK = 64
EPS = 1e-6
N_EVEN = (DEPTH + 1) // 2
N_ODD = DEPTH // 2
EVEN_IN = 2 * CONV_CH + 3 * ATT_DIM
EVEN_OUT = CONV_CH + ATT_DIM
ODD_IN = 4 * HGRN_WIDTH

kernel_name = "hybrid_conformer_dilated_hgrn2_macaron"

F32 = jnp.float32


def rmsnorm(x, g):
    xf = x.astype(F32)
    y = xf * lax.rsqrt(jnp.mean(xf * xf, axis=-1, keepdims=True) + EPS)
    return (y * g.astype(F32)).astype(x.dtype)


def layernorm(x, g, b):
    xf = x.astype(F32)
    mu = jnp.mean(xf, axis=-1, keepdims=True)
    xc = xf - mu
    var = jnp.mean(xc * xc, axis=-1, keepdims=True)
    return (xc * lax.rsqrt(var + EPS) * g.astype(F32) + b.astype(F32)).astype(x.dtype)


def swiglu(x, wg, wu, wd):
    return (jax.nn.silu(x @ wg) * (x @ wu)) @ wd


def rope(x, pos):
    half = HEAD_DIM // 2
    inv = jnp.exp(-math.log(ROPE_THETA) * jnp.arange(half, dtype=F32) / half)
    ang = pos.astype(F32)[:, None] * inv[None, :]
    cos = jnp.cos(ang)[None, :, None, :]
    sin = jnp.sin(ang)[None, :, None, :]
    xf = x.astype(F32)
    x1, x2 = xf[..., :half], xf[..., half:]
    return jnp.concatenate([x1 * cos - x2 * sin, x2 * cos + x1 * sin], axis=-1).astype(x.dtype)


def dilated_branch(q, k, v, dil, reach):
    b, s, h, hd = q.shape
    L = s // dil
    nb = -(-L // Q_BLOCK)
    Lp = nb * Q_BLOCK

    def to_classes(t):
        return t.reshape(b, L, dil, h, hd).transpose(0, 2, 3, 1, 4)

    qc, kc, vc = to_classes(q), to_classes(k), to_classes(v)
    pad_q = [(0, 0)] * 3 + [(0, Lp - L), (0, 0)]
    pad_kv = [(0, 0)] * 3 + [(Q_BLOCK, Lp - L), (0, 0)]
    qb = jnp.pad(qc, pad_q).reshape(b, dil, h, nb, Q_BLOCK, hd)
    kp = jnp.pad(kc, pad_kv).reshape(b, dil, h, nb + 1, Q_BLOCK, hd)
    vp = jnp.pad(vc, pad_kv).reshape(b, dil, h, nb + 1, Q_BLOCK, hd)
    kb = jnp.concatenate([kp[:, :, :, :-1], kp[:, :, :, 1:]], axis=-2)
    vb = jnp.concatenate([vp[:, :, :, :-1], vp[:, :, :, 1:]], axis=-2)
    scores = jnp.einsum('bchnqd,bchnkd->bchnqk', qb, kb, preferred_element_type=F32)
    qi = jnp.arange(Q_BLOCK)[:, None]
    km = jnp.arange(2 * Q_BLOCK)[None, :]
    dist = Q_BLOCK + qi - km
    band = (dist >= 0) & (dist <= reach)
    blk = jnp.arange(nb)[:, None, None]
    valid = band[None] & ((blk * Q_BLOCK + km[None] - Q_BLOCK) >= 0)
    scores = jnp.where(valid, scores, -jnp.inf)
    mx = jnp.max(scores, axis=-1, keepdims=True)
    p = jnp.exp(scores - mx)
    den = jnp.sum(p, axis=-1, keepdims=True)
    o = jnp.einsum('bchnqk,bchnkd->bchnqd', p, vb.astype(F32)) / den
    lse = (mx + jnp.log(den))[..., 0]
    o = o.reshape(b, dil, h, Lp, hd)[:, :, :, :L].transpose(0, 3, 1, 2, 4).reshape(b, s, h, hd)
    lse = lse.reshape(b, dil, h, Lp)[..., :L].transpose(0, 3, 1, 2).reshape(b, s, h)
    return o, lse


def conv_attn_mixer(hn, pos, w_in, conv_w, conv_b, cn_g, cn_b, qn_g, kn_g, w_out):
    b, s, _ = hn.shape
    u = hn @ w_in
    a_val = u[..., :CONV_CH]
    a_gate = u[..., CONV_CH:2 * CONV_CH]
    o0 = 2 * CONV_CH
    q = u[..., o0:o0 + ATT_DIM].reshape(b, s, ATT_HEADS, HEAD_DIM)
    k = u[..., o0 + ATT_DIM:o0 + 2 * ATT_DIM].reshape(b, s, ATT_HEADS, HEAD_DIM)
    v = u[..., o0 + 2 * ATT_DIM:].reshape(b, s, ATT_HEADS, HEAD_DIM)
    a = a_val * jax.nn.sigmoid(a_gate)
    a = lax.conv_general_dilated(
        a, conv_w[:, None, :].astype(a.dtype), window_strides=(1,),
        padding=[(CONV_WIDTH - 1, 0)], dimension_numbers=('NWC', 'WIO', 'NWC'),
        feature_group_count=CONV_CH) + conv_b
    a = jax.nn.silu(layernorm(a, cn_g, cn_b))
    scale = HEAD_DIM ** -0.5
    q = (rope(rmsnorm(q, qn_g), pos).astype(F32) * scale).astype(hn.dtype)
    k = rope(rmsnorm(k, kn_g), pos)
    outs, lses = [], []
    for window, dil in DIL_PATTERNS:
        o_i, l_i = dilated_branch(q, k, v, dil, window // dil)
        outs.append(o_i)
        lses.append(l_i)
    wts = jax.nn.softmax(jnp.stack(lses, axis=0), axis=0)
    o = jnp.einsum('pbsh,pbshd->bshd', wts, jnp.stack(outs, axis=0))
    o = o.reshape(b, s, ATT_DIM).astype(hn.dtype)
    return jnp.concatenate([a, o], axis=-1) @ w_out


def hgrn2_mixer(hn, w_in, lb, gn_g, w_out):
    b, s, _ = hn.shape
    u = (hn @ w_in).astype(F32)
    qz = u[..., :HGRN_WIDTH]
    fz = u[..., HGRN_WIDTH:2 * HGRN_WIDTH]
    iz = u[..., 2 * HGRN_WIDTH:3 * HGRN_WIDTH]
    gz = u[..., 3 * HGRN_WIDTH:]
    lb = lb.astype(F32)
    q = jax.nn.silu(qz)
    logf = jnp.logaddexp(jnp.log(lb), jnp.log1p(-lb) + jax.nn.log_sigmoid(fz))
    kk = (1.0 - lb) * jax.nn.sigmoid(-fz)
    nc = s // CHUNK

    def heads(t, d):
        return t.reshape(b, nc, CHUNK, HGRN_HEADS, d).transpose(1, 0, 3, 2, 4)

    qh, kh = heads(q, HGRN_KDIM), heads(kk, HGRN_KDIM)
    vh, gh = heads(iz, HGRN_VDIM), heads(logf, HGRN_KDIM)
    bcum = jnp.cumsum(gh, axis=-2)
    causal = jnp.tril(jnp.ones((CHUNK, CHUNK), dtype=bool))

    def step(S, xs):
        qc, kc, vc, bc = xs
        diff = bc[:, :, :, None, :] - bc[:, :, None, :, :]
        decay = jnp.exp(jnp.where(causal[:, :, None], diff, -jnp.inf))
        attn = jnp.einsum('bhtd,bhsd,bhtsd->bhts', qc, kc, decay)
        o = jnp.einsum('bhts,bhsv->bhtv', attn, vc) + \
            jnp.einsum('bhtd,bhdv->bhtv', qc * jnp.exp(bc), S)
        blast = bc[:, :, -1:, :]
        S = jnp.exp(blast[:, :, 0, :])[..., None] * S + \
            jnp.einsum('bhsd,bhsv->bhdv', kc * jnp.exp(blast - bc), vc)
        return S, o

    S0 = jnp.zeros((b, HGRN_HEADS, HGRN_KDIM, HGRN_VDIM), F32)
    _, o = lax.scan(step, S0, (qh, kh, vh, bcum))
    o = o.transpose(1, 0, 3, 2, 4).reshape(b, s, HGRN_HEADS, HGRN_VDIM)
    o = o * lax.rsqrt(jnp.mean(o * o, axis=-1, keepdims=True) + EPS)
    o = o.reshape(b, s, HGRN_WIDTH) * gn_g.astype(F32) * jax.nn.silu(gz)
    return o.astype(hn.dtype) @ w_out


def setup_inputs(seed: int = 0) -> dict:
    key = jax.random.key(seed)
    ks = iter(jax.random.split(key, 32))

    def nrm(shape, fan_in):
        return jax.random.normal(next(ks), shape, F32) * (fan_in ** -0.5)

    def gain(shape):
        return 1.0 + 0.02 * jax.random.normal(next(ks), shape, F32)

    def bias(shape):
        return 0.01 * jax.random.normal(next(ks), shape, F32)

    return {
        "x": jax.random.normal(next(ks), (BATCH, SEQ, D_MODEL), F32),
        "norm_ffn1": gain((DEPTH, D_MODEL)),
        "ffn1_wg": nrm((DEPTH, D_MODEL, D_FF), D_MODEL),
        "ffn1_wu": nrm((DEPTH, D_MODEL, D_FF), D_MODEL),
        "ffn1_wd": nrm((DEPTH, D_FF, D_MODEL), D_FF),
        "norm_mix": gain((DEPTH, D_MODEL)),
        "norm_ffn2": gain((DEPTH, D_MODEL)),
        "ffn2_wg": nrm((DEPTH, D_MODEL, D_FF), D_MODEL),
        "ffn2_wu": nrm((DEPTH, D_MODEL, D_FF), D_MODEL),
        "ffn2_wd": nrm((DEPTH, D_FF, D_MODEL), D_FF),
        "ev_w_in": nrm((N_EVEN, D_MODEL, EVEN_IN), D_MODEL),
        "ev_conv_w": nrm((N_EVEN, CONV_WIDTH, CONV_CH), CONV_WIDTH),
        "ev_conv_b": bias((N_EVEN, CONV_CH)),
        "ev_cn_g": gain((N_EVEN, CONV_CH)),
        "ev_cn_b": bias((N_EVEN, CONV_CH)),
        "ev_qn_g": gain((N_EVEN, HEAD_DIM)),
        "ev_kn_g": gain((N_EVEN, HEAD_DIM)),
        "ev_w_out": nrm((N_EVEN, EVEN_OUT, D_MODEL), EVEN_OUT),
        "od_w_in": nrm((N_ODD, D_MODEL, ODD_IN), D_MODEL),
        "od_lb_logits": jax.random.normal(next(ks), (DEPTH, HGRN_WIDTH), F32),
        "od_gn_g": gain((N_ODD, HGRN_WIDTH)),
        "od_w_out": nrm((N_ODD, HGRN_WIDTH, D_MODEL), HGRN_WIDTH),
    }


def reference(x, norm_ffn1, ffn1_wg, ffn1_wu, ffn1_wd, norm_mix, norm_ffn2, ffn2_wg,
              ffn2_wu, ffn2_wd, ev_w_in, ev_conv_w, ev_conv_b, ev_cn_g, ev_cn_b,
              ev_qn_g, ev_kn_g, ev_w_out, od_w_in, od_lb_logits, od_gn_g, od_w_out):
    pos = jnp.arange(x.shape[1], dtype=jnp.int32)
    p = jax.nn.softmax(od_lb_logits.astype(F32), axis=0)
    lower_bounds = jnp.cumsum(p, axis=0) - p[0:1]
    for l in range(DEPTH):
        j = l // 2
        x = x + 0.5 * swiglu(rmsnorm(x, norm_ffn1[l]), ffn1_wg[l], ffn1_wu[l], ffn1_wd[l])
        hn = rmsnorm(x, norm_mix[l])
        if l % 2 == 0:
            x = x + conv_attn_mixer(hn, pos, ev_w_in[j], ev_conv_w[j], ev_conv_b[j],
                                    ev_cn_g[j], ev_cn_b[j], ev_qn_g[j], ev_kn_g[j], ev_w_out[j])
        else:
            x = x + hgrn2_mixer(hn, od_w_in[j], lower_bounds[l], od_gn_g[j], od_w_out[j])
        x = x + 0.5 * swiglu(rmsnorm(x, norm_ffn2[l]), ffn2_wg[l], ffn2_wu[l], ffn2_wd[l])
    return x
```

```python
import numpy as np
from contextlib import ExitStack
import concourse.bass as bass
import concourse.mybir as mybir
from concourse.bass_utils import run_bass_kernel_spmd

F32 = mybir.dt.float32
BF16 = mybir.dt.bfloat16
AF = mybir.ActivationFunctionType
ALU = mybir.AluOpType

D = 2048
KC = D // 128
DFF = 5632
FC = DFF // 128
NT = 2048
NCORES = 8
EPS = 1e-6


class Sem:
    def __init__(self, es, nc, name):
        self.h = es.enter_context(nc.semaphore(name))
        self.n = 0
        self.name = name


class Prog:
    def __init__(self, G, es, pfx):
        self.G = G
        self.nc = G.nc
        self.es = es
        self.pfx = pfx
        self.q = {k: [] for k in ("sync", "scalar", "vector", "tensor", "gpsimd")}

    def sem(self, name):
        if name not in self.G.sems:
            self.G.sems[name] = Sem(self.G.es, self.nc, name)
        return self.G.sems[name]

    def sbuf(self, name, shape, dt):
        return self.es.enter_context(self.nc.sbuf_tensor(f"sb_{self.pfx}_{name}", list(shape), dt))

    def psum(self, name, shape, dt):
        return self.es.enter_context(self.nc.psum_tensor(f"ps_{self.pfx}_{name}", list(shape), dt))

    def op(self, eng, fn, waits=(), sig=None, inc=1):
        v = None
        auto = getattr(self, "auto", {})
        if eng in auto:
            asem = auto[eng]
            if sig is None:
                sig = asem
            if sig is asem and asem.n > 0:
                waits = list(waits) + [(asem, asem.n)]
        if sig is not None:
            sig.n += inc
            v = sig.n
            assert v < 60000, (sig.name, v)
        waits = [(s, val) for (s, val) in waits if s is not None and val is not None and val > 0]

        def run(e, fn=fn, waits=waits, sig=sig, inc=inc):
            for (s, val) in waits:
                e.wait_ge(s.h, val)
            ins = fn(e)
            if sig is not None:
                ins.then_inc(sig.h, inc)

        self.q[eng].append(run)
        return v

    def dma(self, eng, out, in_, waits=(), sig=None):
        return self.op(eng, lambda e: e.dma_start(out=out, in_=in_), waits, sig, 16)

    def wait(self, eng, waits):
        waits = [(s, val) for (s, val) in waits if val is not None and val > 0]

        def run(e):
            for (s, val) in waits:
                e.wait_ge(s.h, val)

        self.q[eng].append(run)

    def emit(self):
        nc = self.nc
        with nc.Block() as block:
            @block.sync
            def _(e):
                for f in self.q["sync"]:
                    f(e)

            @block.scalar
            def _(e):
                for f in self.q["scalar"]:
                    f(e)

            @block.vector
            def _(e):
                for f in self.q["vector"]:
                    f(e)

            @block.tensor
            def _(e):
                for f in self.q["tensor"]:
                    f(e)

            @block.gpsimd
            def _(e):
                for f in self.q["gpsimd"]:
                    f(e)


class Glob:
    def __init__(self, nc, es):
        self.nc = nc
        self.es = es
        self.sems = {}


def run_phase(G, pfx, body):
    with ExitStack() as es:
        P = Prog(G, es, pfx)
        body(P)
        P.emit()
    G.nc.all_engine_barrier()


class Ring:
    def __init__(self, bufs):
        self.bufs = bufs
        self.free = [None] * len(bufs)
        self.i = 0

    def next(self):
        k = self.i % len(self.bufs)
        self.i += 1
        return k


def emit_stageA(P, R, x_in_v, t0, TS, v_gain, dst_off=0):
    a_done = []
    for a in range(TS // 256):
        ta = t0 + a * 256
        ab = R["xa_ring"].next()
        xa = R["xa"][ab]
        sq = R["sq"][ab]
        v_ld = P.dma("gpsimd", xa[:], x_in_v[:, :, ta:ta + 256],
                     waits=R["xa_free"][ab], sig=R["s_io"])
        v_sq = P.op("scalar",
                    lambda e, xa=xa, sq=sq: e.activation(out=sq[:], in_=xa[:], func=AF.Square),
                    waits=[(R["s_io"], v_ld)] + R["sq_free"][ab], sig=R["s_act"])
        for kc in range(KC):
            w = [(R["s_act"], v_sq)] + R["psA_free"] if kc == 0 else []
            v_mm = P.op("tensor",
                        lambda e, kc=kc, sq=sq: e.matmul(R["psA"][:, 0:256], R["ones"][:], sq[:, kc, :],
                                                         start=(kc == 0), stop=(kc == KC - 1)),
                        waits=w, sig=(R["s_pe"] if kc == KC - 1 else None))
        R["sq_free"][ab] = [(R["s_pe"], v_mm)]
        v_sd = P.op("scalar",
                    lambda e: e.activation(out=R["sd"][:], in_=R["psA"][:, 0:256], func=AF.Sqrt,
                                           bias=R["epsb"][:], scale=1.0),
                    waits=[(R["s_pe"], v_mm)] + R["rstd_free"], sig=R["s_act"])
        R["psA_free"] = [(R["s_act"], v_sd)]
        v_r = P.op("vector",
                   lambda e: e.reciprocal(out=R["rstd"][:], in_=R["sd"][:]),
                   waits=[(R["s_act"], v_sd)], sig=R["s_dve"])
        for kc in range(KC):
            w = [(R["s_dve"], v_r), (R["s_io"], v_gain)] + R["xn_free"] if kc == 0 else []
            v_x = P.op("vector",
                       lambda e, kc=kc, xa=xa, ta=ta, t0=t0, xnT=R["xnT"]: e.scalar_tensor_tensor(
                           out=xnT[:, kc, dst_off + ta - t0:dst_off + ta - t0 + 256], in0=xa[:, kc, :],
                           scalar=R["gain"][:, kc:kc + 1], in1=R["rstd"][:],
                           op0=ALU.mult, op1=ALU.mult),
                       waits=w, sig=(R["s_dve"] if kc == KC - 1 else None))
        R["xa_free"][ab] = [(R["s_dve"], v_x)]
        R["rstd_free"] = [(R["s_dve"], v_x)]
        a_done = [(R["s_dve"], v_x)]
    R["xn_free"] = []
    return a_done

def emit_ffn(P, R, x_in, x_out, gain, wg, wu, wd, TS, nt=NT, final_waits=None):
    nsup = nt // TS
    nth = TS // 512
    x_in_v = x_in.rearrange("(kc p) t -> p kc t", p=128)
    x_out_v = x_out.rearrange("(kc p) t -> p kc t", p=128)

    v_gain = P.dma("gpsimd", R["gain"][:], gain, waits=R["gain_free"], sig=R["s_io"])
    R["gain_free"] = []

    def stageA(s):
        b = s % 2
        R["xnT"] = R["xnT2"][b]
        R["xn_free"] = R["xn2_free"][b]
        return emit_stageA(P, R, x_in_v, s * TS, TS, v_gain)

    def prefetch_up(fc):
        wb = R["w_ring"].next()
        conv_v = []
        for (wsrc, wdst) in ((wg, R["wbf_g"][wb]), (wu, R["wbf_u"][wb])):
            sb = R["stg_ring"].next()
            stg = R["stg"][sb]
            v_l = P.dma("sync", stg[:, 0:KC * 128], wsrc[fc], waits=R["stg_free"][sb], sig=R["s_stg"][sb])
            v_c = P.op("scalar",
                       lambda e, stg=stg, wdst=wdst: e.activation(
                           out=wdst[:].rearrange("p k f -> p (k f)"), in_=stg[:, 0:KC * 128], func=AF.Copy),
                       waits=[(R["s_stg"][sb], v_l)] + R["wbf_free"][wb], sig=R["s_act"])
            R["stg_free"][sb] = [(R["s_act"], v_c)]
            conv_v.append(v_c)
        R["wbf_free"][wb] = []
        return wb, conv_v[-1]

    def prefetch_dn(dc):
        db = R["wd_ring"].next()
        wdb = R["wbf_d"][db]
        conv_v = None
        for q4 in range(4):
            sb = R["stg_ring"].next()
            stg = R["stg"][sb]
            n = 11 * 128
            v_l = P.dma("sync", stg[:, 0:n], wd[dc][:, q4 * n:(q4 + 1) * n],
                        waits=R["stg_free"][sb], sig=R["s_stg"][sb])
            v_c = P.op("scalar",
                       lambda e, stg=stg, wdb=wdb, q4=q4, n=n: e.activation(
                           out=wdb[:, q4 * 11:(q4 + 1) * 11, :].rearrange("p k f -> p (k f)"),
                           in_=stg[:, 0:n], func=AF.Copy),
                       waits=[(R["s_stg"][sb], v_l)] + (R["wd_free"][db] if q4 == 0 else []), sig=R["s_act"])
            R["stg_free"][sb] = [(R["s_act"], v_c)]
            conv_v = v_c
        R["wd_free"][db] = []
        return db, conv_v

    xn_ready = stageA(0)
    nxt = prefetch_up(0)
    for s in range(nsup):
        t0 = s * TS
        xb = s % 2
        xnT = R["xnT2"][xb]
        xn_ready_next = None

        last_up_mm = None
        h_done = None
        nxt_d = None
        for fc in range(FC):
            wb, conv_last = nxt
            if fc + 1 < FC:
                nxt = prefetch_up(fc + 1)
            else:
                nxt_d = prefetch_dn(0)
            if fc == FC - 8 and s + 1 < nsup:
                xn_ready_next = stageA(s + 1)
            for th in range(nth):
                pb = R["pu_ring"].next()
                ps_g, ps_u = R["ps_g"][pb], R["ps_u"][pb]
                first = True
                for (wt, ps) in ((R["wbf_g"][wb], ps_g), (R["wbf_u"][wb], ps_u)):
                    for kc in range(KC):
                        w = []
                        if first:
                            w = [(R["s_act"], conv_last)] + xn_ready + R["pu_free"][pb]
                            first = False
                        last = (wt is R["wbf_u"][wb] and kc == KC - 1)
                        v_mm = P.op("tensor",
                                    lambda e, wt=wt, ps=ps, kc=kc, th=th, xnT=xnT: e.matmul(
                                        ps[:], wt[:, kc, :], xnT[:, kc, th * 512:(th + 1) * 512],
                                        start=(kc == 0), stop=(kc == KC - 1)),
                                    waits=w, sig=(R["s_pe"] if last else None))
                last_up_mm = v_mm
                sg = R["sg"][pb]
                v_s = P.op("scalar",
                           lambda e, sg=sg, ps_g=ps_g: e.activation(out=sg[:], in_=ps_g[:], func=AF.Silu),
                           waits=[(R["s_pe"], v_mm)] + R["sg_free"][pb], sig=R["s_act"])
                v_h = P.op("vector",
                           lambda e, sg=sg, ps_u=ps_u, fc=fc, th=th: e.tensor_tensor(
                               out=R["hT"][:, fc, th * 512:(th + 1) * 512], in0=ps_u[:], in1=sg[:], op=ALU.mult),
                           waits=[(R["s_act"], v_s)] + (R["h_free"] if (fc == 0 and th == 0) else []),
                           sig=R["s_dve"])
                R["pu_free"][pb] = [(R["s_dve"], v_h)]
                R["sg_free"][pb] = [(R["s_dve"], v_h)]
                h_done = v_h
            R["wbf_free"][wb] = [(R["s_pe"], last_up_mm)]
        R["h_free"] = []
        R["xn2_free"][xb] = [(R["s_pe"], last_up_mm)]

        last_dn_mm = None
        for dc in range(KC):
            db, conv_v = nxt_d
            wdb = R["wbf_d"][db]
            if dc + 1 < KC:
                nxt_d = prefetch_dn(dc + 1)
            elif s + 1 < nsup:
                nxt = prefetch_up(0)
            for th in range(nth):
                yb = R["py_ring"].next()
                ps_y = R["ps_y"][yb]
                tt = t0 + th * 512
                rb = R["xr_ring"].next()
                xr = R["xr"][rb]
                v_xr = P.dma("gpsimd", xr[:], x_in_v[:, dc, tt:tt + 512], waits=R["xr_free"][rb], sig=R["s_io"])
                for fc in range(FC):
                    w = []
                    if fc == 0:
                        w = [(R["s_act"], conv_v), (R["s_dve"], h_done)] + R["py_free"][yb]
                    v_mm = P.op("tensor",
                                lambda e, wdb=wdb, ps_y=ps_y, fc=fc, th=th: e.matmul(
                                    ps_y[:], wdb[:, fc, :], R["hT"][:, fc, th * 512:(th + 1) * 512],
                                    start=(fc == 0), stop=(fc == FC - 1)),
                                waits=w, sig=(R["s_pe"] if fc == FC - 1 else None))
                last_dn_mm = v_mm
                v_o = P.op("vector",
                           lambda e, xr=xr, ps_y=ps_y: e.scalar_tensor_tensor(
                               out=xr[:], in0=ps_y[:], scalar=0.5, in1=xr[:], op0=ALU.mult, op1=ALU.add),
                           waits=[(R["s_pe"], v_mm), (R["s_io"], v_xr)], sig=R["s_dve"])
                R["py_free"][yb] = [(R["s_dve"], v_o)]
                v_st = P.dma("gpsimd", x_out_v[:, dc, tt:tt + 512], xr[:], waits=[(R["s_dve"], v_o)], sig=R["s_out"])
                R["xr_free"][rb] = [(R["s_out"], v_st)]
            R["wd_free"][db] = [(R["s_pe"], last_dn_mm)]
        R["h_free"] = [(R["s_pe"], last_dn_mm)]
        if xn_ready_next is not None:
            xn_ready = xn_ready_next
    R["gain_free"] = [(R["s_dve"], R["s_dve"].n)]
    return [(R["s_out"], R["s_out"].n)]


def alloc_ffn_resources(P, TS):
    R = {}
    R["s_io"] = P.sem("s_io")
    R["s_out"] = P.sem("s_out")
    R["s_act"] = P.sem("s_act")
    R["s_dve"] = P.sem("s_dve")
    R["s_pe"] = P.sem("s_pe")
    R["s_stg"] = [P.sem(f"s_stg{i}") for i in range(3)]
    R["gain"] = P.sbuf("gain", [128, KC], F32)
    R["gain_free"] = []
    R["ones"] = P.sbuf("ones", [128, 128], BF16)
    R["xa"] = [P.sbuf(f"xa{i}", [128, KC, 256], F32) for i in range(1)]
    R["sq"] = [P.sbuf(f"sq{i}", [128, KC, 256], BF16) for i in range(1)]
    R["xa_ring"] = Ring(R["xa"])
    R["xa_free"] = [[] for _ in R["xa"]]
    R["sq_free"] = [[] for _ in R["sq"]]
    R["rstd"] = P.sbuf("rstd", [128, 256], F32)
    R["rstd_free"] = []
    R["sd"] = P.sbuf("sd", [128, 256], F32)
    R["epsb"] = P.sbuf("epsb", [128, 1], F32)
    R["xnT2"] = [P.sbuf(f"xnT{i}", [128, KC, TS], BF16) for i in range(2)]
    R["xn2_free"] = [[], []]
    R["xnT"] = R["xnT2"][0]
    R["xn_free"] = []
    R["hT"] = P.sbuf("hT", [128, FC, TS], BF16)
    R["h_free"] = []
    R["stg"] = [P.sbuf(f"stg{i}", [128, KC * 128], F32) for i in range(3)]
    R["stg_ring"] = Ring(R["stg"])
    R["stg_free"] = [[] for _ in R["stg"]]
    R["wbf_g"] = [P.sbuf(f"wbfg{i}", [128, KC, 128], BF16) for i in range(2)]
    R["wbf_u"] = [P.sbuf(f"wbfu{i}", [128, KC, 128], BF16) for i in range(2)]
    R["w_ring"] = Ring(R["wbf_g"])
    R["wbf_free"] = [[] for _ in range(2)]
    R["wbf_d"] = [P.sbuf(f"wbfd{i}", [128, FC, 128], BF16) for i in range(2)]
    R["wd_ring"] = Ring(R["wbf_d"])
    R["wd_free"] = [[] for _ in range(2)]
    R["sg"] = [P.sbuf(f"sg{i}", [128, 512], F32) for i in range(2)]
    R["sg_free"] = [[] for _ in range(2)]
    R["xr"] = [P.sbuf(f"xr{i}", [128, 512], F32) for i in range(2)]
    R["xr_ring"] = Ring(R["xr"])
    R["xr_free"] = [[] for _ in range(2)]
    R["psA"] = P.psum("psA", [128, 512], F32)
    R["psA_free"] = []
    R["ps_g"] = [P.psum(f"psg{i}", [128, 512], F32) for i in range(2)]
    R["ps_u"] = [P.psum(f"psu{i}", [128, 512], F32) for i in range(2)]
    R["pu_ring"] = Ring(R["ps_g"])
    R["pu_free"] = [[] for _ in range(2)]
    R["ps_y"] = [P.psum(f"psy{i}", [128, 512], F32) for i in range(2)]
    R["py_ring"] = Ring(R["ps_y"])
    R["py_free"] = [[] for _ in range(2)]
    P.op("vector", lambda e: e.memset(R["epsb"][:], EPS), sig=R["s_dve"])
    v = P.op("vector", lambda e: e.memset(R["ones"][:], 1.0 / D), sig=R["s_dve"])
    R["psA_free"] = [(R["s_dve"], v)]
    return R


def phase_ffn(G, pfx, x_in, x_out, gain, wg, wu, wd, TS=512, nt=NT):
    def body(P):
        R = alloc_ffn_resources(P, TS)
        fin = emit_ffn(P, R, x_in, x_out, gain, wg, wu, wd, TS, nt)
        P.wait("gpsimd", fin)
    run_phase(G, pfx, body)


def tile_w_up(w):
    return np.ascontiguousarray(
        w.reshape(KC, 128, FC, 128).transpose(2, 1, 0, 3).reshape(FC, 128, KC * 128))


def tile_w_down(w):
    return np.ascontiguousarray(
        w.reshape(FC, 128, KC, 128).transpose(2, 1, 0, 3).reshape(KC, 128, FC * 128))


def tile_gain(g):
    return np.ascontiguousarray(g.reshape(KC, 128).T)


def alloc_common(P, TS):
    R = {}
    for k in ("s_io", "s_out", "s_act", "s_dve", "s_pe", "s_pool"):
        R[k] = P.sem(k)
    R["s_stg"] = [P.sem(f"s_stg{i}") for i in range(3)]
    R["stg"] = [P.sbuf(f"stg{i}", [128, KC * 128], F32) for i in range(3)]
    R["stg_ring"] = Ring(R["stg"])
    R["stg_free"] = [[] for _ in R["stg"]]
    R["wbf"] = [P.sbuf(f"wbf{i}", [128, KC, 128], BF16) for i in range(2)]
    R["w_ring"] = Ring(R["wbf"])
    R["wbf_free"] = [[] for _ in range(2)]
    return R


def alloc_stageA(P, R, TS):
    R["gain"] = P.sbuf("gain", [128, KC], F32)
    R["gain_free"] = []
    R["ones"] = P.sbuf("ones", [128, 128], BF16)
    R["xa"] = [P.sbuf("xa0", [128, KC, 256], F32)]
    R["sq"] = [P.sbuf("sq0", [128, KC, 256], BF16)]
    R["xa_ring"] = Ring(R["xa"])
    R["xa_free"] = [[]]
    R["sq_free"] = [[]]
    R["rstd"] = P.sbuf("rstd", [128, 256], F32)
    R["rstd_free"] = []
    R["sd"] = P.sbuf("sd", [128, 256], F32)
    R["epsb"] = P.sbuf("epsb", [128, 1], F32)
    R["xnT"] = P.sbuf("xnT", [128, KC, TS], BF16)
    R["xn_free"] = []
    R["psA"] = P.psum("psA", [128, 512], F32)
    P.op("vector", lambda e: e.memset(R["epsb"][:], EPS), sig=R["s_dve"])
    v = P.op("vector", lambda e: e.memset(R["ones"][:], 1.0 / D), sig=R["s_dve"])
    R["psA_free"] = [(R["s_dve"], v)]


def emit_wload(P, R, wsrc_ap):
    wb = R["w_ring"].next()
    sb = R["stg_ring"].next()
    stg = R["stg"][sb]
    wdst = R["wbf"][wb]
    v_l = P.dma("sync", stg[:, 0:KC * 128], wsrc_ap, waits=R["stg_free"][sb], sig=R["s_stg"][sb])
    v_c = P.op("scalar",
               lambda e, stg=stg, wdst=wdst: e.activation(
                   out=wdst[:].rearrange("p k f -> p (k f)"), in_=stg[:, 0:KC * 128], func=AF.Copy),
               waits=[(R["s_stg"][sb], v_l)] + R["wbf_free"][wb], sig=R["s_act"])
    R["stg_free"][sb] = [(R["s_act"], v_c)]
    return wdst, wb, [(R["s_act"], v_c)]


def phase_normproj(G, pfx, x_in, gain, w, u, nch, nt=NT):
    nth = nt // 512

    def body(P):
        R = alloc_common(P, nt)
        alloc_stageA(P, R, nt)
        ps = [P.psum(f"pp{i}", [128, 512], F32) for i in range(4)]
        ps_free = [[] for _ in range(4)]
        osb = [P.sbuf(f"osb{i}", [128, 512], F32) for i in range(4)]
        osb_free = [[] for _ in range(4)]
        x_in_v = x_in.rearrange("(kc p) t -> p kc t", p=128)
        v_gain = P.dma("gpsimd", R["gain"][:], gain, sig=R["s_io"])
        cnt = 0
        nxt_w = emit_wload(P, R, w[0])
        for ch in range(nch):
            wt, wb, w_ready = nxt_w
            if ch + 1 < nch:
                nxt_w = emit_wload(P, R, w[ch + 1])
            for th in range(nth):
                if ch == 0:
                    xn_ready = emit_stageA(P, R, x_in_v, th * 512, 512, v_gain, dst_off=th * 512)
                pb = cnt % 4
                cnt += 1
                for kc in range(KC):
                    wts = (w_ready + xn_ready + ps_free[pb]) if kc == 0 else []
                    v_mm = P.op("tensor",
                                lambda e, wt=wt, pb=pb, kc=kc, th=th: e.matmul(
                                    ps[pb][:], wt[:, kc, :], R["xnT"][:, kc, th * 512:(th + 1) * 512],
                                    start=(kc == 0), stop=(kc == KC - 1)),
                                waits=wts, sig=(R["s_pe"] if kc == KC - 1 else None))
                if pb % 2 == 0:
                    v_c = P.op("vector", lambda e, pb=pb: e.tensor_copy(out=osb[pb][:], in_=ps[pb][:]),
                               waits=[(R["s_pe"], v_mm)] + osb_free[pb], sig=R["s_dve"])
                    cw = [(R["s_dve"], v_c)]
                else:
                    v_c = P.op("scalar", lambda e, pb=pb: e.activation(out=osb[pb][:], in_=ps[pb][:], func=AF.Copy),
                               waits=[(R["s_pe"], v_mm)] + osb_free[pb], sig=R["s_act"])
                    cw = [(R["s_act"], v_c)]
                ps_free[pb] = cw
                v_st = P.dma("gpsimd", u[ch][:, th * 512:(th + 1) * 512], osb[pb][:], waits=cw, sig=R["s_out"])
                osb_free[pb] = [(R["s_out"], v_st)]
            R["wbf_free"][wb] = [(R["s_pe"], v_mm)]
        P.wait("gpsimd", [(R["s_out"], R["s_out"].n)])
    run_phase(G, pfx, body)


def phase_outproj(G, pfx, x_in, r_in, w, x_out, nt=NT, scale=1.0):
    nth = nt // 512

    def body(P):
        R = alloc_common(P, nt)
        rT = P.sbuf("rT", [128, KC, nt], BF16)
        ps = [P.psum(f"pp{i}", [128, 512], F32) for i in range(4)]
        ps_free = [[] for _ in range(4)]
        xr = [P.sbuf(f"xr{i}", [128, 512], F32) for i in range(4)]
        xr_free = [[] for _ in range(4)]
        x_in_v = x_in.rearrange("(kc p) t -> p kc t", p=128)
        x_out_v = x_out.rearrange("(kc p) t -> p kc t", p=128)
        for kc in range(KC):
            v_r = P.dma("gpsimd", rT[:, kc, :], r_in[kc], sig=R["s_io"])
        r_ready = [(R["s_io"], v_r)]
        cnt = 0
        for dc in range(KC):
            wt, wb, w_ready = emit_wload(P, R, w[dc])
            for th in range(nth):
                pb = cnt % 4
                cnt += 1
                v_x = P.dma("gpsimd", xr[pb][:], x_in_v[:, dc, th * 512:(th + 1) * 512],
                            waits=xr_free[pb], sig=R["s_io"])
                for kc in range(KC):
                    wts = (w_ready + r_ready + ps_free[pb]) if kc == 0 else []
                    v_mm = P.op("tensor",
                                lambda e, wt=wt, pb=pb, kc=kc, th=th: e.matmul(
                                    ps[pb][:], wt[:, kc, :], rT[:, kc, th * 512:(th + 1) * 512],
                                    start=(kc == 0), stop=(kc == KC - 1)),
                                waits=wts, sig=(R["s_pe"] if kc == KC - 1 else None))
                v_o = P.op("vector",
                           lambda e, pb=pb: e.scalar_tensor_tensor(
                               out=xr[pb][:], in0=ps[pb][:], scalar=float(scale), in1=xr[pb][:],
                               op0=ALU.mult, op1=ALU.add),
                           waits=[(R["s_pe"], v_mm), (R["s_io"], v_x)], sig=R["s_dve"])
                ps_free[pb] = [(R["s_dve"], v_o)]
                v_st = P.dma("gpsimd", x_out_v[:, dc, th * 512:(th + 1) * 512], xr[pb][:],
                             waits=[(R["s_dve"], v_o)], sig=R["s_out"])
                xr_free[pb] = [(R["s_out"], v_st)]
            R["wbf_free"][wb] = [(R["s_pe"], v_mm)]
        P.wait("gpsimd", [(R["s_out"], R["s_out"].n)])
    run_phase(G, pfx, body)


def tile_w(w):
    n = w.shape[1] // 128
    return np.ascontiguousarray(w.reshape(KC, 128, n, 128).transpose(2, 1, 0, 3).reshape(n, 128, KC * 128))


CW = 31
NA = 8


def phase_even_conv(G, pfx, u, tail, flagf, cw, cvec, out, nt=NT):
    nth = nt // 512

    def body(P):
        R = {k: P.sem(k) for k in ("s_io", "s_out", "s_act", "s_dve", "s_pe")}
        s_ld = [P.sem("s_ld0"), P.sem("s_ld1")]
        P.auto = {"vector": R["s_dve"], "scalar": R["s_act"]}
        Y = P.sbuf("Y", [128, NA, nt], F32)
        ve = [P.sbuf(f"ve{i}", [128, 32 + nt], F32) for i in range(2)]
        ge = [P.sbuf(f"ge{i}", [128, 32 + nt], F32) for i in range(2)]
        ao = P.sbuf("ao", [128, NA, nt], BF16)
        cw_sb = P.sbuf("cw_sb", [128, NA, CW], F32)
        cv_sb = P.sbuf("cv_sb", [128, 3, NA], F32)
        ones = P.sbuf("ones", [128, 128], BF16)
        epsb = P.sbuf("epsb", [128, 1], F32)
        ybf = [P.sbuf(f"ybf{i}", [128, 512], BF16) for i in range(2)]
        ysq = [P.sbuf(f"ysq{i}", [128, 512], BF16) for i in range(2)]
        mu = P.sbuf("mu", [128, 512], F32)
        tt = P.sbuf("tt", [128, 512], F32)
        psS = P.psum("psS", [128, 512], F32)
        psQ = P.psum("psQ", [128, 512], F32)
        P.op("vector", lambda e: e.memset(epsb[:], EPS))
        P.op("vector", lambda e: e.memset(ones[:], 1.0 / (NA * 128)))
        fl_sb = P.sbuf("fl_sb", [128, 1], F32)
        P.dma("sync", cw_sb[:], cw, sig=R["s_io"])
        P.dma("sync", fl_sb[:], flagf, sig=R["s_io"])
        v_c = P.dma("sync", cv_sb[:], cvec, sig=R["s_io"])
        c_ready = [(R["s_io"], v_c)]
        buf_free = [[], []]
        for c in range(NA):
            b = c % 2
            P.dma("sync", ve[b][:, 0:32], tail[c], waits=buf_free[b], sig=s_ld[b])
            P.dma("sync", ve[b][:, 32:32 + nt], u[c], sig=s_ld[b])
            P.dma("sync", ge[b][:, 0:32], tail[NA + c], sig=s_ld[b])
            v_l = P.dma("sync", ge[b][:, 32:32 + nt], u[NA + c], sig=s_ld[b])
            v_s = P.op("scalar", lambda e, b=b: e.activation(out=ge[b][:], in_=ge[b][:], func=AF.Sigmoid),
                       waits=[(s_ld[b], v_l)])
            P.op("vector", lambda e, b=b: e.tensor_tensor(out=ve[b][:], in0=ve[b][:], in1=ge[b][:], op=ALU.mult),
                 waits=[(R["s_act"], v_s)] + c_ready)
            P.op("vector", lambda e, b=b: e.tensor_scalar_mul(out=ve[b][:, 0:32], in0=ve[b][:, 0:32], scalar1=fl_sb[:, 0:1]))
            P.op("vector", lambda e, b=b, c=c: e.tensor_scalar(
                out=Y[:, c, :], in0=ve[b][:, 2:2 + nt], scalar1=cw_sb[:, c, 0:1], scalar2=cv_sb[:, 0, c:c + 1],
                op0=ALU.mult, op1=ALU.add))
            for j in range(1, CW):
                v_y = P.op("vector", lambda e, b=b, c=c, j=j: e.scalar_tensor_tensor(
                    out=Y[:, c, :], in0=ve[b][:, 2 + j:2 + j + nt], scalar=cw_sb[:, c, j:j + 1], in1=Y[:, c, :],
                    op0=ALU.mult, op1=ALU.add))
            buf_free[b] = [(R["s_dve"], v_y)]
        y_ready = [(R["s_dve"], v_y)]
        ps_free = []
        yb_free = [[], []]
        cnt = 0
        for th in range(nth):
            sl = slice(th * 512, (th + 1) * 512)
            for c in range(NA):
                k = cnt % 2
                cnt += 1
                P.op("scalar", lambda e, k=k, c=c, sl=sl: e.activation(out=ybf[k][:], in_=Y[:, c, sl], func=AF.Copy),
                     waits=y_ready + yb_free[k])
                v_a = P.op("scalar", lambda e, k=k, c=c, sl=sl: e.activation(out=ysq[k][:], in_=Y[:, c, sl], func=AF.Square))
                P.op("tensor", lambda e, k=k, c=c: e.matmul(psS[:], ones[:], ybf[k][:], start=(c == 0), stop=(c == NA - 1)),
                     waits=[(R["s_act"], v_a)] + (ps_free if c == 0 else []))
                v_m = P.op("tensor", lambda e, k=k, c=c: e.matmul(psQ[:], ones[:], ysq[k][:], start=(c == 0), stop=(c == NA - 1)),
                           sig=R["s_pe"])
                yb_free[k] = [(R["s_pe"], v_m)]
            v_mu = P.op("scalar", lambda e: e.activation(out=mu[:], in_=psS[:], func=AF.Copy), waits=[(R["s_pe"], v_m)])
            P.op("vector", lambda e: e.tensor_tensor(out=tt[:], in0=mu[:], in1=mu[:], op=ALU.mult), waits=[(R["s_act"], v_mu)])
            v_v = P.op("vector", lambda e: e.tensor_tensor(out=tt[:], in0=psQ[:], in1=tt[:], op=ALU.subtract))
            ps_free = [(R["s_dve"], v_v)]
            v_sd = P.op("scalar", lambda e: e.activation(out=tt[:], in_=tt[:], func=AF.Sqrt, bias=epsb[:], scale=1.0),
                        waits=[(R["s_dve"], v_v)])
            P.op("vector", lambda e: e.reciprocal(out=tt[:], in_=tt[:]), waits=[(R["s_act"], v_sd)])
            for c in range(NA):
                P.op("vector", lambda e, c=c, sl=sl: e.tensor_tensor(out=Y[:, c, sl], in0=Y[:, c, sl], in1=mu[:], op=ALU.subtract))
                v_z = P.op("vector", lambda e, c=c, sl=sl: e.tensor_tensor(out=Y[:, c, sl], in0=Y[:, c, sl], in1=tt[:], op=ALU.mult))
                v_o = P.op("scalar", lambda e, c=c, sl=sl: e.activation(
                    out=ao[:, c, sl], in_=Y[:, c, sl], func=AF.Silu, scale=cv_sb[:, 1, c:c + 1], bias=cv_sb[:, 2, c:c + 1]),
                    waits=[(R["s_dve"], v_z)])
        for c in range(NA):
            P.dma("sync", out[c], ao[:, c, :], waits=[(R["s_act"], v_o)], sig=R["s_out"])
        P.wait("sync", [(R["s_out"], R["s_out"].n)])
    run_phase(G, pfx, body)


NH = 8
HD = 128


def phase_even_qk(G, pfx, uq, uk, uv, cs, gqk, prot, qT, kT, vT, nt=NT):
    nth = nt // 512

    def body(P):
        R = {k: P.sem(k) for k in ("s_io", "s_out", "s_act", "s_dve", "s_pe")}
        s_ld = [P.sem("s_ld0"), P.sem("s_ld1")]
        P.auto = {"vector": R["s_dve"], "scalar": R["s_act"]}
        raw = [P.sbuf(f"raw{i}", [128, nt], F32) for i in range(2)]
        sq = P.sbuf("sq", [128, nt], BF16)
        sd = P.sbuf("sd", [128, nt], F32)
        qn = P.sbuf("qn", [128, nt], BF16)
        t1 = P.sbuf("t1", [128, nt], F32)
        t2 = P.sbuf("t2", [128, nt], F32)
        ob = [P.sbuf(f"ob{i}", [128, nt], BF16) for i in range(2)]
        cs_sb = P.sbuf("cs_sb", [128, 2, nt], F32)
        g_sb = P.sbuf("g_sb", [128, 2], F32)
        prot_sb = P.sbuf("prot_sb", [128, 128], BF16)
        ones = P.sbuf("ones", [128, 128], BF16)
        epsb = P.sbuf("epsb", [128, 1], F32)
        psB = P.psum("psB", [128, nt], F32)
        psR = P.psum("psR", [128, nt], F32)
        P.op("vector", lambda e: e.memset(epsb[:], EPS))
        P.op("vector", lambda e: e.memset(ones[:], 1.0 / HD))
        P.dma("sync", cs_sb[:, 0, :], cs[0], sig=R["s_io"])
        P.dma("sync", cs_sb[:, 1, :], cs[1], sig=R["s_io"])
        P.dma("sync", prot_sb[:], prot, sig=R["s_io"])
        v_g = P.dma("sync", g_sb[:], gqk, sig=R["s_io"])
        v_gs = P.op("vector", lambda e: e.tensor_scalar_mul(out=g_sb[:, 0:1], in0=g_sb[:, 0:1], scalar1=float(HD ** -0.5)),
                    waits=[(R["s_io"], v_g)])
        raw_free = [[], []]
        ob_free = [[], []]
        psB_free, psR_free = [], []
        units = []
        for h in range(NH):
            units.append((uq[h], qT[h], 0, 0))
            units.append((uk[h], kT[h], 1, 0))
        for i, (src, dst, gi, poff) in enumerate(units):
            b = i % 2
            v_l = P.dma("sync", raw[b][:], src, waits=raw_free[b], sig=s_ld[b])
            v_sq = P.op("scalar", lambda e, b=b: e.activation(out=sq[:], in_=raw[b][:], func=AF.Square),
                        waits=[(s_ld[b], v_l)])
            for th in range(nth):
                sl = slice(th * 512, (th + 1) * 512)
                v_m = P.op("tensor", lambda e, sl=sl: e.matmul(psB[:, sl], ones[:], sq[:, sl], start=True, stop=True),
                           waits=([(R["s_act"], v_sq)] + psB_free) if th == 0 else [],
                           sig=R["s_pe"] if th == nth - 1 else None)
            v_sd = P.op("scalar", lambda e: e.activation(out=sd[:], in_=psB[:], func=AF.Sqrt, bias=epsb[:], scale=1.0),
                        waits=[(R["s_pe"], v_m)])
            psB_free = [(R["s_act"], v_sd)]
            P.op("vector", lambda e: e.reciprocal(out=sd[:], in_=sd[:]), waits=[(R["s_act"], v_sd)])
            v_qn = P.op("vector", lambda e, b=b, gi=gi: e.scalar_tensor_tensor(
                out=qn[:], in0=raw[b][:], scalar=g_sb[:, gi:gi + 1], in1=sd[:], op0=ALU.mult, op1=ALU.mult))
            raw_free[b] = [(R["s_dve"], v_qn)]
            for th in range(nth):
                sl = slice(th * 512, (th + 1) * 512)
                v_m = P.op("tensor", lambda e, sl=sl: e.matmul(psR[:, sl], prot_sb[:], qn[:, sl], start=True, stop=True),
                           waits=([(R["s_dve"], v_qn)] + psR_free) if th == 0 else [],
                           sig=R["s_pe"] if th == nth - 1 else None)
            P.op("vector", lambda e, poff=poff: e.tensor_tensor(out=t1[:], in0=qn[:], in1=cs_sb[:, 0, poff:poff + nt], op=ALU.mult))
            v_t2 = P.op("vector", lambda e, poff=poff: e.tensor_tensor(out=t2[:], in0=psR[:], in1=cs_sb[:, 1, poff:poff + nt], op=ALU.mult),
                        waits=[(R["s_pe"], v_m)])
            psR_free = [(R["s_dve"], v_t2)]
            v_o = P.op("vector", lambda e, b=b: e.tensor_tensor(out=ob[b][:], in0=t1[:], in1=t2[:], op=ALU.add),
                       waits=ob_free[b])
            v_st = P.dma("sync", dst, ob[b][:], waits=[(R["s_dve"], v_o)], sig=R["s_out"])
            ob_free[b] = [(R["s_out"], v_st)]
        i0 = len(units)
        for j in range(NH):
            h = j
            b = (i0 + j) % 2
            v_l = P.dma("sync", raw[b][:], uv[h], waits=raw_free[b], sig=s_ld[b])
            v_c = P.op("scalar", lambda e, b=b: e.activation(out=ob[b][:], in_=raw[b][:], func=AF.Copy),
                       waits=[(s_ld[b], v_l)] + ob_free[b])
            raw_free[b] = [(R["s_act"], v_c)]
            v_st = P.dma("sync", vT[h], ob[b][:], waits=[(R["s_act"], v_c)], sig=R["s_out"])
            ob_free[b] = [(R["s_out"], v_st)]
        P.wait("sync", [(R["s_out"], R["s_out"].n)])
    run_phase(G, pfx, body)


def rope_tables(pos0, n):
    half = HD // 2
    inv = np.exp(-np.log(10000.0) * np.arange(half, dtype=np.float32) / half).astype(np.float32)
    pos = (pos0 + np.arange(n)).astype(np.float32)
    ang = pos[None, :] * inv[:, None]
    c = np.cos(ang).astype(np.float32)
    s_ = np.sin(ang).astype(np.float32)
    return np.ascontiguousarray(np.stack([np.concatenate([c, c], 0), np.concatenate([s_, s_], 0)], 0))


def rot_matrix():
    import ml_dtypes
    m = np.zeros((128, 128), np.float32)
    for dp in range(64):
        m[dp + 64, dp] = -1.0
        m[dp, dp + 64] = 1.0
    return m.astype(ml_dtypes.bfloat16)


DILS = (1, 4, 16)
NEG = -30000.0


def attn_units(nt):
    slots = {}
    units = []
    for d in DILS:
        nb = (2 * nt) // d // 128
        for c in range(d):
            for n in range(nb // 2 - 1, nb):
                slots[(d, c, n)] = len(slots)
            for n in range(nb // 2, nb):
                units.append((d, c, n))
    return units, slots


def phase_even_attn(G, pfx, qT, kT_own, vT_own, kT_prev, vT_prev, ident, maskb, fones, flagf, oT, nt=NT):
    units, slots = attn_units(nt)
    nslot = len(slots)

    def body(P):
        R = {k: P.sem(k) for k in ("s_io", "s_out", "s_act", "s_dve", "s_pe", "s_tr")}
        s_ld = [P.sem("s_ld0"), P.sem("s_ld1")]
        P.auto = {"vector": R["s_dve"], "scalar": R["s_act"]}
        k_sb = [P.sbuf(f"k_sb{i}", [128, 2 * nt], BF16) for i in range(2)]
        q_sb = [P.sbuf(f"q_sb{i}", [128, nt], BF16) for i in range(2)]
        v_sb = [P.sbuf(f"v_sb{i}", [128, 2 * nt], BF16) for i in range(2)]
        Vb = [P.sbuf(f"Vb{i}", [128, nslot, 128], BF16) for i in range(2)]
        acc = P.sbuf("acc", [128, 2, nt], F32)
        rden = P.sbuf("rden", [128, nt], F32)
        ob = [P.sbuf(f"ob{i}", [128, nt], BF16) for i in range(2)]
        pT = [P.sbuf(f"pT{i}", [128, 2, 128], BF16) for i in range(2)]
        id_sb = P.sbuf("id_sb", [128, 128], BF16)
        mb_sb = P.sbuf("mb_sb", [128, 2, 128], BF16)
        fo_sb = P.sbuf("fo_sb", [128, 128], BF16)
        on_sb = P.sbuf("on_sb", [128, 128], BF16)
        psT = [P.psum(f"psT{i}", [128, 8, 128], BF16) for i in range(2)]
        psS = [P.psum(f"psS{i}", [128, 2, 128], F32) for i in range(2)]
        psO = [P.psum(f"psO{i}", [128, 2, 128], F32) for i in range(2)]
        fl_sb = P.sbuf("fl_sb", [128, 1], F32)
        v_ms = P.op("vector", lambda e: e.memset(on_sb[:], 1.0))
        P.dma("sync", fl_sb[:], flagf, sig=R["s_io"])
        P.dma("sync", id_sb[:], ident, sig=R["s_io"])
        P.dma("sync", mb_sb[:], maskb, sig=R["s_io"])
        v_c = P.dma("sync", fo_sb[:], fones, sig=R["s_io"])
        c_ready = [(R["s_io"], v_c), (R["s_dve"], v_ms)]
        in_free = [[], []]
        vb_free = [[], []]
        ob_free = [[], []]
        psT_free = [[], []]
        psS_free = [[], []]
        psO_free = [[], []]
        pT_free = [[], []]
        tcnt = 0
        ucnt = 0
        for h in range(NH):
            hb = h % 2
            P.dma("sync", k_sb[hb][:, 0:nt], kT_prev[h], waits=in_free[hb], sig=s_ld[hb])
            P.dma("sync", k_sb[hb][:, nt:2 * nt], kT_own[h], sig=s_ld[hb])
            P.dma("sync", q_sb[hb][:], qT[h], sig=s_ld[hb])
            P.dma("sync", v_sb[hb][:, 0:nt], vT_prev[h], sig=s_ld[hb])
            v_l = P.dma("sync", v_sb[hb][:, nt:2 * nt], vT_own[h], sig=s_ld[hb])
            v_fz = P.op("vector", lambda e, hb=hb: e.tensor_scalar_mul(
                out=v_sb[hb][:, 0:nt], in0=v_sb[hb][:, 0:nt], scalar1=fl_sb[:, 0:1]),
                waits=[(s_ld[hb], v_l)] + c_ready)
            h_ready = [(s_ld[hb], v_l), (R["s_dve"], v_fz)] + c_ready
            keys = list(slots.keys())
            v_ev = None
            for g0 in range(0, nslot, 8):
                grp = keys[g0:g0 + 8]
                tb = tcnt % 2
                tcnt += 1
                for j, (d, c, n) in enumerate(grp):
                    st = 128 * n * d + c
                    v_t = P.op("tensor", lambda e, tb=tb, j=j, st=st, d=d, hb=hb: e.transpose(
                        out=psT[tb][:, j, :], in_=v_sb[hb][:, st:st + 127 * d + 1:d], identity=id_sb[:]),
                        waits=(h_ready + psT_free[tb] + vb_free[hb]) if j == 0 else [],
                        sig=R["s_tr"] if j == len(grp) - 1 else None)
                v_ev = P.op("vector", lambda e, tb=tb, g0=g0, ng=len(grp), hb=hb: e.tensor_copy(
                    out=Vb[hb][:, g0:g0 + ng, :], in_=psT[tb][:, 0:ng, :]), waits=[(R["s_tr"], v_t)])
                psT_free[tb] = [(R["s_dve"], v_ev)]
            vb_ready = [(R["s_dve"], v_ev)]
            last_tr = [(R["s_tr"], v_t)]

            def emit_S(u):
                nonlocal ucnt
                d, c, n = u
                pb = ucnt % 2
                ucnt += 1
                qst = 128 * n * d + c - nt
                qsl = slice(qst, qst + 127 * d + 1, d)
                for kb in range(2):
                    kst = 128 * (n - 1 + kb) * d + c
                    P.op("tensor", lambda e, pb=pb, kb=kb, kst=kst, d=d, qsl=qsl, hb=hb: e.matmul(
                        psS[pb][:, kb, :], k_sb[hb][:, kst:kst + 127 * d + 1:d], q_sb[hb][:, qsl], start=True, stop=False),
                        waits=(h_ready + psS_free[pb]) if kb == 0 else [])
                    v_s = P.op("tensor", lambda e, pb=pb, kb=kb: e.matmul(
                        psS[pb][:, kb, :], id_sb[:], mb_sb[:, kb, :], start=False, stop=True),
                        sig=R["s_pe"] if kb == 1 else None)
                v_e = P.op("scalar", lambda e, pb=pb: e.activation(out=pT[pb][:], in_=psS[pb][:], func=AF.Exp),
                           waits=[(R["s_pe"], v_s)] + pT_free[pb])
                psS_free[pb] = [(R["s_act"], v_e)]
                return (u, pb, qsl, v_e)

            def emit_ND(st_):
                (d, c, n), pb, qsl, v_e = st_
                prev_in_prev_half = (128 * (n - 1) * d + c) < nt
                for kb in range(2):
                    sl_ = slots[(d, c, n - 1 + kb)]
                    P.op("tensor", lambda e, pb=pb, kb=kb, sl_=sl_, hb=hb: e.matmul(
                        psO[pb][:, 0, :], Vb[hb][:, sl_, :], pT[pb][:, kb, :], start=(kb == 0), stop=(kb == 1)),
                        waits=([(R["s_act"], v_e)] + vb_ready + psO_free[pb]) if kb == 0 else [])
                for kb in range(2):
                    on = fo_sb if (kb == 0 and prev_in_prev_half) else on_sb
                    v_n = P.op("tensor", lambda e, pb=pb, kb=kb, on=on: e.matmul(
                        psO[pb][:, 1, :], on[:], pT[pb][:, kb, :], start=(kb == 0), stop=(kb == 1)),
                        sig=R["s_pe"] if kb == 1 else None)
                pT_free[pb] = [(R["s_pe"], v_n)]
                if d == 1:
                    v_a = P.op("vector", lambda e, pb=pb, qsl=qsl: e.tensor_copy(out=acc[:, :, qsl], in_=psO[pb][:]),
                               waits=[(R["s_pe"], v_n)])
                else:
                    v_a = P.op("vector", lambda e, pb=pb, qsl=qsl: e.tensor_tensor(
                        out=acc[:, :, qsl], in0=psO[pb][:], in1=acc[:, :, qsl], op=ALU.add), waits=[(R["s_pe"], v_n)])
                psO_free[pb] = [(R["s_dve"], v_a)]
                return v_n

            pend = None
            v_n = None
            for u in units:
                st_ = emit_S(u)
                if pend is not None:
                    v_n = emit_ND(pend)
                pend = st_
            v_n = emit_ND(pend)
            in_free[hb] = [(R["s_pe"], v_n)] + last_tr
            vb_free[hb] = [(R["s_pe"], v_n)]
            P.op("vector", lambda e: e.reciprocal(out=rden[:], in_=acc[:, 1, :]))
            v_o = P.op("vector", lambda e, hb=hb: e.tensor_tensor(out=ob[hb][:], in0=acc[:, 0, :], in1=rden[:], op=ALU.mult),
                       waits=ob_free[hb])
            v_st = P.dma("sync", oT[h], ob[hb][:], waits=[(R["s_dve"], v_o)], sig=R["s_out"])
            ob_free[hb] = [(R["s_out"], v_st)]
        P.wait("sync", [(R["s_out"], R["s_out"].n)])
    run_phase(G, pfx, body)


def attn_consts(flag):
    import ml_dtypes
    k = np.arange(128)[:, None]
    q = np.arange(128)[None, :]
    mb = np.zeros((128, 2, 128), np.float32)
    mb[:, 0, :] = np.where(k >= q, 0.0, NEG)
    mb[:, 1, :] = np.where(k <= q, 0.0, NEG)
    return {"ident": np.eye(128, dtype=np.float32).astype(ml_dtypes.bfloat16),
            "maskb": mb.astype(ml_dtypes.bfloat16),
            "fones": np.full((128, 128), float(flag), np.float32).astype(ml_dtypes.bfloat16)}


HH = 16
CH = 64
HG_BCAST = True


def phase_hgrn(G, pfx, u, lbl, gng, ident, cmask, S0, flagf, oT, S_end, state_only, nt=NT):
    ncs = nt // CH
    full = not state_only
    LAG = 2

    def body(P):
        R = {k: P.sem(k) for k in ("s_io", "s_out", "s_act", "s_dve", "s_pe", "s_ld0", "s_ld1")}
        s_ldb = [R["s_ld0"], R["s_ld1"]]
        P.auto = {"vector": R["s_dve"], "scalar": R["s_act"]}
        nin = 2
        fz = [P.sbuf(f"fz{i}", [128, nt], F32) for i in range(nin)]
        iz = [P.sbuf(f"iz{i}", [128, nt], F32) for i in range(nin)]
        if full:
            qz = [P.sbuf(f"qz{i}", [128, nt], F32) for i in range(nin)]
            gz = [P.sbuf(f"gz{i}", [128, nt], F32) for i in range(nin)]
            e1 = P.sbuf("e1", [128, ncs, CH], F32)
            e2 = P.sbuf("e2", [128, ncs, CH], F32)
            eb = P.sbuf("eb", [128, nt], F32)
            Qt = P.sbuf("Qt", [128, nt], BF16)
            Kt = P.sbuf("Kt", [128, nt], BF16)
            Qh = [P.sbuf(f"Qh{i}", [128, nt], BF16) for i in range(2)]
            Am = P.sbuf("Am", [64, ncs, CH], BF16)
            Sb_all = P.sbuf("Sb_all", [128, ncs, 128], BF16)
            osq = P.sbuf("osq", [128, 512], BF16)
            sd = P.sbuf("sd", [128, 512], F32)
            on = P.sbuf("on", [128, 512], F32)
            ob = [P.sbuf(f"ob{i}", [128, nt], BF16) for i in range(2)]
        kk = P.sbuf("kk", [128, nt], F32)
        bb = P.sbuf("bb", [128, ncs, CH], F32)
        e3 = P.sbuf("e3", [128, ncs, CH], F32)
        el = [P.sbuf(f"el{i}", [128, ncs], F32) for i in range(2)]
        rmask = P.sbuf("rmask", [128, nt], F32)
        Kh = P.sbuf("Kh", [128, nt], BF16)
        ibf = P.sbuf("ibf", [128, nt], BF16)
        Vtok = P.sbuf("Vtok", [64, ncs, 128], BF16)
        Ktok = P.sbuf("Ktok", [64, ncs, 128], BF16)
        S = [P.sbuf(f"S{i}", [128, 128], F32) for i in range(2)]
        lb_sb = P.sbuf("lb_sb", [128, 2, HH], F32)
        lb = P.sbuf("lb", [128, HH], F32)
        oml = P.sbuf("oml", [128, HH], F32)
        gn_sb = P.sbuf("gn_sb", [128, HH], F32)
        id_sb = P.sbuf("id_sb", [128, 128], BF16)
        cm_sb = P.sbuf("cm_sb", [64, 64], F32)
        ones = P.sbuf("ones", [128, 128], BF16)
        epsb = P.sbuf("epsb", [128, 1], F32)
        fl_sb = P.sbuf("fl_sb", [128, 1], F32)
        psT = [P.psum(f"psT{i}", [64, 8, 128], BF16) for i in range(2)]
        NKB = 2
        psK_l = [P.psum(f"psK{i}", [128, 128], F32) for i in range(NKB)]
        psK_at = lambda kb: psK_l[kb][:]
        if full:
            psA = P.psum("psA", [64, 8, CH], F32)
            psO = [P.psum(f"psO{i}", [128, 512], F32) for i in range(2)]
            psN = P.psum("psN", [128, 512], F32)

        P.op("vector", lambda e: e.memset(epsb[:], EPS))
        P.op("vector", lambda e: e.memset(ones[:], 1.0 / 128))
        P.op("vector", lambda e: e.memset(rmask[:], 1.0))
        P.op("vector", lambda e: e.memset(rmask[:, 0:nt - CH + 1:CH], 0.0))
        P.dma("sync", lb_sb[:], lbl, sig=R["s_io"])
        P.dma("sync", fl_sb[:], flagf, sig=R["s_io"])
        P.dma("sync", gn_sb[:], gng, sig=R["s_io"])
        P.dma("sync", id_sb[:], ident, sig=R["s_io"])
        v_c = P.dma("sync", cm_sb[:], cmask, sig=R["s_io"])
        c_io = [(R["s_io"], v_c)]
        P.op("vector", lambda e: e.tensor_tensor(out=lb[:], in0=lb_sb[:, 1, :], in1=lb_sb[:, 0, :], op=ALU.subtract), waits=c_io)
        v_a = P.op("scalar", lambda e: e.activation(out=lb[:], in_=lb[:], func=AF.Sigmoid), waits=[(R["s_dve"], R["s_dve"].n)])
        v_c2 = P.op("vector", lambda e: e.tensor_scalar(out=oml[:], in0=lb[:], scalar1=-1.0, scalar2=1.0, op0=ALU.mult, op1=ALU.add),
                    waits=[(R["s_act"], v_a)])
        c_ready = c_io + [(R["s_dve"], v_c2)]

        in_free = [[] for _ in range(nin)]
        ob_free = [[], []]
        psT_free = [[], []]
        psA_free = []
        psO_free = [[], []]
        psK_free = [[] for _ in range(2)]
        psN_free = []
        s0_free = []
        copy_wait = [[], []]
        pe_prev = []
        tr_prev = []
        tcnt = 0
        ocnt = 0
        kcnt = 0

        def issue_loads(h):
            b = h % nin
            v = P.dma("sync", fz[b][:], u[HH + h], waits=in_free[b], sig=s_ldb[b])
            v = P.dma("sync", iz[b][:], u[2 * HH + h], sig=s_ldb[b])
            if full:
                P.dma("sync", qz[b][:], u[h], sig=s_ldb[b])
                v = P.dma("sync", gz[b][:], u[3 * HH + h], sig=s_ldb[b])
            return [(s_ldb[b], v)]

        def prologue(h, ld, info):
            b = h % nin
            fzb, izb = fz[b], iz[b]
            elb = el[h % 2]
            a1 = P.op("scalar", lambda e, fzb=fzb: e.activation(out=fzb[:], in_=fzb[:], func=AF.Sigmoid), waits=ld)
            yield
            if full:
                qzb, gzb = qz[b], gz[b]
                Qhb = Qh[h % 2]
                P.op("scalar", lambda e, qzb=qzb: e.activation(out=qzb[:], in_=qzb[:], func=AF.Silu))
                yield
                P.op("scalar", lambda e, gzb=gzb: e.activation(out=gzb[:], in_=gzb[:], func=AF.Silu))
                yield
            P.op("scalar", lambda e, izb=izb: e.activation(out=ibf[:], in_=izb[:], func=AF.Copy), waits=info["tr_prev"])
            yield
            P.op("vector", lambda e, h=h, fzb=fzb: e.tensor_scalar(out=fzb[:], in0=fzb[:], scalar1=oml[:, h:h + 1], scalar2=lb[:, h:h + 1],
                                                                   op0=ALU.mult, op1=ALU.add), waits=[(R["s_act"], a1)])
            yield
            d2 = P.op("vector", lambda e, fzb=fzb: e.tensor_scalar(out=kk[:], in0=fzb[:], scalar1=-1.0, scalar2=1.0, op0=ALU.mult, op1=ALU.add))
            yield
            a5 = P.op("scalar", lambda e, fzb=fzb: e.activation(out=fzb[:], in_=fzb[:], func=AF.Ln), waits=[(R["s_dve"], d2)])
            yield
            bb2 = bb[:].rearrange("p c t -> p (c t)")
            d3 = P.op("vector", lambda e, bb2=bb2, fzb=fzb: e.tensor_tensor_scan(out=bb2, data0=rmask[:], data1=fzb[:], initial=0.0,
                                                                               op0=ALU.mult, op1=ALU.add), waits=[(R["s_act"], a5)])
            yield
            if full:
                d4 = P.op("vector", lambda e: e.tensor_tensor(out=e1[:], in0=bb[:], in1=bb[:, :, 31:32].broadcast_to([128, ncs, CH]),
                                                              op=ALU.subtract))
                yield
            d5 = P.op("vector", lambda e: e.tensor_tensor(out=e3[:], in0=bb[:], in1=bb[:, :, CH - 1:CH].broadcast_to([128, ncs, CH]),
                                                          op=ALU.subtract))
            yield
            if full:
                P.op("scalar", lambda e, bb2=bb2: e.activation(out=eb[:], in_=bb2, func=AF.Exp), waits=[(R["s_dve"], d3)])
                yield
            P.op("scalar", lambda e, elb=elb: e.activation(out=elb[:], in_=bb[:, :, CH - 1], func=AF.Exp),
                 waits=[(R["s_dve"], d3)] + info["el_free"][h % 2])
            yield
            if full:
                P.op("scalar", lambda e: e.activation(out=e2[:], in_=e1[:], func=AF.Exp, scale=-1.0), waits=[(R["s_dve"], d4)])
                yield
                P.op("scalar", lambda e: e.activation(out=e1[:], in_=e1[:], func=AF.Exp))
                yield
            v_e = P.op("scalar", lambda e: e.activation(out=e3[:], in_=e3[:], func=AF.Exp, scale=-1.0), waits=[(R["s_dve"], d5)])
            yield
            e32 = e3[:].rearrange("p c t -> p (c t)")
            wpro = [(R["s_act"], v_e)] + info["tr_prev"] + info["a_prev"]
            if full:
                e12 = e1[:].rearrange("p c t -> p (c t)")
                e22 = e2[:].rearrange("p c t -> p (c t)")
                P.op("vector", lambda e, e12=e12, qzb=qzb: e.tensor_tensor(out=Qt[:], in0=qzb[:], in1=e12, op=ALU.mult), waits=wpro)
                yield
                P.op("vector", lambda e, e22=e22: e.tensor_tensor(out=Kt[:], in0=kk[:], in1=e22, op=ALU.mult))
                yield
                P.op("vector", lambda e, qzb=qzb, Qhb=Qhb: e.tensor_tensor(out=Qhb[:], in0=qzb[:], in1=eb[:], op=ALU.mult),
                     waits=info["qh_free"][h % 2])
                yield
            v_pro = P.op("vector", lambda e, e32=e32: e.tensor_tensor(out=Kh[:], in0=kk[:], in1=e32, op=ALU.mult), waits=wpro)
            info["pro"] = [(R["s_dve"], v_pro)]
            yield

        info = {"tr_prev": [], "a_prev": [], "el_free": [[], []], "qh_free": [[], []], "pro": None}
        ld_next = issue_loads(0)
        for _ in prologue(0, ld_next + c_ready, info):
            pass
        for h in range(HH):
            b = h % nin
            if h + 1 < HH:
                ld_next = issue_loads(h + 1)
            if full:
                gzb = gz[b]
                Qhb = Qh[h % 2]
            elb = el[h % 2]
            pro = info["pro"]
            if S0 is not None:
                v_l = P.dma("sync", S[0][:], S0[h], waits=s0_free + copy_wait[0], sig=R["s_io"])
                v_s0 = P.op("vector", lambda e: e.tensor_scalar_mul(out=S[0][:], in0=S[0][:], scalar1=fl_sb[:, 0:1]),
                            waits=[(R["s_io"], v_l)] + c_ready)
            else:
                v_s0 = P.op("vector", lambda e: e.memset(S[0][:], 0.0), waits=s0_free + copy_wait[0])
            s0_ready = [(R["s_dve"], v_s0)]
            for (src, dst) in ((ibf, Vtok), (Kh, Ktok)):
                for g0 in range(0, ncs, 8):
                    tb = tcnt % 2
                    tcnt += 1
                    for j in range(8):
                        c = g0 + j
                        v_t = P.op("tensor", lambda e, tb=tb, j=j, c=c, src=src: e.transpose(
                            out=psT[tb][:, j, :], in_=src[:, c * CH:(c + 1) * CH], identity=id_sb[:]),
                            waits=(pro + psT_free[tb]) if j == 0 else [], sig=R["s_pe"] if j == 7 else None)
                    v_ev = P.op("vector", lambda e, tb=tb, g0=g0, dst=dst: e.tensor_copy(out=dst[:, g0:g0 + 8, :], in_=psT[tb][:]),
                                waits=[(R["s_pe"], v_t)])
                    psT_free[tb] = [(R["s_dve"], v_ev)]
            tok_ready = [(R["s_dve"], v_ev)]
            info["tr_prev"] = [(R["s_pe"], v_t)]
            if full:
                for g0 in range(0, ncs, 8):
                    for j in range(8):
                        c = g0 + j
                        v_m = P.op("tensor", lambda e, j=j, c=c: e.matmul(
                            psA[:, j, :], Kt[:, c * CH:(c + 1) * CH], Qt[:, c * CH:(c + 1) * CH], start=True, stop=True),
                            waits=(pro + psA_free) if j == 0 else [], sig=R["s_pe"] if j == 7 else None)
                    for j in range(8):
                        c = g0 + j
                        v_am = P.op("vector", lambda e, j=j, c=c: e.tensor_tensor(out=Am[:, c, :], in0=psA[:, j, :], in1=cm_sb[:], op=ALU.mult),
                                    waits=[(R["s_pe"], v_m)] if j == 0 else [])
                    psA_free = [(R["s_dve"], v_am)]
                am_ready = [(R["s_dve"], v_am)]
                info["a_prev"] = [(R["s_pe"], v_m)]
                v_cp = P.op("scalar", lambda e: e.activation(out=Sb_all[:, 0, :], in_=S[0][:], func=AF.Copy),
                            waits=s0_ready + pe_prev)
                copy_wait[0] = [(R["s_act"], v_cp)]
                cp_val = {0: v_cp}
            hb = h % 2
            v_ob = None
            gen = prologue(h + 1, ld_next + c_ready, info) if h + 1 < HH else iter(())
            for c in range(ncs + LAG):
                next(gen, None)
                if c < ncs:
                    kb = kcnt % NKB
                    kcnt += 1
                    v_k = P.op("tensor", lambda e, kb=kb, c=c: e.matmul(psK_at(kb), Ktok[:, c, :], Vtok[:, c, :], start=True, stop=True),
                               waits=psK_free[kb] + (tok_ready if c == 0 else []), sig=R["s_pe"])
                    src_s, dst_s = S[c % 2], S[(c + 1) % 2]
                    v_s = P.op("vector", lambda e, kb=kb, c=c, src_s=src_s, dst_s=dst_s, elb=elb: e.scalar_tensor_tensor(
                        out=dst_s[:], in0=src_s[:], scalar=elb[:, c:c + 1], in1=psK_at(kb), op0=ALU.mult, op1=ALU.add),
                        waits=[(R["s_pe"], v_k)] + (s0_ready if c == 0 else []) + copy_wait[(c + 1) % 2])
                    psK_free[kb] = [(R["s_dve"], v_s)]
                    if full and c + 1 < ncs:
                        v_cp = P.op("scalar", lambda e, c=c, dst_s=dst_s: e.activation(out=Sb_all[:, c + 1, :], in_=dst_s[:], func=AF.Copy),
                                    waits=[(R["s_dve"], v_s)])
                        copy_wait[(c + 1) % 2] = [(R["s_act"], v_cp)]
                        cp_val[c + 1] = v_cp
                cc = c - LAG
                if full and cc >= 0:
                    j = cc % 8
                    if j == 0:
                        pb = ocnt % 2
                        ocnt += 1
                    cs_ = slice(cc * CH, (cc + 1) * CH)
                    osl = slice(j * CH, (j + 1) * CH)
                    P.op("tensor", lambda e, pb=pb, cc=cc, osl=osl: e.matmul(psO[pb][:, osl], Vtok[:, cc, :], Am[:, cc, :], start=True, stop=False),
                         waits=(am_ready + tok_ready + psO_free[pb]) if j == 0 else [])
                    v_o = P.op("tensor", lambda e, pb=pb, cc=cc, cs_=cs_, osl=osl, Qhb=Qhb: e.matmul(psO[pb][:, osl], Sb_all[:, cc, :], Qhb[:, cs_], start=False, stop=True),
                               waits=[(R["s_act"], cp_val[cc])], sig=R["s_pe"])
                    if j == 7:
                        tsl = slice((cc - 7) * CH, (cc + 1) * CH)
                        v_q = P.op("scalar", lambda e, pb=pb: e.activation(out=osq[:], in_=psO[pb][:], func=AF.Square),
                                   waits=[(R["s_pe"], v_o)])
                        v_n = P.op("tensor", lambda e: e.matmul(psN[:], ones[:], osq[:], start=True, stop=True),
                                   waits=[(R["s_act"], v_q)] + psN_free, sig=R["s_pe"])
                        v_d = P.op("scalar", lambda e: e.activation(out=sd[:], in_=psN[:], func=AF.Sqrt, bias=epsb[:], scale=1.0),
                                   waits=[(R["s_pe"], v_n)])
                        psN_free = [(R["s_act"], v_d)]
                        P.op("vector", lambda e: e.reciprocal(out=sd[:], in_=sd[:]), waits=[(R["s_act"], v_d)])
                        v_on = P.op("vector", lambda e, pb=pb, h=h: e.scalar_tensor_tensor(
                            out=on[:], in0=psO[pb][:], scalar=gn_sb[:, h:h + 1], in1=sd[:], op0=ALU.mult, op1=ALU.mult))
                        psO_free[pb] = [(R["s_dve"], v_on)]
                        v_ob = P.op("vector", lambda e, hb=hb, tsl=tsl, gzb=gzb: e.tensor_tensor(out=ob[hb][:, tsl], in0=on[:], in1=gzb[:, tsl], op=ALU.mult),
                                    waits=ob_free[hb] if cc == 7 else [])
            for _ in gen:
                pass
            s_fin = S[ncs % 2]
            info["el_free"][h % 2] = [(R["s_dve"], v_s)]
            if full:
                info["qh_free"][h % 2] = [(R["s_pe"], v_o)]
            if full:
                v_st = P.dma("sync", oT[h], ob[hb][:], waits=[(R["s_dve"], v_ob)], sig=R["s_out"])
                ob_free[hb] = [(R["s_out"], v_st)]
                pe_prev = [(R["s_pe"], v_n)]
            else:
                pe_prev = [(R["s_pe"], v_k)]
            if S_end is not None:
                v_ss = P.dma("sync", S_end[h], s_fin[:], waits=[(R["s_dve"], v_s)], sig=R["s_out"])
                s0_free = [(R["s_out"], v_ss)]
            else:
                s0_free = [(R["s_dve"], v_s)]
            in_free[b] = [(R["s_dve"], R["s_dve"].n), (R["s_act"], R["s_act"].n)] + pe_prev
        P.wait("sync", [(R["s_out"], R["s_out"].n)])
    run_phase(G, pfx, body)


def hgrn_consts():
    import ml_dtypes
    s = np.arange(64)[:, None]
    t = np.arange(64)[None, :]
    return {"ident": np.eye(128, dtype=np.float32).astype(ml_dtypes.bfloat16),
            "cmask": (s <= t).astype(np.float32)}


RG_PAIRS = [[0, 1], [2, 3], [4, 5], [6, 7]]


def phase_copy(G, pfx, pairs):
    def body(P):
        s_out = P.sem("s_out")
        for (dst, src) in pairs:
            P.dma("sync", dst, src, sig=s_out)
        P.wait("sync", [(s_out, s_out.n)])
    run_phase(G, pfx, body)


def phase_allgather(G, pfx, pairs):
    def body(P):
        s_cc = P.sem("s_cc")
        for (src, dst) in pairs:
            v = P.op("gpsimd", lambda e, src=src, dst=dst: e.collective_compute(
                "AllGather", ALU.bypass, replica_groups=RG_PAIRS, ins=[src], outs=[dst]), sig=s_cc, inc=1)
            P.wait("gpsimd", [(s_cc, v)])
    run_phase(G, pfx, body)


def build_fused(nt=NT):
    nc = bass.Bass("TRN2", target_bir_lowering=False)

    def din(name, shape, dt=F32):
        return nc.dram_tensor(name, list(shape), dt, kind="ExternalInput").ap()

    def dint(name, shape, dt=F32):
        return nc.dram_tensor(name, list(shape), dt).ap()

    x_in = din("xT", [D, nt])
    ffn_w = {}
    for l in range(2):
        for f in (1, 2):
            ffn_w[(l, f)] = (din(f"g{f}{l}", [128, KC]), din(f"wg{f}{l}", [FC, 128, KC * 128]),
                             din(f"wu{f}{l}", [FC, 128, KC * 128]), din(f"wd{f}{l}", [KC, 128, FC * 128]))
    gm = [din("gm0", [128, KC]), din("gm1", [128, KC])]
    w_ei = din("w_ei", [40, 128, KC * 128])
    w_eo = din("w_eo", [KC, 128, KC * 128])
    w_oi = din("w_oi", [64, 128, KC * 128])
    w_oo = din("w_oo", [KC, 128, KC * 128])
    cw = din("cw", [128, NA, CW])
    cvec = din("cvec", [128, 3, NA])
    cs = din("cs", [2, 128, nt])
    gqk = din("gqk", [128, 2])
    prot = din("prot", [128, 128], BF16)
    ident = din("ident", [128, 128], BF16)
    maskb = din("maskb", [128, 2, 128], BF16)
    fones = din("fones", [128, 128], BF16)
    flagf = din("flagf", [128, 1])
    lbl = din("lbl", [128, 2, HH])
    gng = din("gng", [128, HH])
    cmask = din("cmask", [64, 64])
    y_out = nc.dram_tensor("yT", [D, nt], F32, kind="ExternalOutput").ap()

    xa = dint("xa", [D, nt])
    xb = dint("xb", [D, nt])
    u = dint("u", [64, 128, nt])
    rT = dint("rT", [KC, 128, nt], BF16)
    qT = dint("qT", [NH, 128, nt], BF16)
    HP = NH // 2
    k_own = [dint(f"k_own{i}", [HP * 128, nt], BF16) for i in range(2)]
    v_own = [dint(f"v_own{i}", [HP * 128, nt], BF16) for i in range(2)]
    k_all = [dint(f"k_all{i}", [2 * HP * 128, nt], BF16) for i in range(2)]
    v_all = [dint(f"v_all{i}", [2 * HP * 128, nt], BF16) for i in range(2)]
    t_own = dint("t_own", [2 * NA * 128, 32])
    t_all = dint("t_all", [2 * 2 * NA * 128, 32])
    s_own = dint("s_own", [HH * 128, 128])
    s_all = dint("s_all", [2 * HH * 128, 128])

    def ch(ap2d, n):
        return ap2d.rearrange("(c p) f -> c p f", p=128)

    with ExitStack() as ges:
        G = Glob(nc, ges)
        phase_ffn(G, "f10", x_in, xa, *ffn_w[(0, 1)], nt=nt)
        phase_normproj(G, "np0", xa, gm[0], w_ei, u, 40, nt)
        phase_copy(G, "tl", [(ch(t_own, 2 * NA)[c], u[c][:, nt - 32:nt]) for c in range(2 * NA)])
        k_own_h = [ch(k_own[h // HP], HP)[h % HP] for h in range(NH)]
        v_own_h = [ch(v_own[h // HP], HP)[h % HP] for h in range(NH)]
        k_prev_h = [ch(k_all[h // HP], 2 * HP)[h % HP] for h in range(NH)]
        v_prev_h = [ch(v_all[h // HP], 2 * HP)[h % HP] for h in range(NH)]
        phase_even_qk(G, "qk", u[16:24], u[24:32], u[32:40], cs, gqk, prot, qT, k_own_h, v_own_h, nt)
        phase_allgather(G, "ag0", [(t_own, t_all)] + [(k_own[i], k_all[i]) for i in range(2)]
                        + [(v_own[i], v_all[i]) for i in range(2)])
        phase_even_conv(G, "cv", u, ch(t_all, 4 * NA)[0:2 * NA], flagf, cw, cvec, rT[0:NA], nt)
        phase_even_attn(G, "at", qT, k_own_h, v_own_h, k_prev_h, v_prev_h,
                        ident, maskb, fones, flagf, rT[NA:KC], nt)
        phase_outproj(G, "op0", xa, rT, w_eo, xb, nt)
        phase_ffn(G, "f20", xb, xa, *ffn_w[(0, 2)], nt=nt)
        phase_ffn(G, "f11", xa, xb, *ffn_w[(1, 1)], nt=nt)
        phase_normproj(G, "np1", xb, gm[1], w_oi, u, 64, nt)
        phase_hgrn(G, "h1", u, lbl, gng, ident, cmask, None, flagf, None, ch(s_own, HH), True, nt)
        phase_allgather(G, "ag1", [(s_own, s_all)])
        phase_hgrn(G, "h2", u, lbl, gng, ident, cmask, ch(s_all, 2 * HH)[0:HH], flagf, rT, None, False, nt)
        phase_outproj(G, "op1", xb, rT, w_oo, xa, nt)
        phase_ffn(G, "f21", xa, y_out, *ffn_w[(1, 2)], nt=nt)
    return nc


_PROGS = {}


def _prog(key, builder):
    if key not in _PROGS:
        _PROGS[key] = builder()
    return _PROGS[key]


def _vec_pk(v, n):
    return np.ascontiguousarray(v.reshape(n, 128).T)


def kernel(x, norm_ffn1, ffn1_wg, ffn1_wu, ffn1_wd, norm_mix, norm_ffn2, ffn2_wg,
           ffn2_wu, ffn2_wd, ev_w_in, ev_conv_w, ev_conv_b, ev_cn_g, ev_cn_b,
           ev_qn_g, ev_kn_g, ev_w_out, od_w_in, od_lb_logits, od_gn_g, od_w_out):
    import ml_dtypes
    f32 = np.float32
    x = np.asarray(x, f32)
    A = lambda a: np.asarray(a, f32)
    shared = {}
    fw = {1: (norm_ffn1, ffn1_wg, ffn1_wu, ffn1_wd), 2: (norm_ffn2, ffn2_wg, ffn2_wu, ffn2_wd)}
    for l in range(2):
        for f in (1, 2):
            g, wg, wu, wd = fw[f]
            shared[f"g{f}{l}"] = tile_gain(A(g)[l])
            shared[f"wg{f}{l}"] = tile_w_up(A(wg)[l])
            shared[f"wu{f}{l}"] = tile_w_up(A(wu)[l])
            shared[f"wd{f}{l}"] = tile_w_down(A(wd)[l])
    shared["gm0"] = tile_gain(A(norm_mix)[0])
    shared["gm1"] = tile_gain(A(norm_mix)[1])
    shared["w_ei"] = tile_w(A(ev_w_in)[0])
    shared["w_eo"] = tile_w(A(ev_w_out)[0])
    shared["w_oi"] = tile_w(A(od_w_in)[0])
    shared["w_oo"] = tile_w(A(od_w_out)[0])
    shared["cw"] = np.ascontiguousarray(A(ev_conv_w)[0].T.reshape(NA, 128, CW).transpose(1, 0, 2))
    shared["cvec"] = np.ascontiguousarray(np.stack([_vec_pk(A(v)[0], NA) for v in (ev_conv_b, ev_cn_g, ev_cn_b)], 1))
    shared["gqk"] = np.ascontiguousarray(np.stack([A(ev_qn_g)[0], A(ev_kn_g)[0]], 1))
    shared["prot"] = rot_matrix()
    ac = attn_consts(1.0)
    shared["ident"] = ac["ident"]
    shared["maskb"] = ac["maskb"]
    shared["lbl"] = np.ascontiguousarray(A(od_lb_logits).reshape(2, HH, 128).transpose(2, 0, 1))
    shared["gng"] = _vec_pk(A(od_gn_g)[0], HH)
    shared["cmask"] = hgrn_consts()["cmask"]

    maps = []
    for c in range(NCORES):
        half = c % 2
        m = dict(shared)
        m["xT"] = np.ascontiguousarray(x[c // 2, half * NT:(half + 1) * NT, :].T)
        m["cs"] = rope_tables(half * NT, NT)
        m["fones"] = np.full((128, 128), float(half), f32).astype(ml_dtypes.bfloat16)
        m["flagf"] = np.full((128, 1), float(half), f32)
        maps.append(m)

    nc = _prog("fused", lambda: build_fused(NT))
    res = run_bass_kernel_spmd(nc, maps, core_ids=list(range(NCORES)))
    out = np.empty_like(x)
    for c in range(NCORES):
        out[c // 2, (c % 2) * NT:(c % 2 + 1) * NT, :] = res.results[c]["yT"].T
    return out
```

```python
import numpy as np
from contextlib import ExitStack
import concourse.bass as bass
import concourse.mybir as mybir
from concourse.bass_utils import run_bass_kernel_spmd

F32 = mybir.dt.float32
BF16 = mybir.dt.bfloat16
AF = mybir.ActivationFunctionType
ALU = mybir.AluOpType

D = 2048
KC = D // 128
DFF = 5632
FC = DFF // 128
NT = 2048
NCORES = 8
EPS = 1e-6


class Sem:
    def __init__(self, es, nc, name):
        self.h = es.enter_context(nc.semaphore(name))
        self.n = 0
        self.name = name


class Prog:
    def __init__(self, G, es, pfx):
        self.G = G
        self.nc = G.nc
        self.es = es
        self.pfx = pfx
        self.q = {k: [] for k in ("sync", "scalar", "vector", "tensor", "gpsimd")}

    def sem(self, name):
        if name not in self.G.sems:
            self.G.sems[name] = Sem(self.G.es, self.nc, name)
        return self.G.sems[name]

    def sbuf(self, name, shape, dt):
        return self.es.enter_context(self.nc.sbuf_tensor(f"sb_{self.pfx}_{name}", list(shape), dt))

    def psum(self, name, shape, dt):
        return self.es.enter_context(self.nc.psum_tensor(f"ps_{self.pfx}_{name}", list(shape), dt))

    def op(self, eng, fn, waits=(), sig=None, inc=1):
        v = None
        auto = getattr(self, "auto", {})
        if eng in auto:
            asem = auto[eng]
            if sig is None:
                sig = asem
            if sig is asem and asem.n > 0:
                waits = list(waits) + [(asem, asem.n)]
        if sig is not None:
            sig.n += inc
            v = sig.n
            assert v < 60000, (sig.name, v)
        waits = [(s, val) for (s, val) in waits if s is not None and val is not None and val > 0]

        def run(e, fn=fn, waits=waits, sig=sig, inc=inc):
            for (s, val) in waits:
                e.wait_ge(s.h, val)
            ins = fn(e)
            if sig is not None:
                ins.then_inc(sig.h, inc)

        self.q[eng].append(run)
        return v

    def dma(self, eng, out, in_, waits=(), sig=None):
        return self.op(eng, lambda e: e.dma_start(out=out, in_=in_), waits, sig, 16)

    def wait(self, eng, waits):
        waits = [(s, val) for (s, val) in waits if val is not None and val > 0]

        def run(e):
            for (s, val) in waits:
                e.wait_ge(s.h, val)

        self.q[eng].append(run)

    def emit(self):
        nc = self.nc
        with nc.Block() as block:
            @block.sync
            def _(e):
                for f in self.q["sync"]:
                    f(e)

            @block.scalar
            def _(e):
                for f in self.q["scalar"]:
                    f(e)

            @block.vector
            def _(e):
                for f in self.q["vector"]:
                    f(e)

            @block.tensor
            def _(e):
                for f in self.q["tensor"]:
                    f(e)

            @block.gpsimd
            def _(e):
                for f in self.q["gpsimd"]:
                    f(e)


class Glob:
    def __init__(self, nc, es):
        self.nc = nc
        self.es = es
        self.sems = {}


def run_phase(G, pfx, body):
    with ExitStack() as es:
        P = Prog(G, es, pfx)
        body(P)
        P.emit()
    G.nc.all_engine_barrier()


class Ring:
    def __init__(self, bufs):
        self.bufs = bufs
        self.free = [None] * len(bufs)
        self.i = 0

    def next(self):
        k = self.i % len(self.bufs)
        self.i += 1
        return k


def emit_stageA(P, R, x_in_v, t0, TS, v_gain, dst_off=0):
    a_done = []
    for a in range(TS // 256):
        ta = t0 + a * 256
        ab = R["xa_ring"].next()
        xa = R["xa"][ab]
        sq = R["sq"][ab]
        v_ld = P.dma("gpsimd", xa[:], x_in_v[:, :, ta:ta + 256],
                     waits=R["xa_free"][ab], sig=R["s_io"])
        v_sq = P.op("scalar",
                    lambda e, xa=xa, sq=sq: e.activation(out=sq[:], in_=xa[:], func=AF.Square),
                    waits=[(R["s_io"], v_ld)] + R["sq_free"][ab], sig=R["s_act"])
        for kc in range(KC):
            w = [(R["s_act"], v_sq)] + R["psA_free"] if kc == 0 else []
            v_mm = P.op("tensor",
                        lambda e, kc=kc, sq=sq: e.matmul(R["psA"][:, 0:256], R["ones"][:], sq[:, kc, :],
                                                         start=(kc == 0), stop=(kc == KC - 1)),
                        waits=w, sig=(R["s_pe"] if kc == KC - 1 else None))
        R["sq_free"][ab] = [(R["s_pe"], v_mm)]
        v_sd = P.op("scalar",
                    lambda e: e.activation(out=R["sd"][:], in_=R["psA"][:, 0:256], func=AF.Sqrt,
                                           bias=R["epsb"][:], scale=1.0),
                    waits=[(R["s_pe"], v_mm)] + R["rstd_free"], sig=R["s_act"])
        R["psA_free"] = [(R["s_act"], v_sd)]
        v_r = P.op("vector",
                   lambda e: e.reciprocal(out=R["rstd"][:], in_=R["sd"][:]),
                   waits=[(R["s_act"], v_sd)], sig=R["s_dve"])
        for kc in range(KC):
            w = [(R["s_dve"], v_r), (R["s_io"], v_gain)] + R["xn_free"] if kc == 0 else []
            v_x = P.op("vector",
                       lambda e, kc=kc, xa=xa, ta=ta, t0=t0, xnT=R["xnT"]: e.scalar_tensor_tensor(
                           out=xnT[:, kc, dst_off + ta - t0:dst_off + ta - t0 + 256], in0=xa[:, kc, :],
                           scalar=R["gain"][:, kc:kc + 1], in1=R["rstd"][:],
                           op0=ALU.mult, op1=ALU.mult),
                       waits=w, sig=(R["s_dve"] if kc == KC - 1 else None))
        R["xa_free"][ab] = [(R["s_dve"], v_x)]
        R["rstd_free"] = [(R["s_dve"], v_x)]
        a_done = [(R["s_dve"], v_x)]
    R["xn_free"] = []
    return a_done

def emit_ffn(P, R, x_in, x_out, gain, wg, wu, wd, TS, nt=NT, final_waits=None):
    nsup = nt // TS
    nth = TS // 512
    x_in_v = x_in.rearrange("(kc p) t -> p kc t", p=128)
    x_out_v = x_out.rearrange("(kc p) t -> p kc t", p=128)

    v_gain = P.dma("gpsimd", R["gain"][:], gain, waits=R["gain_free"], sig=R["s_io"])
    R["gain_free"] = []

    def stageA(s):
        b = s % 2
        R["xnT"] = R["xnT2"][b]
        R["xn_free"] = R["xn2_free"][b]
        return emit_stageA(P, R, x_in_v, s * TS, TS, v_gain)

    def prefetch_up(fc):
        wb = R["w_ring"].next()
        conv_v = []
        for (wsrc, wdst) in ((wg, R["wbf_g"][wb]), (wu, R["wbf_u"][wb])):
            sb = R["stg_ring"].next()
            stg = R["stg"][sb]
            v_l = P.dma("sync", stg[:, 0:KC * 128], wsrc[fc], waits=R["stg_free"][sb], sig=R["s_stg"][sb])
            v_c = P.op("scalar",
                       lambda e, stg=stg, wdst=wdst: e.activation(
                           out=wdst[:].rearrange("p k f -> p (k f)"), in_=stg[:, 0:KC * 128], func=AF.Copy),
                       waits=[(R["s_stg"][sb], v_l)] + R["wbf_free"][wb], sig=R["s_act"])
            R["stg_free"][sb] = [(R["s_act"], v_c)]
            conv_v.append(v_c)
        R["wbf_free"][wb] = []
        return wb, conv_v[-1]

    def prefetch_dn(dc):
        db = R["wd_ring"].next()
        wdb = R["wbf_d"][db]
        conv_v = None
        for q4 in range(4):
            sb = R["stg_ring"].next()
            stg = R["stg"][sb]
            n = 11 * 128
            v_l = P.dma("sync", stg[:, 0:n], wd[dc][:, q4 * n:(q4 + 1) * n],
                        waits=R["stg_free"][sb], sig=R["s_stg"][sb])
            v_c = P.op("scalar",
                       lambda e, stg=stg, wdb=wdb, q4=q4, n=n: e.activation(
                           out=wdb[:, q4 * 11:(q4 + 1) * 11, :].rearrange("p k f -> p (k f)"),
                           in_=stg[:, 0:n], func=AF.Copy),
                       waits=[(R["s_stg"][sb], v_l)] + (R["wd_free"][db] if q4 == 0 else []), sig=R["s_act"])
            R["stg_free"][sb] = [(R["s_act"], v_c)]
            conv_v = v_c
        R["wd_free"][db] = []
        return db, conv_v

    xn_ready = stageA(0)
    nxt = prefetch_up(0)
    for s in range(nsup):
        t0 = s * TS
        xb = s % 2
        xnT = R["xnT2"][xb]
        xn_ready_next = None

        last_up_mm = None
        h_done = None
        nxt_d = None
        for fc in range(FC):
            wb, conv_last = nxt
            if fc + 1 < FC:
                nxt = prefetch_up(fc + 1)
            else:
                nxt_d = prefetch_dn(0)
            if fc == FC - 8 and s + 1 < nsup:
                xn_ready_next = stageA(s + 1)
            for th in range(nth):
                pb = R["pu_ring"].next()
                ps_g, ps_u = R["ps_g"][pb], R["ps_u"][pb]
                first = True
                for (wt, ps) in ((R["wbf_g"][wb], ps_g), (R["wbf_u"][wb], ps_u)):
                    for kc in range(KC):
                        w = []
                        if first:
                            w = [(R["s_act"], conv_last)] + xn_ready + R["pu_free"][pb]
                            first = False
                        last = (wt is R["wbf_u"][wb] and kc == KC - 1)
                        v_mm = P.op("tensor",
                                    lambda e, wt=wt, ps=ps, kc=kc, th=th, xnT=xnT: e.matmul(
                                        ps[:], wt[:, kc, :], xnT[:, kc, th * 512:(th + 1) * 512],
                                        start=(kc == 0), stop=(kc == KC - 1)),
                                    waits=w, sig=(R["s_pe"] if last else None))
                last_up_mm = v_mm
                sg = R["sg"][pb]
                v_s = P.op("scalar",
                           lambda e, sg=sg, ps_g=ps_g: e.activation(out=sg[:], in_=ps_g[:], func=AF.Silu),
                           waits=[(R["s_pe"], v_mm)] + R["sg_free"][pb], sig=R["s_act"])
                v_h = P.op("vector",
                           lambda e, sg=sg, ps_u=ps_u, fc=fc, th=th: e.tensor_tensor(
                               out=R["hT"][:, fc, th * 512:(th + 1) * 512], in0=ps_u[:], in1=sg[:], op=ALU.mult),
                           waits=[(R["s_act"], v_s)] + (R["h_free"] if (fc == 0 and th == 0) else []),
                           sig=R["s_dve"])
                R["pu_free"][pb] = [(R["s_dve"], v_h)]
                R["sg_free"][pb] = [(R["s_dve"], v_h)]
                h_done = v_h
            R["wbf_free"][wb] = [(R["s_pe"], last_up_mm)]
        R["h_free"] = []
        R["xn2_free"][xb] = [(R["s_pe"], last_up_mm)]

        last_dn_mm = None
        for dc in range(KC):
            db, conv_v = nxt_d
            wdb = R["wbf_d"][db]
            if dc + 1 < KC:
                nxt_d = prefetch_dn(dc + 1)
            elif s + 1 < nsup:
                nxt = prefetch_up(0)
            for th in range(nth):
                yb = R["py_ring"].next()
                ps_y = R["ps_y"][yb]
                tt = t0 + th * 512
                rb = R["xr_ring"].next()
                xr = R["xr"][rb]
                v_xr = P.dma("gpsimd", xr[:], x_in_v[:, dc, tt:tt + 512], waits=R["xr_free"][rb], sig=R["s_io"])
                for fc in range(FC):
                    w = []
                    if fc == 0:
                        w = [(R["s_act"], conv_v), (R["s_dve"], h_done)] + R["py_free"][yb]
                    v_mm = P.op("tensor",
                                lambda e, wdb=wdb, ps_y=ps_y, fc=fc, th=th: e.matmul(
                                    ps_y[:], wdb[:, fc, :], R["hT"][:, fc, th * 512:(th + 1) * 512],
                                    start=(fc == 0), stop=(fc == FC - 1)),
                                waits=w, sig=(R["s_pe"] if fc == FC - 1 else None))
                last_dn_mm = v_mm
                v_o = P.op("vector",
                           lambda e, xr=xr, ps_y=ps_y: e.scalar_tensor_tensor(
                               out=xr[:], in0=ps_y[:], scalar=0.5, in1=xr[:], op0=ALU.mult, op1=ALU.add),
                           waits=[(R["s_pe"], v_mm), (R["s_io"], v_xr)], sig=R["s_dve"])
                R["py_free"][yb] = [(R["s_dve"], v_o)]
                v_st = P.dma("gpsimd", x_out_v[:, dc, tt:tt + 512], xr[:], waits=[(R["s_dve"], v_o)], sig=R["s_out"])
                R["xr_free"][rb] = [(R["s_out"], v_st)]
            R["wd_free"][db] = [(R["s_pe"], last_dn_mm)]
        R["h_free"] = [(R["s_pe"], last_dn_mm)]
        if xn_ready_next is not None:
            xn_ready = xn_ready_next
    R["gain_free"] = [(R["s_dve"], R["s_dve"].n)]
    return [(R["s_out"], R["s_out"].n)]


def alloc_ffn_resources(P, TS):
    R = {}
    R["s_io"] = P.sem("s_io")
    R["s_out"] = P.sem("s_out")
    R["s_act"] = P.sem("s_act")
    R["s_dve"] = P.sem("s_dve")
    R["s_pe"] = P.sem("s_pe")
    R["s_stg"] = [P.sem(f"s_stg{i}") for i in range(3)]
    R["gain"] = P.sbuf("gain", [128, KC], F32)
    R["gain_free"] = []
    R["ones"] = P.sbuf("ones", [128, 128], BF16)
    R["xa"] = [P.sbuf(f"xa{i}", [128, KC, 256], F32) for i in range(1)]
    R["sq"] = [P.sbuf(f"sq{i}", [128, KC, 256], BF16) for i in range(1)]
    R["xa_ring"] = Ring(R["xa"])
    R["xa_free"] = [[] for _ in R["xa"]]
    R["sq_free"] = [[] for _ in R["sq"]]
    R["rstd"] = P.sbuf("rstd", [128, 256], F32)
    R["rstd_free"] = []
    R["sd"] = P.sbuf("sd", [128, 256], F32)
    R["epsb"] = P.sbuf("epsb", [128, 1], F32)
    R["xnT2"] = [P.sbuf(f"xnT{i}", [128, KC, TS], BF16) for i in range(2)]
    R["xn2_free"] = [[], []]
    R["xnT"] = R["xnT2"][0]
    R["xn_free"] = []
    R["hT"] = P.sbuf("hT", [128, FC, TS], BF16)
    R["h_free"] = []
    R["stg"] = [P.sbuf(f"stg{i}", [128, KC * 128], F32) for i in range(3)]
    R["stg_ring"] = Ring(R["stg"])
    R["stg_free"] = [[] for _ in R["stg"]]
    R["wbf_g"] = [P.sbuf(f"wbfg{i}", [128, KC, 128], BF16) for i in range(2)]
    R["wbf_u"] = [P.sbuf(f"wbfu{i}", [128, KC, 128], BF16) for i in range(2)]
    R["w_ring"] = Ring(R["wbf_g"])
    R["wbf_free"] = [[] for _ in range(2)]
    R["wbf_d"] = [P.sbuf(f"wbfd{i}", [128, FC, 128], BF16) for i in range(2)]
    R["wd_ring"] = Ring(R["wbf_d"])
    R["wd_free"] = [[] for _ in range(2)]
    R["sg"] = [P.sbuf(f"sg{i}", [128, 512], F32) for i in range(2)]
    R["sg_free"] = [[] for _ in range(2)]
    R["xr"] = [P.sbuf(f"xr{i}", [128, 512], F32) for i in range(2)]
    R["xr_ring"] = Ring(R["xr"])
    R["xr_free"] = [[] for _ in range(2)]
    R["psA"] = P.psum("psA", [128, 512], F32)
    R["psA_free"] = []
    R["ps_g"] = [P.psum(f"psg{i}", [128, 512], F32) for i in range(2)]
    R["ps_u"] = [P.psum(f"psu{i}", [128, 512], F32) for i in range(2)]
    R["pu_ring"] = Ring(R["ps_g"])
    R["pu_free"] = [[] for _ in range(2)]
    R["ps_y"] = [P.psum(f"psy{i}", [128, 512], F32) for i in range(2)]
    R["py_ring"] = Ring(R["ps_y"])
    R["py_free"] = [[] for _ in range(2)]
    P.op("vector", lambda e: e.memset(R["epsb"][:], EPS), sig=R["s_dve"])
    v = P.op("vector", lambda e: e.memset(R["ones"][:], 1.0 / D), sig=R["s_dve"])
    R["psA_free"] = [(R["s_dve"], v)]
    return R


def phase_ffn(G, pfx, x_in, x_out, gain, wg, wu, wd, TS=512, nt=NT):
    def body(P):
        R = alloc_ffn_resources(P, TS)
        fin = emit_ffn(P, R, x_in, x_out, gain, wg, wu, wd, TS, nt)
        P.wait("gpsimd", fin)
    run_phase(G, pfx, body)


def tile_w_up(w):
    return np.ascontiguousarray(
        w.reshape(KC, 128, FC, 128).transpose(2, 1, 0, 3).reshape(FC, 128, KC * 128))


def tile_w_down(w):
    return np.ascontiguousarray(
        w.reshape(FC, 128, KC, 128).transpose(2, 1, 0, 3).reshape(KC, 128, FC * 128))


def tile_gain(g):
    return np.ascontiguousarray(g.reshape(KC, 128).T)


def alloc_common(P, TS):
    R = {}
    for k in ("s_io", "s_out", "s_act", "s_dve", "s_pe", "s_pool"):
        R[k] = P.sem(k)
    R["s_stg"] = [P.sem(f"s_stg{i}") for i in range(3)]
    R["stg"] = [P.sbuf(f"stg{i}", [128, KC * 128], F32) for i in range(3)]
    R["stg_ring"] = Ring(R["stg"])
    R["stg_free"] = [[] for _ in R["stg"]]
    R["wbf"] = [P.sbuf(f"wbf{i}", [128, KC, 128], BF16) for i in range(2)]
    R["w_ring"] = Ring(R["wbf"])
    R["wbf_free"] = [[] for _ in range(2)]
    return R


def alloc_stageA(P, R, TS):
    R["gain"] = P.sbuf("gain", [128, KC], F32)
    R["gain_free"] = []
    R["ones"] = P.sbuf("ones", [128, 128], BF16)
    R["xa"] = [P.sbuf("xa0", [128, KC, 256], F32)]
    R["sq"] = [P.sbuf("sq0", [128, KC, 256], BF16)]
    R["xa_ring"] = Ring(R["xa"])
    R["xa_free"] = [[]]
    R["sq_free"] = [[]]
    R["rstd"] = P.sbuf("rstd", [128, 256], F32)
    R["rstd_free"] = []
    R["sd"] = P.sbuf("sd", [128, 256], F32)
    R["epsb"] = P.sbuf("epsb", [128, 1], F32)
    R["xnT"] = P.sbuf("xnT", [128, KC, TS], BF16)
    R["xn_free"] = []
    R["psA"] = P.psum("psA", [128, 512], F32)
    P.op("vector", lambda e: e.memset(R["epsb"][:], EPS), sig=R["s_dve"])
    v = P.op("vector", lambda e: e.memset(R["ones"][:], 1.0 / D), sig=R["s_dve"])
    R["psA_free"] = [(R["s_dve"], v)]


def emit_wload(P, R, wsrc_ap):
    wb = R["w_ring"].next()
    sb = R["stg_ring"].next()
    stg = R["stg"][sb]
    wdst = R["wbf"][wb]
    v_l = P.dma("sync", stg[:, 0:KC * 128], wsrc_ap, waits=R["stg_free"][sb], sig=R["s_stg"][sb])
    v_c = P.op("scalar",
               lambda e, stg=stg, wdst=wdst: e.activation(
                   out=wdst[:].rearrange("p k f -> p (k f)"), in_=stg[:, 0:KC * 128], func=AF.Copy),
               waits=[(R["s_stg"][sb], v_l)] + R["wbf_free"][wb], sig=R["s_act"])
    R["stg_free"][sb] = [(R["s_act"], v_c)]
    return wdst, wb, [(R["s_act"], v_c)]


def phase_normproj(G, pfx, x_in, gain, w, u, nch, nt=NT):
    nth = nt // 512

    def body(P):
        R = alloc_common(P, nt)
        alloc_stageA(P, R, nt)
        ps = [P.psum(f"pp{i}", [128, 512], F32) for i in range(4)]
        ps_free = [[] for _ in range(4)]
        osb = [P.sbuf(f"osb{i}", [128, 512], F32) for i in range(4)]
        osb_free = [[] for _ in range(4)]
        x_in_v = x_in.rearrange("(kc p) t -> p kc t", p=128)
        v_gain = P.dma("gpsimd", R["gain"][:], gain, sig=R["s_io"])
        cnt = 0
        nxt_w = emit_wload(P, R, w[0])
        for ch in range(nch):
            wt, wb, w_ready = nxt_w
            if ch + 1 < nch:
                nxt_w = emit_wload(P, R, w[ch + 1])
            for th in range(nth):
                if ch == 0:
                    xn_ready = emit_stageA(P, R, x_in_v, th * 512, 512, v_gain, dst_off=th * 512)
                pb = cnt % 4
                cnt += 1
                for kc in range(KC):
                    wts = (w_ready + xn_ready + ps_free[pb]) if kc == 0 else []
                    v_mm = P.op("tensor",
                                lambda e, wt=wt, pb=pb, kc=kc, th=th: e.matmul(
                                    ps[pb][:], wt[:, kc, :], R["xnT"][:, kc, th * 512:(th + 1) * 512],
                                    start=(kc == 0), stop=(kc == KC - 1)),
                                waits=wts, sig=(R["s_pe"] if kc == KC - 1 else None))
                if pb % 2 == 0:
                    v_c = P.op("vector", lambda e, pb=pb: e.tensor_copy(out=osb[pb][:], in_=ps[pb][:]),
                               waits=[(R["s_pe"], v_mm)] + osb_free[pb], sig=R["s_dve"])
                    cw = [(R["s_dve"], v_c)]
                else:
                    v_c = P.op("scalar", lambda e, pb=pb: e.activation(out=osb[pb][:], in_=ps[pb][:], func=AF.Copy),
                               waits=[(R["s_pe"], v_mm)] + osb_free[pb], sig=R["s_act"])
                    cw = [(R["s_act"], v_c)]
                ps_free[pb] = cw
                v_st = P.dma("gpsimd", u[ch][:, th * 512:(th + 1) * 512], osb[pb][:], waits=cw, sig=R["s_out"])
                osb_free[pb] = [(R["s_out"], v_st)]
            R["wbf_free"][wb] = [(R["s_pe"], v_mm)]
        P.wait("gpsimd", [(R["s_out"], R["s_out"].n)])
    run_phase(G, pfx, body)


def phase_outproj(G, pfx, x_in, r_in, w, x_out, nt=NT, scale=1.0):
    nth = nt // 512

    def body(P):
        R = alloc_common(P, nt)
        rT = P.sbuf("rT", [128, KC, nt], BF16)
        ps = [P.psum(f"pp{i}", [128, 512], F32) for i in range(4)]
        ps_free = [[] for _ in range(4)]
        xr = [P.sbuf(f"xr{i}", [128, 512], F32) for i in range(4)]
        xr_free = [[] for _ in range(4)]
        x_in_v = x_in.rearrange("(kc p) t -> p kc t", p=128)
        x_out_v = x_out.rearrange("(kc p) t -> p kc t", p=128)
        s_r = P.sem("s_ld0")
        nxt_w = emit_wload(P, R, w[0])
        for kc in range(KC):
            v_r = P.dma("sync", rT[:, kc, :], r_in[kc], sig=s_r)
        r_ready = [(s_r, v_r)]
        total = KC * nth

        def load_x(idx):
            pb = idx % 4
            dc, th = divmod(idx, nth)
            return P.dma("gpsimd", xr[pb][:], x_in_v[:, dc, th * 512:(th + 1) * 512], waits=xr_free[pb], sig=sem_x[pb])

        sem_x = [P.sem(n) for n in ("s_ld1", "s_tr", "s_cc", "s_pool")]
        AHEAD = 2
        v_xs = {i: load_x(i) for i in range(min(AHEAD, total))}
        cnt = 0
        for dc in range(KC):
            wt, wb, w_ready = nxt_w
            if dc + 1 < KC:
                nxt_w = emit_wload(P, R, w[dc + 1])
            for th in range(nth):
                pb = cnt % 4
                idx = cnt
                cnt += 1
                if idx + AHEAD < total:
                    v_xs[idx + AHEAD] = load_x(idx + AHEAD)
                v_x = v_xs.pop(idx)
                for kc in range(KC):
                    wts = (w_ready + r_ready + ps_free[pb]) if kc == 0 else []
                    v_mm = P.op("tensor",
                                lambda e, wt=wt, pb=pb, kc=kc, th=th: e.matmul(
                                    ps[pb][:], wt[:, kc, :], rT[:, kc, th * 512:(th + 1) * 512],
                                    start=(kc == 0), stop=(kc == KC - 1)),
                                waits=wts, sig=(R["s_pe"] if kc == KC - 1 else None))
                v_o = P.op("vector",
                           lambda e, pb=pb: e.scalar_tensor_tensor(
                               out=xr[pb][:], in0=ps[pb][:], scalar=float(scale), in1=xr[pb][:],
                               op0=ALU.mult, op1=ALU.add),
                           waits=[(R["s_pe"], v_mm), (sem_x[pb], v_x)], sig=R["s_dve"])
                ps_free[pb] = [(R["s_dve"], v_o)]
                v_st = P.dma("gpsimd", x_out_v[:, dc, th * 512:(th + 1) * 512], xr[pb][:],
                             waits=[(R["s_dve"], v_o)], sig=R["s_out"])
                xr_free[pb] = [(R["s_out"], v_st)]
            R["wbf_free"][wb] = [(R["s_pe"], v_mm)]
        P.wait("gpsimd", [(R["s_out"], R["s_out"].n)])
    run_phase(G, pfx, body)


def tile_w(w):
    n = w.shape[1] // 128
    return np.ascontiguousarray(w.reshape(KC, 128, n, 128).transpose(2, 1, 0, 3).reshape(n, 128, KC * 128))


CW = 31
NA = 8


def phase_even_conv(G, pfx, u, tail, flagf, cw, cvec, out, nt=NT):
    nth = nt // 512

    def body(P):
        R = {k: P.sem(k) for k in ("s_io", "s_out", "s_act", "s_dve", "s_pe")}
        s_ld = [P.sem("s_ld0"), P.sem("s_ld1")]
        P.auto = {"vector": R["s_dve"], "scalar": R["s_act"]}
        Y = P.sbuf("Y", [128, NA, nt], F32)
        ve = [P.sbuf(f"ve{i}", [128, 32 + nt], F32) for i in range(2)]
        ge = [P.sbuf(f"ge{i}", [128, 32 + nt], F32) for i in range(2)]
        ao = P.sbuf("ao", [128, NA, nt], BF16)
        cw_sb = P.sbuf("cw_sb", [128, NA, CW], F32)
        cv_sb = P.sbuf("cv_sb", [128, 3, NA], F32)
        ones = P.sbuf("ones", [128, 128], BF16)
        epsb = P.sbuf("epsb", [128, 1], F32)
        ybf = [P.sbuf(f"ybf{i}", [128, 512], BF16) for i in range(2)]
        ysq = [P.sbuf(f"ysq{i}", [128, 512], BF16) for i in range(2)]
        mu = P.sbuf("mu", [128, 512], F32)
        tt = P.sbuf("tt", [128, 512], F32)
        psS = P.psum("psS", [128, 512], F32)
        psQ = P.psum("psQ", [128, 512], F32)
        P.op("vector", lambda e: e.memset(epsb[:], EPS))
        P.op("vector", lambda e: e.memset(ones[:], 1.0 / (NA * 128)))
        fl_sb = P.sbuf("fl_sb", [128, 1], F32)
        P.dma("sync", cw_sb[:], cw, sig=R["s_io"])
        P.dma("sync", fl_sb[:], flagf, sig=R["s_io"])
        v_c = P.dma("sync", cv_sb[:], cvec, sig=R["s_io"])
        c_ready = [(R["s_io"], v_c)]
        buf_free = [[], []]
        for c in range(NA):
            b = c % 2
            P.dma("sync", ve[b][:, 0:32], tail[c], waits=buf_free[b], sig=s_ld[b])
            P.dma("sync", ve[b][:, 32:32 + nt], u[c], sig=s_ld[b])
            P.dma("sync", ge[b][:, 0:32], tail[NA + c], sig=s_ld[b])
            v_l = P.dma("sync", ge[b][:, 32:32 + nt], u[NA + c], sig=s_ld[b])
            v_s = P.op("scalar", lambda e, b=b: e.activation(out=ge[b][:], in_=ge[b][:], func=AF.Sigmoid),
                       waits=[(s_ld[b], v_l)])
            P.op("vector", lambda e, b=b: e.tensor_tensor(out=ve[b][:], in0=ve[b][:], in1=ge[b][:], op=ALU.mult),
                 waits=[(R["s_act"], v_s)] + c_ready)
            P.op("vector", lambda e, b=b: e.tensor_scalar_mul(out=ve[b][:, 0:32], in0=ve[b][:, 0:32], scalar1=fl_sb[:, 0:1]))
            P.op("vector", lambda e, b=b, c=c: e.tensor_scalar(
                out=Y[:, c, :], in0=ve[b][:, 2:2 + nt], scalar1=cw_sb[:, c, 0:1], scalar2=cv_sb[:, 0, c:c + 1],
                op0=ALU.mult, op1=ALU.add))
            for j in range(1, CW):
                v_y = P.op("vector", lambda e, b=b, c=c, j=j: e.scalar_tensor_tensor(
                    out=Y[:, c, :], in0=ve[b][:, 2 + j:2 + j + nt], scalar=cw_sb[:, c, j:j + 1], in1=Y[:, c, :],
                    op0=ALU.mult, op1=ALU.add))
            buf_free[b] = [(R["s_dve"], v_y)]
        y_ready = [(R["s_dve"], v_y)]
        ps_free = []
        yb_free = [[], []]
        cnt = 0
        for th in range(nth):
            sl = slice(th * 512, (th + 1) * 512)
            for c in range(NA):
                k = cnt % 2
                cnt += 1
                P.op("scalar", lambda e, k=k, c=c, sl=sl: e.activation(out=ybf[k][:], in_=Y[:, c, sl], func=AF.Copy),
                     waits=y_ready + yb_free[k])
                v_a = P.op("scalar", lambda e, k=k, c=c, sl=sl: e.activation(out=ysq[k][:], in_=Y[:, c, sl], func=AF.Square))
                P.op("tensor", lambda e, k=k, c=c: e.matmul(psS[:], ones[:], ybf[k][:], start=(c == 0), stop=(c == NA - 1)),
                     waits=[(R["s_act"], v_a)] + (ps_free if c == 0 else []))
                v_m = P.op("tensor", lambda e, k=k, c=c: e.matmul(psQ[:], ones[:], ysq[k][:], start=(c == 0), stop=(c == NA - 1)),
                           sig=R["s_pe"])
                yb_free[k] = [(R["s_pe"], v_m)]
            v_mu = P.op("scalar", lambda e: e.activation(out=mu[:], in_=psS[:], func=AF.Copy), waits=[(R["s_pe"], v_m)])
            P.op("vector", lambda e: e.tensor_tensor(out=tt[:], in0=mu[:], in1=mu[:], op=ALU.mult), waits=[(R["s_act"], v_mu)])
            v_v = P.op("vector", lambda e: e.tensor_tensor(out=tt[:], in0=psQ[:], in1=tt[:], op=ALU.subtract))
            ps_free = [(R["s_dve"], v_v)]
            v_sd = P.op("scalar", lambda e: e.activation(out=tt[:], in_=tt[:], func=AF.Sqrt, bias=epsb[:], scale=1.0),
                        waits=[(R["s_dve"], v_v)])
            P.op("vector", lambda e: e.reciprocal(out=tt[:], in_=tt[:]), waits=[(R["s_act"], v_sd)])
            for c in range(NA):
                P.op("vector", lambda e, c=c, sl=sl: e.tensor_tensor(out=Y[:, c, sl], in0=Y[:, c, sl], in1=mu[:], op=ALU.subtract))
                v_z = P.op("vector", lambda e, c=c, sl=sl: e.tensor_tensor(out=Y[:, c, sl], in0=Y[:, c, sl], in1=tt[:], op=ALU.mult))
                v_o = P.op("scalar", lambda e, c=c, sl=sl: e.activation(
                    out=ao[:, c, sl], in_=Y[:, c, sl], func=AF.Silu, scale=cv_sb[:, 1, c:c + 1], bias=cv_sb[:, 2, c:c + 1]),
                    waits=[(R["s_dve"], v_z)])
        for c in range(NA):
            P.dma("sync", out[c], ao[:, c, :], waits=[(R["s_act"], v_o)], sig=R["s_out"])
        P.wait("sync", [(R["s_out"], R["s_out"].n)])
    run_phase(G, pfx, body)


NH = 8
HD = 128


def phase_even_qk(G, pfx, uq, uk, uv, cs, gqk, prot, qT, kT, vT, nt=NT):
    nth = nt // 512

    def body(P):
        R = {k: P.sem(k) for k in ("s_io", "s_out", "s_act", "s_dve", "s_pe")}
        s_ld = [P.sem("s_ld0"), P.sem("s_ld1")]
        P.auto = {"vector": R["s_dve"], "scalar": R["s_act"]}
        raw = [P.sbuf(f"raw{i}", [128, nt], F32) for i in range(2)]
        sq = P.sbuf("sq", [128, nt], BF16)
        sd = P.sbuf("sd", [128, nt], F32)
        qn = P.sbuf("qn", [128, nt], BF16)
        t1 = P.sbuf("t1", [128, nt], F32)
        t2 = P.sbuf("t2", [128, nt], F32)
        ob = [P.sbuf(f"ob{i}", [128, nt], BF16) for i in range(2)]
        cs_sb = P.sbuf("cs_sb", [128, 2, nt], F32)
        g_sb = P.sbuf("g_sb", [128, 2], F32)
        prot_sb = P.sbuf("prot_sb", [128, 128], BF16)
        ones = P.sbuf("ones", [128, 128], BF16)
        epsb = P.sbuf("epsb", [128, 1], F32)
        psB = P.psum("psB", [128, nt], F32)
        psR = P.psum("psR", [128, nt], F32)
        P.op("vector", lambda e: e.memset(epsb[:], EPS))
        P.op("vector", lambda e: e.memset(ones[:], 1.0 / HD))
        P.dma("sync", cs_sb[:, 0, :], cs[0], sig=R["s_io"])
        P.dma("sync", cs_sb[:, 1, :], cs[1], sig=R["s_io"])
        P.dma("sync", prot_sb[:], prot, sig=R["s_io"])
        v_g = P.dma("sync", g_sb[:], gqk, sig=R["s_io"])
        v_gs = P.op("vector", lambda e: e.tensor_scalar_mul(out=g_sb[:, 0:1], in0=g_sb[:, 0:1], scalar1=float(HD ** -0.5)),
                    waits=[(R["s_io"], v_g)])
        raw_free = [[], []]
        ob_free = [[], []]
        psB_free, psR_free = [], []
        units = []
        for h in range(NH):
            units.append((uq[h], qT[h], 0, 0))
            units.append((uk[h], kT[h], 1, 0))
        for i, (src, dst, gi, poff) in enumerate(units):
            b = i % 2
            v_l = P.dma("sync", raw[b][:], src, waits=raw_free[b], sig=s_ld[b])
            v_sq = P.op("scalar", lambda e, b=b: e.activation(out=sq[:], in_=raw[b][:], func=AF.Square),
                        waits=[(s_ld[b], v_l)])
            for th in range(nth):
                sl = slice(th * 512, (th + 1) * 512)
                v_m = P.op("tensor", lambda e, sl=sl: e.matmul(psB[:, sl], ones[:], sq[:, sl], start=True, stop=True),
                           waits=([(R["s_act"], v_sq)] + psB_free) if th == 0 else [],
                           sig=R["s_pe"] if th == nth - 1 else None)
            v_sd = P.op("scalar", lambda e: e.activation(out=sd[:], in_=psB[:], func=AF.Sqrt, bias=epsb[:], scale=1.0),
                        waits=[(R["s_pe"], v_m)])
            psB_free = [(R["s_act"], v_sd)]
            P.op("vector", lambda e: e.reciprocal(out=sd[:], in_=sd[:]), waits=[(R["s_act"], v_sd)])
            v_qn = P.op("vector", lambda e, b=b, gi=gi: e.scalar_tensor_tensor(
                out=qn[:], in0=raw[b][:], scalar=g_sb[:, gi:gi + 1], in1=sd[:], op0=ALU.mult, op1=ALU.mult))
            raw_free[b] = [(R["s_dve"], v_qn)]
            for th in range(nth):
                sl = slice(th * 512, (th + 1) * 512)
                v_m = P.op("tensor", lambda e, sl=sl: e.matmul(psR[:, sl], prot_sb[:], qn[:, sl], start=True, stop=True),
                           waits=([(R["s_dve"], v_qn)] + psR_free) if th == 0 else [],
                           sig=R["s_pe"] if th == nth - 1 else None)
            P.op("vector", lambda e, poff=poff: e.tensor_tensor(out=t1[:], in0=qn[:], in1=cs_sb[:, 0, poff:poff + nt], op=ALU.mult))
            v_t2 = P.op("vector", lambda e, poff=poff: e.tensor_tensor(out=t2[:], in0=psR[:], in1=cs_sb[:, 1, poff:poff + nt], op=ALU.mult),
                        waits=[(R["s_pe"], v_m)])
            psR_free = [(R["s_dve"], v_t2)]
            v_o = P.op("vector", lambda e, b=b: e.tensor_tensor(out=ob[b][:], in0=t1[:], in1=t2[:], op=ALU.add),
                       waits=ob_free[b])
            v_st = P.dma("sync", dst, ob[b][:], waits=[(R["s_dve"], v_o)], sig=R["s_out"])
            ob_free[b] = [(R["s_out"], v_st)]
        i0 = len(units)
        for j in range(NH):
            h = j
            b = (i0 + j) % 2
            v_l = P.dma("sync", raw[b][:], uv[h], waits=raw_free[b], sig=s_ld[b])
            v_c = P.op("scalar", lambda e, b=b: e.activation(out=ob[b][:], in_=raw[b][:], func=AF.Copy),
                       waits=[(s_ld[b], v_l)] + ob_free[b])
            raw_free[b] = [(R["s_act"], v_c)]
            v_st = P.dma("sync", vT[h], ob[b][:], waits=[(R["s_act"], v_c)], sig=R["s_out"])
            ob_free[b] = [(R["s_out"], v_st)]
        P.wait("sync", [(R["s_out"], R["s_out"].n)])
    run_phase(G, pfx, body)


def rope_tables(pos0, n):
    half = HD // 2
    inv = np.exp(-np.log(10000.0) * np.arange(half, dtype=np.float32) / half).astype(np.float32)
    pos = (pos0 + np.arange(n)).astype(np.float32)
    ang = pos[None, :] * inv[:, None]
    c = np.cos(ang).astype(np.float32)
    s_ = np.sin(ang).astype(np.float32)
    return np.ascontiguousarray(np.stack([np.concatenate([c, c], 0), np.concatenate([s_, s_], 0)], 0))


def rot_matrix():
    import ml_dtypes
    m = np.zeros((128, 128), np.float32)
    for dp in range(64):
        m[dp + 64, dp] = -1.0
        m[dp, dp + 64] = 1.0
    return m.astype(ml_dtypes.bfloat16)


DILS = (1, 4, 16)
NEG = -30000.0


def attn_units(nt):
    slots = {}
    units = []
    for d in DILS:
        nb = (2 * nt) // d // 128
        for c in range(d):
            for n in range(nb // 2 - 1, nb):
                slots[(d, c, n)] = len(slots)
            for n in range(nb // 2, nb):
                units.append((d, c, n))
    return units, slots


def phase_even_attn(G, pfx, qT, kT_own, vT_own, kT_prev, vT_prev, ident, maskb, fones, flagf, oT, nt=NT):
    units, slots = attn_units(nt)
    nslot = len(slots)

    def body(P):
        R = {k: P.sem(k) for k in ("s_io", "s_out", "s_act", "s_dve", "s_pe", "s_tr")}
        s_ld = [P.sem("s_ld0"), P.sem("s_ld1")]
        P.auto = {"vector": R["s_dve"], "scalar": R["s_act"]}
        k_sb = [P.sbuf(f"k_sb{i}", [128, 2 * nt], BF16) for i in range(2)]
        q_sb = [P.sbuf(f"q_sb{i}", [128, nt], BF16) for i in range(2)]
        v_sb = [P.sbuf(f"v_sb{i}", [128, 2 * nt], BF16) for i in range(2)]
        Vb = [P.sbuf(f"Vb{i}", [128, nslot, 128], BF16) for i in range(2)]
        acc = P.sbuf("acc", [128, 2, nt], F32)
        rden = P.sbuf("rden", [128, nt], F32)
        ob = [P.sbuf(f"ob{i}", [128, nt], BF16) for i in range(2)]
        pT = [P.sbuf(f"pT{i}", [128, 2, 128], BF16) for i in range(2)]
        id_sb = P.sbuf("id_sb", [128, 128], BF16)
        mb_sb = P.sbuf("mb_sb", [128, 2, 128], BF16)
        fo_sb = P.sbuf("fo_sb", [128, 128], BF16)
        on_sb = P.sbuf("on_sb", [128, 128], BF16)
        psT = [P.psum(f"psT{i}", [128, 8, 128], BF16) for i in range(2)]
        psS = [P.psum(f"psS{i}", [128, 2, 128], F32) for i in range(2)]
        psO = [P.psum(f"psO{i}", [128, 2, 128], F32) for i in range(2)]
        fl_sb = P.sbuf("fl_sb", [128, 1], F32)
        v_ms = P.op("vector", lambda e: e.memset(on_sb[:], 1.0))
        P.dma("sync", fl_sb[:], flagf, sig=R["s_io"])
        P.dma("sync", id_sb[:], ident, sig=R["s_io"])
        P.dma("sync", mb_sb[:], maskb, sig=R["s_io"])
        v_c = P.dma("sync", fo_sb[:], fones, sig=R["s_io"])
        c_ready = [(R["s_io"], v_c), (R["s_dve"], v_ms)]
        in_free = [[], []]
        vb_free = [[], []]
        ob_free = [[], []]
        psT_free = [[], []]
        psS_free = [[], []]
        psO_free = [[], []]
        pT_free = [[], []]
        tcnt = 0
        ucnt = 0
        for h in range(NH):
            hb = h % 2
            P.dma("sync", k_sb[hb][:, 0:nt], kT_prev[h], waits=in_free[hb], sig=s_ld[hb])
            P.dma("sync", k_sb[hb][:, nt:2 * nt], kT_own[h], sig=s_ld[hb])
            P.dma("sync", q_sb[hb][:], qT[h], sig=s_ld[hb])
            P.dma("sync", v_sb[hb][:, 0:nt], vT_prev[h], sig=s_ld[hb])
            v_l = P.dma("sync", v_sb[hb][:, nt:2 * nt], vT_own[h], sig=s_ld[hb])
            v_fz = P.op("vector", lambda e, hb=hb: e.tensor_scalar_mul(
                out=v_sb[hb][:, 0:nt], in0=v_sb[hb][:, 0:nt], scalar1=fl_sb[:, 0:1]),
                waits=[(s_ld[hb], v_l)] + c_ready)
            h_ready = [(s_ld[hb], v_l), (R["s_dve"], v_fz)] + c_ready
            keys = list(slots.keys())
            v_ev = None
            for g0 in range(0, nslot, 8):
                grp = keys[g0:g0 + 8]
                tb = tcnt % 2
                tcnt += 1
                for j, (d, c, n) in enumerate(grp):
                    st = 128 * n * d + c
                    v_t = P.op("tensor", lambda e, tb=tb, j=j, st=st, d=d, hb=hb: e.transpose(
                        out=psT[tb][:, j, :], in_=v_sb[hb][:, st:st + 127 * d + 1:d], identity=id_sb[:]),
                        waits=(h_ready + psT_free[tb] + vb_free[hb]) if j == 0 else [],
                        sig=R["s_tr"] if j == len(grp) - 1 else None)
                v_ev = P.op("vector", lambda e, tb=tb, g0=g0, ng=len(grp), hb=hb: e.tensor_copy(
                    out=Vb[hb][:, g0:g0 + ng, :], in_=psT[tb][:, 0:ng, :]), waits=[(R["s_tr"], v_t)])
                psT_free[tb] = [(R["s_dve"], v_ev)]
            vb_ready = [(R["s_dve"], v_ev)]
            last_tr = [(R["s_tr"], v_t)]

            def emit_S(u):
                nonlocal ucnt
                d, c, n = u
                pb = ucnt % 2
                ucnt += 1
                qst = 128 * n * d + c - nt
                qsl = slice(qst, qst + 127 * d + 1, d)
                for kb in range(2):
                    kst = 128 * (n - 1 + kb) * d + c
                    P.op("tensor", lambda e, pb=pb, kb=kb, kst=kst, d=d, qsl=qsl, hb=hb: e.matmul(
                        psS[pb][:, kb, :], k_sb[hb][:, kst:kst + 127 * d + 1:d], q_sb[hb][:, qsl], start=True, stop=False),
                        waits=(h_ready + psS_free[pb]) if kb == 0 else [])
                    v_s = P.op("tensor", lambda e, pb=pb, kb=kb: e.matmul(
                        psS[pb][:, kb, :], id_sb[:], mb_sb[:, kb, :], start=False, stop=True),
                        sig=R["s_pe"] if kb == 1 else None)
                v_e = P.op("scalar", lambda e, pb=pb: e.activation(out=pT[pb][:], in_=psS[pb][:], func=AF.Exp),
                           waits=[(R["s_pe"], v_s)] + pT_free[pb])
                psS_free[pb] = [(R["s_act"], v_e)]
                return (u, pb, qsl, v_e)

            def emit_ND(st_):
                (d, c, n), pb, qsl, v_e = st_
                prev_in_prev_half = (128 * (n - 1) * d + c) < nt
                for kb in range(2):
                    sl_ = slots[(d, c, n - 1 + kb)]
                    P.op("tensor", lambda e, pb=pb, kb=kb, sl_=sl_, hb=hb: e.matmul(
                        psO[pb][:, 0, :], Vb[hb][:, sl_, :], pT[pb][:, kb, :], start=(kb == 0), stop=(kb == 1)),
                        waits=([(R["s_act"], v_e)] + vb_ready + psO_free[pb]) if kb == 0 else [])
                for kb in range(2):
                    on = fo_sb if (kb == 0 and prev_in_prev_half) else on_sb
                    v_n = P.op("tensor", lambda e, pb=pb, kb=kb, on=on: e.matmul(
                        psO[pb][:, 1, :], on[:], pT[pb][:, kb, :], start=(kb == 0), stop=(kb == 1)),
                        sig=R["s_pe"] if kb == 1 else None)
                pT_free[pb] = [(R["s_pe"], v_n)]
                if d == 1:
                    v_a = P.op("vector", lambda e, pb=pb, qsl=qsl: e.tensor_copy(out=acc[:, :, qsl], in_=psO[pb][:]),
                               waits=[(R["s_pe"], v_n)])
                else:
                    v_a = P.op("vector", lambda e, pb=pb, qsl=qsl: e.tensor_tensor(
                        out=acc[:, :, qsl], in0=psO[pb][:], in1=acc[:, :, qsl], op=ALU.add), waits=[(R["s_pe"], v_n)])
                psO_free[pb] = [(R["s_dve"], v_a)]
                return v_n

            pend = None
            v_n = None
            for u in units:
                st_ = emit_S(u)
                if pend is not None:
                    v_n = emit_ND(pend)
                pend = st_
            v_n = emit_ND(pend)
            in_free[hb] = [(R["s_pe"], v_n)] + last_tr
            vb_free[hb] = [(R["s_pe"], v_n)]
            P.op("vector", lambda e: e.reciprocal(out=rden[:], in_=acc[:, 1, :]))
            v_o = P.op("vector", lambda e, hb=hb: e.tensor_tensor(out=ob[hb][:], in0=acc[:, 0, :], in1=rden[:], op=ALU.mult),
                       waits=ob_free[hb])
            v_st = P.dma("sync", oT[h], ob[hb][:], waits=[(R["s_dve"], v_o)], sig=R["s_out"])
            ob_free[hb] = [(R["s_out"], v_st)]
        P.wait("sync", [(R["s_out"], R["s_out"].n)])
    run_phase(G, pfx, body)


def attn_consts(flag):
    import ml_dtypes
    k = np.arange(128)[:, None]
    q = np.arange(128)[None, :]
    mb = np.zeros((128, 2, 128), np.float32)
    mb[:, 0, :] = np.where(k >= q, 0.0, NEG)
    mb[:, 1, :] = np.where(k <= q, 0.0, NEG)
    return {"ident": np.eye(128, dtype=np.float32).astype(ml_dtypes.bfloat16),
            "maskb": mb.astype(ml_dtypes.bfloat16),
            "fones": np.full((128, 128), float(flag), np.float32).astype(ml_dtypes.bfloat16)}


HH = 16
CH = 64
HG_BCAST = True


def phase_hgrn(G, pfx, u, lbl, gng, ident, cmask, S0, flagf, oT, S_end, state_only, nt=NT):
    ncs = nt // CH
    full = not state_only
    LAG = 2

    def body(P):
        R = {k: P.sem(k) for k in ("s_io", "s_out", "s_act", "s_dve", "s_pe", "s_ld0", "s_ld1")}
        s_ldb = [R["s_ld0"], R["s_ld1"]]
        P.auto = {"vector": R["s_dve"], "scalar": R["s_act"]}
        nin = 2
        fz = [P.sbuf(f"fz{i}", [128, nt], F32) for i in range(nin)]
        iz = [P.sbuf(f"iz{i}", [128, nt], F32) for i in range(nin)]
        if full:
            qz = [P.sbuf(f"qz{i}", [128, nt], F32) for i in range(nin)]
            gz = [P.sbuf(f"gz{i}", [128, nt], F32) for i in range(nin)]
            e1 = P.sbuf("e1", [128, ncs, CH], F32)
            e2 = P.sbuf("e2", [128, ncs, CH], F32)
            eb = P.sbuf("eb", [128, nt], F32)
            Qt = P.sbuf("Qt", [128, nt], BF16)
            Kt = P.sbuf("Kt", [128, nt], BF16)
            Qh = [P.sbuf(f"Qh{i}", [128, nt], BF16) for i in range(2)]
            Am = P.sbuf("Am", [64, ncs, CH], BF16)
            Sb_all = P.sbuf("Sb_all", [128, ncs, 128], BF16)
            osq = P.sbuf("osq", [128, 512], BF16)
            sd = P.sbuf("sd", [128, 512], F32)
            on = P.sbuf("on", [128, 512], F32)
            ob = [P.sbuf(f"ob{i}", [128, nt], BF16) for i in range(2)]
        kk = P.sbuf("kk", [128, nt], F32)
        bb = P.sbuf("bb", [128, ncs, CH], F32)
        e3 = P.sbuf("e3", [128, ncs, CH], F32)
        el = [P.sbuf(f"el{i}", [128, ncs], F32) for i in range(2)]
        rmask = P.sbuf("rmask", [128, nt], F32)
        Kh = P.sbuf("Kh", [128, nt], BF16)
        ibf = P.sbuf("ibf", [128, nt], BF16)
        Vtok = P.sbuf("Vtok", [64, ncs, 128], BF16)
        Ktok = P.sbuf("Ktok", [64, ncs, 128], BF16)
        S = [P.sbuf(f"S{i}", [128, 128], F32) for i in range(2)]
        lb_sb = P.sbuf("lb_sb", [128, 2, HH], F32)
        lb = P.sbuf("lb", [128, HH], F32)
        oml = P.sbuf("oml", [128, HH], F32)
        gn_sb = P.sbuf("gn_sb", [128, HH], F32)
        id_sb = P.sbuf("id_sb", [128, 128], BF16)
        cm_sb = P.sbuf("cm_sb", [64, 64], F32)
        ones = P.sbuf("ones", [128, 128], BF16)
        epsb = P.sbuf("epsb", [128, 1], F32)
        fl_sb = P.sbuf("fl_sb", [128, 1], F32)
        psT = [P.psum(f"psT{i}", [64, 8, 128], BF16) for i in range(2)]
        NKB = 2
        psK_l = [P.psum(f"psK{i}", [128, 128], F32) for i in range(NKB)]
        psK_at = lambda kb: psK_l[kb][:]
        if full:
            psA = P.psum("psA", [64, 8, CH], F32)
            psO = [P.psum(f"psO{i}", [128, 512], F32) for i in range(2)]
            psN = P.psum("psN", [128, 512], F32)

        P.op("vector", lambda e: e.memset(epsb[:], EPS))
        P.op("vector", lambda e: e.memset(ones[:], 1.0 / 128))
        P.op("vector", lambda e: e.memset(rmask[:], 1.0))
        P.op("vector", lambda e: e.memset(rmask[:, 0:nt - CH + 1:CH], 0.0))
        P.dma("sync", lb_sb[:], lbl, sig=R["s_io"])
        P.dma("sync", fl_sb[:], flagf, sig=R["s_io"])
        P.dma("sync", gn_sb[:], gng, sig=R["s_io"])
        P.dma("sync", id_sb[:], ident, sig=R["s_io"])
        v_c = P.dma("sync", cm_sb[:], cmask, sig=R["s_io"])
        c_io = [(R["s_io"], v_c)]
        P.op("vector", lambda e: e.tensor_tensor(out=lb[:], in0=lb_sb[:, 1, :], in1=lb_sb[:, 0, :], op=ALU.subtract), waits=c_io)
        v_a = P.op("scalar", lambda e: e.activation(out=lb[:], in_=lb[:], func=AF.Sigmoid), waits=[(R["s_dve"], R["s_dve"].n)])
        v_c2 = P.op("vector", lambda e: e.tensor_scalar(out=oml[:], in0=lb[:], scalar1=-1.0, scalar2=1.0, op0=ALU.mult, op1=ALU.add),
                    waits=[(R["s_act"], v_a)])
        c_ready = c_io + [(R["s_dve"], v_c2)]

        in_free = [[] for _ in range(nin)]
        ob_free = [[], []]
        psT_free = [[], []]
        psA_free = []
        psO_free = [[], []]
        psK_free = [[] for _ in range(2)]
        psN_free = []
        s0_free = []
        copy_wait = [[], []]
        pe_prev = []
        tr_prev = []
        tcnt = 0
        ocnt = 0
        kcnt = 0

        def issue_loads(h):
            b = h % nin
            v = P.dma("sync", fz[b][:], u[HH + h], waits=in_free[b], sig=s_ldb[b])
            v = P.dma("sync", iz[b][:], u[2 * HH + h], sig=s_ldb[b])
            if full:
                P.dma("sync", qz[b][:], u[h], sig=s_ldb[b])
                v = P.dma("sync", gz[b][:], u[3 * HH + h], sig=s_ldb[b])
            return [(s_ldb[b], v)]

        def prologue(h, ld, info):
            b = h % nin
            fzb, izb = fz[b], iz[b]
            elb = el[h % 2]
            a1 = P.op("scalar", lambda e, fzb=fzb: e.activation(out=fzb[:], in_=fzb[:], func=AF.Sigmoid), waits=ld)
            yield
            if full:
                qzb, gzb = qz[b], gz[b]
                Qhb = Qh[h % 2]
                P.op("scalar", lambda e, qzb=qzb: e.activation(out=qzb[:], in_=qzb[:], func=AF.Silu))
                yield
                P.op("scalar", lambda e, gzb=gzb: e.activation(out=gzb[:], in_=gzb[:], func=AF.Silu))
                yield
            P.op("scalar", lambda e, izb=izb: e.activation(out=ibf[:], in_=izb[:], func=AF.Copy), waits=info["tr_prev"])
            yield
            P.op("vector", lambda e, h=h, fzb=fzb: e.tensor_scalar(out=fzb[:], in0=fzb[:], scalar1=oml[:, h:h + 1], scalar2=lb[:, h:h + 1],
                                                                   op0=ALU.mult, op1=ALU.add), waits=[(R["s_act"], a1)])
            yield
            d2 = P.op("vector", lambda e, fzb=fzb: e.tensor_scalar(out=kk[:], in0=fzb[:], scalar1=-1.0, scalar2=1.0, op0=ALU.mult, op1=ALU.add))
            yield
            a5 = P.op("scalar", lambda e, fzb=fzb: e.activation(out=fzb[:], in_=fzb[:], func=AF.Ln), waits=[(R["s_dve"], d2)])
            yield
            bb2 = bb[:].rearrange("p c t -> p (c t)")
            d3 = P.op("vector", lambda e, bb2=bb2, fzb=fzb: e.tensor_tensor_scan(out=bb2, data0=rmask[:], data1=fzb[:], initial=0.0,
                                                                               op0=ALU.mult, op1=ALU.add), waits=[(R["s_act"], a5)])
            yield
            if full:
                d4 = P.op("vector", lambda e: e.tensor_tensor(out=e1[:], in0=bb[:], in1=bb[:, :, 31:32].broadcast_to([128, ncs, CH]),
                                                              op=ALU.subtract))
                yield
            d5 = P.op("vector", lambda e: e.tensor_tensor(out=e3[:], in0=bb[:], in1=bb[:, :, CH - 1:CH].broadcast_to([128, ncs, CH]),
                                                          op=ALU.subtract))
            yield
            if full:
                P.op("scalar", lambda e, bb2=bb2: e.activation(out=eb[:], in_=bb2, func=AF.Exp), waits=[(R["s_dve"], d3)])
                yield
            P.op("scalar", lambda e, elb=elb: e.activation(out=elb[:], in_=bb[:, :, CH - 1], func=AF.Exp),
                 waits=[(R["s_dve"], d3)] + info["el_free"][h % 2])
            yield
            if full:
                P.op("scalar", lambda e: e.activation(out=e2[:], in_=e1[:], func=AF.Exp, scale=-1.0), waits=[(R["s_dve"], d4)])
                yield
                P.op("scalar", lambda e: e.activation(out=e1[:], in_=e1[:], func=AF.Exp))
                yield
            v_e = P.op("scalar", lambda e: e.activation(out=e3[:], in_=e3[:], func=AF.Exp, scale=-1.0), waits=[(R["s_dve"], d5)])
            yield
            e32 = e3[:].rearrange("p c t -> p (c t)")
            wpro = [(R["s_act"], v_e)] + info["tr_prev"] + info["a_prev"]
            if full:
                e12 = e1[:].rearrange("p c t -> p (c t)")
                e22 = e2[:].rearrange("p c t -> p (c t)")
                P.op("vector", lambda e, e12=e12, qzb=qzb: e.tensor_tensor(out=Qt[:], in0=qzb[:], in1=e12, op=ALU.mult), waits=wpro)
                yield
                P.op("vector", lambda e, e22=e22: e.tensor_tensor(out=Kt[:], in0=kk[:], in1=e22, op=ALU.mult))
                yield
                P.op("vector", lambda e, qzb=qzb, Qhb=Qhb: e.tensor_tensor(out=Qhb[:], in0=qzb[:], in1=eb[:], op=ALU.mult),
                     waits=info["qh_free"][h % 2])
                yield
            v_pro = P.op("vector", lambda e, e32=e32: e.tensor_tensor(out=Kh[:], in0=kk[:], in1=e32, op=ALU.mult), waits=wpro)
            info["pro"] = [(R["s_dve"], v_pro)]
            yield

        info = {"tr_prev": [], "a_prev": [], "el_free": [[], []], "qh_free": [[], []], "pro": None}
        ld_next = issue_loads(0)
        for _ in prologue(0, ld_next + c_ready, info):
            pass
        for h in range(HH):
            b = h % nin
            if h + 1 < HH:
                ld_next = issue_loads(h + 1)
            if full:
                gzb = gz[b]
                Qhb = Qh[h % 2]
            elb = el[h % 2]
            pro = info["pro"]
            if S0 is not None:
                v_l = P.dma("sync", S[0][:], S0[h], waits=s0_free + copy_wait[0], sig=R["s_io"])
                v_s0 = P.op("vector", lambda e: e.tensor_scalar_mul(out=S[0][:], in0=S[0][:], scalar1=fl_sb[:, 0:1]),
                            waits=[(R["s_io"], v_l)] + c_ready)
            else:
                v_s0 = P.op("vector", lambda e: e.memset(S[0][:], 0.0), waits=s0_free + copy_wait[0])
            s0_ready = [(R["s_dve"], v_s0)]
            for (src, dst) in ((ibf, Vtok), (Kh, Ktok)):
                for g0 in range(0, ncs, 8):
                    tb = tcnt % 2
                    tcnt += 1
                    for j in range(8):
                        c = g0 + j
                        v_t = P.op("tensor", lambda e, tb=tb, j=j, c=c, src=src: e.transpose(
                            out=psT[tb][:, j, :], in_=src[:, c * CH:(c + 1) * CH], identity=id_sb[:]),
                            waits=(pro + psT_free[tb]) if j == 0 else [], sig=R["s_pe"] if j == 7 else None)
                    v_ev = P.op("vector", lambda e, tb=tb, g0=g0, dst=dst: e.tensor_copy(out=dst[:, g0:g0 + 8, :], in_=psT[tb][:]),
                                waits=[(R["s_pe"], v_t)])
                    psT_free[tb] = [(R["s_dve"], v_ev)]
            tok_ready = [(R["s_dve"], v_ev)]
            info["tr_prev"] = [(R["s_pe"], v_t)]
            if full:
                for g0 in range(0, ncs, 8):
                    for j in range(8):
                        c = g0 + j
                        v_m = P.op("tensor", lambda e, j=j, c=c: e.matmul(
                            psA[:, j, :], Kt[:, c * CH:(c + 1) * CH], Qt[:, c * CH:(c + 1) * CH], start=True, stop=True),
                            waits=(pro + psA_free) if j == 0 else [], sig=R["s_pe"] if j == 7 else None)
                    for j in range(8):
                        c = g0 + j
                        v_am = P.op("vector", lambda e, j=j, c=c: e.tensor_tensor(out=Am[:, c, :], in0=psA[:, j, :], in1=cm_sb[:], op=ALU.mult),
                                    waits=[(R["s_pe"], v_m)] if j == 0 else [])
                    psA_free = [(R["s_dve"], v_am)]
                am_ready = [(R["s_dve"], v_am)]
                info["a_prev"] = [(R["s_pe"], v_m)]
                v_cp = P.op("scalar", lambda e: e.activation(out=Sb_all[:, 0, :], in_=S[0][:], func=AF.Copy),
                            waits=s0_ready + pe_prev)
                copy_wait[0] = [(R["s_act"], v_cp)]
                cp_val = {0: v_cp}
            hb = h % 2
            v_ob = None
            gen = prologue(h + 1, ld_next + c_ready, info) if h + 1 < HH else iter(())
            for c in range(ncs + LAG):
                next(gen, None)
                if c < ncs:
                    kb = kcnt % NKB
                    kcnt += 1
                    v_k = P.op("tensor", lambda e, kb=kb, c=c: e.matmul(psK_at(kb), Ktok[:, c, :], Vtok[:, c, :], start=True, stop=True),
                               waits=psK_free[kb] + (tok_ready if c == 0 else []), sig=R["s_pe"])
                    src_s, dst_s = S[c % 2], S[(c + 1) % 2]
                    v_s = P.op("vector", lambda e, kb=kb, c=c, src_s=src_s, dst_s=dst_s, elb=elb: e.scalar_tensor_tensor(
                        out=dst_s[:], in0=src_s[:], scalar=elb[:, c:c + 1], in1=psK_at(kb), op0=ALU.mult, op1=ALU.add),
                        waits=[(R["s_pe"], v_k)] + (s0_ready if c == 0 else []) + copy_wait[(c + 1) % 2])
                    psK_free[kb] = [(R["s_dve"], v_s)]
                    if full and c + 1 < ncs:
                        v_cp = P.op("scalar", lambda e, c=c, dst_s=dst_s: e.activation(out=Sb_all[:, c + 1, :], in_=dst_s[:], func=AF.Copy),
                                    waits=[(R["s_dve"], v_s)])
                        copy_wait[(c + 1) % 2] = [(R["s_act"], v_cp)]
                        cp_val[c + 1] = v_cp
                cc = c - LAG
                if full and cc >= 0:
                    j = cc % 8
                    if j == 0:
                        pb = ocnt % 2
                        ocnt += 1
                    cs_ = slice(cc * CH, (cc + 1) * CH)
                    osl = slice(j * CH, (j + 1) * CH)
                    P.op("tensor", lambda e, pb=pb, cc=cc, osl=osl: e.matmul(psO[pb][:, osl], Vtok[:, cc, :], Am[:, cc, :], start=True, stop=False),
                         waits=(am_ready + tok_ready + psO_free[pb]) if j == 0 else [])
                    v_o = P.op("tensor", lambda e, pb=pb, cc=cc, cs_=cs_, osl=osl, Qhb=Qhb: e.matmul(psO[pb][:, osl], Sb_all[:, cc, :], Qhb[:, cs_], start=False, stop=True),
                               waits=[(R["s_act"], cp_val[cc])], sig=R["s_pe"])
                    if j == 7:
                        tsl = slice((cc - 7) * CH, (cc + 1) * CH)
                        v_q = P.op("scalar", lambda e, pb=pb: e.activation(out=osq[:], in_=psO[pb][:], func=AF.Square),
                                   waits=[(R["s_pe"], v_o)])
                        v_n = P.op("tensor", lambda e: e.matmul(psN[:], ones[:], osq[:], start=True, stop=True),
                                   waits=[(R["s_act"], v_q)] + psN_free, sig=R["s_pe"])
                        v_d = P.op("scalar", lambda e: e.activation(out=sd[:], in_=psN[:], func=AF.Sqrt, bias=epsb[:], scale=1.0),
                                   waits=[(R["s_pe"], v_n)])
                        psN_free = [(R["s_act"], v_d)]
                        P.op("vector", lambda e: e.reciprocal(out=sd[:], in_=sd[:]), waits=[(R["s_act"], v_d)])
                        v_on = P.op("vector", lambda e, pb=pb, h=h: e.scalar_tensor_tensor(
                            out=on[:], in0=psO[pb][:], scalar=gn_sb[:, h:h + 1], in1=sd[:], op0=ALU.mult, op1=ALU.mult))
                        psO_free[pb] = [(R["s_dve"], v_on)]
                        v_ob = P.op("vector", lambda e, hb=hb, tsl=tsl, gzb=gzb: e.tensor_tensor(out=ob[hb][:, tsl], in0=on[:], in1=gzb[:, tsl], op=ALU.mult),
                                    waits=ob_free[hb] if cc == 7 else [])
            for _ in gen:
                pass
            s_fin = S[ncs % 2]
            info["el_free"][h % 2] = [(R["s_dve"], v_s)]
            if full:
                info["qh_free"][h % 2] = [(R["s_pe"], v_o)]
            if full:
                v_st = P.dma("sync", oT[h], ob[hb][:], waits=[(R["s_dve"], v_ob)], sig=R["s_out"])
                ob_free[hb] = [(R["s_out"], v_st)]
                pe_prev = [(R["s_pe"], v_n)]
            else:
                pe_prev = [(R["s_pe"], v_k)]
            if S_end is not None:
                v_ss = P.dma("sync", S_end[h], s_fin[:], waits=[(R["s_dve"], v_s)], sig=R["s_out"])
                s0_free = [(R["s_out"], v_ss)]
            else:
                s0_free = [(R["s_dve"], v_s)]
            in_free[b] = [(R["s_dve"], R["s_dve"].n), (R["s_act"], R["s_act"].n)] + pe_prev
        P.wait("sync", [(R["s_out"], R["s_out"].n)])
    run_phase(G, pfx, body)


def hgrn_consts():
    import ml_dtypes
    s = np.arange(64)[:, None]
    t = np.arange(64)[None, :]
    return {"ident": np.eye(128, dtype=np.float32).astype(ml_dtypes.bfloat16),
            "cmask": (s <= t).astype(np.float32)}


RG_PAIRS = [[0, 1], [2, 3], [4, 5], [6, 7]]


def phase_copy(G, pfx, pairs):
    def body(P):
        s_out = P.sem("s_out")
        for (dst, src) in pairs:
            P.dma("sync", dst, src, sig=s_out)
        P.wait("sync", [(s_out, s_out.n)])
    run_phase(G, pfx, body)


def phase_allgather(G, pfx, pairs):
    def body(P):
        s_cc = P.sem("s_cc")
        for (src, dst) in pairs:
            v = P.op("gpsimd", lambda e, src=src, dst=dst: e.collective_compute(
                "AllGather", ALU.bypass, replica_groups=RG_PAIRS, ins=[src], outs=[dst]), sig=s_cc, inc=1)
            P.wait("gpsimd", [(s_cc, v)])
    run_phase(G, pfx, body)


def build_fused(nt=NT):
    nc = bass.Bass("TRN2", target_bir_lowering=False)

    def din(name, shape, dt=F32):
        return nc.dram_tensor(name, list(shape), dt, kind="ExternalInput").ap()

    def dint(name, shape, dt=F32):
        return nc.dram_tensor(name, list(shape), dt).ap()

    x_in = din("xT", [D, nt])
    ffn_w = {}
    for l in range(2):
        for f in (1, 2):
            ffn_w[(l, f)] = (din(f"g{f}{l}", [128, KC]), din(f"wg{f}{l}", [FC, 128, KC * 128]),
                             din(f"wu{f}{l}", [FC, 128, KC * 128]), din(f"wd{f}{l}", [KC, 128, FC * 128]))
    gm = [din("gm0", [128, KC]), din("gm1", [128, KC])]
    w_ei = din("w_ei", [40, 128, KC * 128])
    w_eo = din("w_eo", [KC, 128, KC * 128])
    w_oi = din("w_oi", [64, 128, KC * 128])
    w_oo = din("w_oo", [KC, 128, KC * 128])
    cw = din("cw", [128, NA, CW])
    cvec = din("cvec", [128, 3, NA])
    cs = din("cs", [2, 128, nt])
    gqk = din("gqk", [128, 2])
    prot = din("prot", [128, 128], BF16)
    ident = din("ident", [128, 128], BF16)
    maskb = din("maskb", [128, 2, 128], BF16)
    fones = din("fones", [128, 128], BF16)
    flagf = din("flagf", [128, 1])
    lbl = din("lbl", [128, 2, HH])
    gng = din("gng", [128, HH])
    cmask = din("cmask", [64, 64])
    y_out = nc.dram_tensor("yT", [D, nt], F32, kind="ExternalOutput").ap()

    xa = dint("xa", [D, nt])
    xb = dint("xb", [D, nt])
    u = dint("u", [64, 128, nt])
    rT = dint("rT", [KC, 128, nt], BF16)
    qT = dint("qT", [NH, 128, nt], BF16)
    HP = NH // 2
    k_own = [dint(f"k_own{i}", [HP * 128, nt], BF16) for i in range(2)]
    v_own = [dint(f"v_own{i}", [HP * 128, nt], BF16) for i in range(2)]
    k_all = [dint(f"k_all{i}", [2 * HP * 128, nt], BF16) for i in range(2)]
    v_all = [dint(f"v_all{i}", [2 * HP * 128, nt], BF16) for i in range(2)]
    t_own = dint("t_own", [2 * NA * 128, 32])
    t_all = dint("t_all", [2 * 2 * NA * 128, 32])
    s_own = dint("s_own", [HH * 128, 128])
    s_all = dint("s_all", [2 * HH * 128, 128])

    def ch(ap2d, n):
        return ap2d.rearrange("(c p) f -> c p f", p=128)

    with ExitStack() as ges:
        G = Glob(nc, ges)
        phase_ffn(G, "f10", x_in, xa, *ffn_w[(0, 1)], nt=nt)
        phase_normproj(G, "np0", xa, gm[0], w_ei, u, 40, nt)
        phase_copy(G, "tl", [(ch(t_own, 2 * NA)[c], u[c][:, nt - 32:nt]) for c in range(2 * NA)])
        k_own_h = [ch(k_own[h // HP], HP)[h % HP] for h in range(NH)]
        v_own_h = [ch(v_own[h // HP], HP)[h % HP] for h in range(NH)]
        k_prev_h = [ch(k_all[h // HP], 2 * HP)[h % HP] for h in range(NH)]
        v_prev_h = [ch(v_all[h // HP], 2 * HP)[h % HP] for h in range(NH)]
        phase_even_qk(G, "qk", u[16:24], u[24:32], u[32:40], cs, gqk, prot, qT, k_own_h, v_own_h, nt)
        phase_allgather(G, "ag0", [(t_own, t_all)] + [(k_own[i], k_all[i]) for i in range(2)]
                        + [(v_own[i], v_all[i]) for i in range(2)])
        phase_even_conv(G, "cv", u, ch(t_all, 4 * NA)[0:2 * NA], flagf, cw, cvec, rT[0:NA], nt)
        phase_even_attn(G, "at", qT, k_own_h, v_own_h, k_prev_h, v_prev_h,
                        ident, maskb, fones, flagf, rT[NA:KC], nt)
        phase_outproj(G, "op0", xa, rT, w_eo, xb, nt)
        phase_ffn(G, "f20", xb, xa, *ffn_w[(0, 2)], nt=nt)
        phase_ffn(G, "f11", xa, xb, *ffn_w[(1, 1)], nt=nt)
        phase_normproj(G, "np1", xb, gm[1], w_oi, u, 64, nt)
        phase_hgrn(G, "h1", u, lbl, gng, ident, cmask, None, flagf, None, ch(s_own, HH), True, nt)
        phase_allgather(G, "ag1", [(s_own, s_all)])
        phase_hgrn(G, "h2", u, lbl, gng, ident, cmask, ch(s_all, 2 * HH)[0:HH], flagf, rT, None, False, nt)
        phase_outproj(G, "op1", xb, rT, w_oo, xa, nt)
        phase_ffn(G, "f21", xa, y_out, *ffn_w[(1, 2)], nt=nt)
    return nc


_PROGS = {}


def _prog(key, builder):
    if key not in _PROGS:
        _PROGS[key] = builder()
    return _PROGS[key]


def _vec_pk(v, n):
    return np.ascontiguousarray(v.reshape(n, 128).T)


def kernel(x, norm_ffn1, ffn1_wg, ffn1_wu, ffn1_wd, norm_mix, norm_ffn2, ffn2_wg,
           ffn2_wu, ffn2_wd, ev_w_in, ev_conv_w, ev_conv_b, ev_cn_g, ev_cn_b,
           ev_qn_g, ev_kn_g, ev_w_out, od_w_in, od_lb_logits, od_gn_g, od_w_out):
    import ml_dtypes
    f32 = np.float32
    x = np.asarray(x, f32)
    A = lambda a: np.asarray(a, f32)
    shared = {}
    fw = {1: (norm_ffn1, ffn1_wg, ffn1_wu, ffn1_wd), 2: (norm_ffn2, ffn2_wg, ffn2_wu, ffn2_wd)}
    for l in range(2):
        for f in (1, 2):
            g, wg, wu, wd = fw[f]
            shared[f"g{f}{l}"] = tile_gain(A(g)[l])
            shared[f"wg{f}{l}"] = tile_w_up(A(wg)[l])
            shared[f"wu{f}{l}"] = tile_w_up(A(wu)[l])
            shared[f"wd{f}{l}"] = tile_w_down(A(wd)[l])
    shared["gm0"] = tile_gain(A(norm_mix)[0])
    shared["gm1"] = tile_gain(A(norm_mix)[1])
    shared["w_ei"] = tile_w(A(ev_w_in)[0])
    shared["w_eo"] = tile_w(A(ev_w_out)[0])
    shared["w_oi"] = tile_w(A(od_w_in)[0])
    shared["w_oo"] = tile_w(A(od_w_out)[0])
    shared["cw"] = np.ascontiguousarray(A(ev_conv_w)[0].T.reshape(NA, 128, CW).transpose(1, 0, 2))
    shared["cvec"] = np.ascontiguousarray(np.stack([_vec_pk(A(v)[0], NA) for v in (ev_conv_b, ev_cn_g, ev_cn_b)], 1))
    shared["gqk"] = np.ascontiguousarray(np.stack([A(ev_qn_g)[0], A(ev_kn_g)[0]], 1))
    shared["prot"] = rot_matrix()
    ac = attn_consts(1.0)
    shared["ident"] = ac["ident"]
    shared["maskb"] = ac["maskb"]
    shared["lbl"] = np.ascontiguousarray(A(od_lb_logits).reshape(2, HH, 128).transpose(2, 0, 1))
    shared["gng"] = _vec_pk(A(od_gn_g)[0], HH)
    shared["cmask"] = hgrn_consts()["cmask"]

    maps = []
    for c in range(NCORES):
        half = c % 2
        m = dict(shared)
        m["xT"] = np.ascontiguousarray(x[c // 2, half * NT:(half + 1) * NT, :].T)
        m["cs"] = rope_tables(half * NT, NT)
        m["fones"] = np.full((128, 128), float(half), f32).astype(ml_dtypes.bfloat16)
        m["flagf"] = np.full((128, 1), float(half), f32)
        maps.append(m)

    nc = _prog("fused", lambda: build_fused(NT))
    res = run_bass_kernel_spmd(nc, maps, core_ids=list(range(NCORES)))
    out = np.empty_like(x)
    for c in range(NCORES):
        out[c // 2, (c % 2) * NT:(c % 2 + 1) * NT, :] = res.results[c]["yT"].T
    return out
```

```python
import numpy as np
from contextlib import ExitStack
import concourse.bass as bass
import concourse.mybir as mybir
from concourse.bass_utils import run_bass_kernel_spmd

F32 = mybir.dt.float32
BF16 = mybir.dt.bfloat16
AF = mybir.ActivationFunctionType
ALU = mybir.AluOpType

D = 2048
KC = D // 128
DFF = 5632
FC = DFF // 128
NT = 2048
NCORES = 8
EPS = 1e-6


class Sem:
    def __init__(self, es, nc, name):
        self.h = es.enter_context(nc.semaphore(name))
        self.n = 0
        self.name = name


class Prog:
    def __init__(self, G, es, pfx):
        self.G = G
        self.nc = G.nc
        self.es = es
        self.pfx = pfx
        self.q = {k: [] for k in ("sync", "scalar", "vector", "tensor", "gpsimd")}

    def sem(self, name):
        if name not in self.G.sems:
            self.G.sems[name] = Sem(self.G.es, self.nc, name)
        return self.G.sems[name]

    def sbuf(self, name, shape, dt):
        return self.es.enter_context(self.nc.sbuf_tensor(f"sb_{self.pfx}_{name}", list(shape), dt))

    def psum(self, name, shape, dt):
        return self.es.enter_context(self.nc.psum_tensor(f"ps_{self.pfx}_{name}", list(shape), dt))

    def op(self, eng, fn, waits=(), sig=None, inc=1):
        v = None
        auto = getattr(self, "auto", {})
        if eng in auto:
            asem = auto[eng]
            if sig is None:
                sig = asem
            if sig is asem and asem.n > 0:
                waits = list(waits) + [(asem, asem.n)]
        if sig is not None:
            sig.n += inc
            v = sig.n
            assert v < 60000, (sig.name, v)
        waits = [(s, val) for (s, val) in waits if s is not None and val is not None and val > 0]

        def run(e, fn=fn, waits=waits, sig=sig, inc=inc):
            for (s, val) in waits:
                e.wait_ge(s.h, val)
            ins = fn(e)
            if sig is not None:
                ins.then_inc(sig.h, inc)

        self.q[eng].append(run)
        return v

    def dma(self, eng, out, in_, waits=(), sig=None):
        return self.op(eng, lambda e: e.dma_start(out=out, in_=in_), waits, sig, 16)

    def wait(self, eng, waits):
        waits = [(s, val) for (s, val) in waits if val is not None and val > 0]

        def run(e):
            for (s, val) in waits:
                e.wait_ge(s.h, val)

        self.q[eng].append(run)

    def emit(self):
        nc = self.nc
        with nc.Block() as block:
            @block.sync
            def _(e):
                for f in self.q["sync"]:
                    f(e)

            @block.scalar
            def _(e):
                for f in self.q["scalar"]:
                    f(e)

            @block.vector
            def _(e):
                for f in self.q["vector"]:
                    f(e)

            @block.tensor
            def _(e):
                for f in self.q["tensor"]:
                    f(e)

            @block.gpsimd
            def _(e):
                for f in self.q["gpsimd"]:
                    f(e)


class Glob:
    def __init__(self, nc, es):
        self.nc = nc
        self.es = es
        self.sems = {}


def run_phase(G, pfx, body):
    with ExitStack() as es:
        P = Prog(G, es, pfx)
        body(P)
        P.emit()
    G.nc.all_engine_barrier()


class Ring:
    def __init__(self, bufs):
        self.bufs = bufs
        self.free = [None] * len(bufs)
        self.i = 0

    def next(self):
        k = self.i % len(self.bufs)
        self.i += 1
        return k


def emit_stageA(P, R, x_in_v, t0, TS, v_gain, dst_off=0):
    a_done = []
    for a in range(TS // 256):
        ta = t0 + a * 256
        ab = R["xa_ring"].next()
        xa = R["xa"][ab]
        sq = R["sq"][ab]
        v_ld = P.dma("gpsimd", xa[:], x_in_v[:, :, ta:ta + 256],
                     waits=R["xa_free"][ab], sig=R["s_io"])
        v_sq = P.op("scalar",
                    lambda e, xa=xa, sq=sq: e.activation(out=sq[:], in_=xa[:], func=AF.Square),
                    waits=[(R["s_io"], v_ld)] + R["sq_free"][ab], sig=R["s_act"])
        for kc in range(KC):
            w = [(R["s_act"], v_sq)] + R["psA_free"] if kc == 0 else []
            v_mm = P.op("tensor",
                        lambda e, kc=kc, sq=sq: e.matmul(R["psA"][:, 0:256], R["ones"][:], sq[:, kc, :],
                                                         start=(kc == 0), stop=(kc == KC - 1)),
                        waits=w, sig=(R["s_pe"] if kc == KC - 1 else None))
        R["sq_free"][ab] = [(R["s_pe"], v_mm)]
        v_sd = P.op("scalar",
                    lambda e: e.activation(out=R["sd"][:], in_=R["psA"][:, 0:256], func=AF.Sqrt,
                                           bias=R["epsb"][:], scale=1.0),
                    waits=[(R["s_pe"], v_mm)] + R["rstd_free"], sig=R["s_act"])
        R["psA_free"] = [(R["s_act"], v_sd)]
        v_r = P.op("vector",
                   lambda e: e.reciprocal(out=R["rstd"][:], in_=R["sd"][:]),
                   waits=[(R["s_act"], v_sd)], sig=R["s_dve"])
        for kc in range(KC):
            w = [(R["s_dve"], v_r), (R["s_io"], v_gain)] + R["xn_free"] if kc == 0 else []
            v_x = P.op("vector",
                       lambda e, kc=kc, xa=xa, ta=ta, t0=t0, xnT=R["xnT"]: e.scalar_tensor_tensor(
                           out=xnT[:, kc, dst_off + ta - t0:dst_off + ta - t0 + 256], in0=xa[:, kc, :],
                           scalar=R["gain"][:, kc:kc + 1], in1=R["rstd"][:],
                           op0=ALU.mult, op1=ALU.mult),
                       waits=w, sig=(R["s_dve"] if kc == KC - 1 else None))
        R["xa_free"][ab] = [(R["s_dve"], v_x)]
        R["rstd_free"] = [(R["s_dve"], v_x)]
        a_done = [(R["s_dve"], v_x)]
    R["xn_free"] = []
    return a_done

def emit_ffn(P, R, x_in, x_out, gain, wg, wu, wd, TS, nt=NT, final_waits=None):
    nsup = nt // TS
    nth = TS // 512
    x_in_v = x_in.rearrange("(kc p) t -> p kc t", p=128)
    x_out_v = x_out.rearrange("(kc p) t -> p kc t", p=128)

    v_gain = P.dma("gpsimd", R["gain"][:], gain, waits=R["gain_free"], sig=R["s_io"])
    R["gain_free"] = []

    def stageA(s):
        b = s % 2
        R["xnT"] = R["xnT2"][b]
        R["xn_free"] = R["xn2_free"][b]
        return emit_stageA(P, R, x_in_v, s * TS, TS, v_gain)

    def prefetch_up(fc):
        wb = R["w_ring"].next()
        conv_v = []
        for (wsrc, wdst) in ((wg, R["wbf_g"][wb]), (wu, R["wbf_u"][wb])):
            sb = R["stg_ring"].next()
            stg = R["stg"][sb]
            v_l = P.dma("sync", stg[:, 0:KC * 128], wsrc[fc], waits=R["stg_free"][sb], sig=R["s_stg"][sb])
            v_c = P.op("scalar",
                       lambda e, stg=stg, wdst=wdst: e.activation(
                           out=wdst[:].rearrange("p k f -> p (k f)"), in_=stg[:, 0:KC * 128], func=AF.Copy),
                       waits=[(R["s_stg"][sb], v_l)] + R["wbf_free"][wb], sig=R["s_act"])
            R["stg_free"][sb] = [(R["s_act"], v_c)]
            conv_v.append(v_c)
        R["wbf_free"][wb] = []
        return wb, conv_v[-1]

    def prefetch_dn(dc):
        db = R["wd_ring"].next()
        wdb = R["wbf_d"][db]
        conv_v = None
        for q4 in range(4):
            sb = R["stg_ring"].next()
            stg = R["stg"][sb]
            n = 11 * 128
            v_l = P.dma("sync", stg[:, 0:n], wd[dc][:, q4 * n:(q4 + 1) * n],
                        waits=R["stg_free"][sb], sig=R["s_stg"][sb])
            v_c = P.op("scalar",
                       lambda e, stg=stg, wdb=wdb, q4=q4, n=n: e.activation(
                           out=wdb[:, q4 * 11:(q4 + 1) * 11, :].rearrange("p k f -> p (k f)"),
                           in_=stg[:, 0:n], func=AF.Copy),
                       waits=[(R["s_stg"][sb], v_l)] + (R["wd_free"][db] if q4 == 0 else []), sig=R["s_act"])
            R["stg_free"][sb] = [(R["s_act"], v_c)]
            conv_v = v_c
        R["wd_free"][db] = []
        return db, conv_v

    xn_ready = stageA(0)
    nxt = prefetch_up(0)
    for s in range(nsup):
        t0 = s * TS
        xb = s % 2
        xnT = R["xnT2"][xb]
        xn_ready_next = None

        last_up_mm = None
        h_done = None
        nxt_d = None
        for fc in range(FC):
            wb, conv_last = nxt
            if fc + 1 < FC:
                nxt = prefetch_up(fc + 1)
            else:
                nxt_d = prefetch_dn(0)
            if fc == FC - 8 and s + 1 < nsup:
                xn_ready_next = stageA(s + 1)
            for th in range(nth):
                pb = R["pu_ring"].next()
                ps_g, ps_u = R["ps_g"][pb], R["ps_u"][pb]
                first = True
                for (wt, ps) in ((R["wbf_g"][wb], ps_g), (R["wbf_u"][wb], ps_u)):
                    for kc in range(KC):
                        w = []
                        if first:
                            w = [(R["s_act"], conv_last)] + xn_ready + R["pu_free"][pb]
                            first = False
                        last = (wt is R["wbf_u"][wb] and kc == KC - 1)
                        v_mm = P.op("tensor",
                                    lambda e, wt=wt, ps=ps, kc=kc, th=th, xnT=xnT: e.matmul(
                                        ps[:], wt[:, kc, :], xnT[:, kc, th * 512:(th + 1) * 512],
                                        start=(kc == 0), stop=(kc == KC - 1)),
                                    waits=w, sig=(R["s_pe"] if last else None))
                last_up_mm = v_mm
                sg = R["sg"][pb]
                v_s = P.op("scalar",
                           lambda e, sg=sg, ps_g=ps_g: e.activation(out=sg[:], in_=ps_g[:], func=AF.Silu),
                           waits=[(R["s_pe"], v_mm)] + R["sg_free"][pb], sig=R["s_act"])
                v_h = P.op("vector",
                           lambda e, sg=sg, ps_u=ps_u, fc=fc, th=th: e.tensor_tensor(
                               out=R["hT"][:, fc, th * 512:(th + 1) * 512], in0=ps_u[:], in1=sg[:], op=ALU.mult),
                           waits=[(R["s_act"], v_s)] + (R["h_free"] if (fc == 0 and th == 0) else []),
                           sig=R["s_dve"])
                R["pu_free"][pb] = [(R["s_dve"], v_h)]
                R["sg_free"][pb] = [(R["s_dve"], v_h)]
                h_done = v_h
            R["wbf_free"][wb] = [(R["s_pe"], last_up_mm)]
        R["h_free"] = []
        R["xn2_free"][xb] = [(R["s_pe"], last_up_mm)]

        last_dn_mm = None
        for dc in range(KC):
            db, conv_v = nxt_d
            wdb = R["wbf_d"][db]
            if dc + 1 < KC:
                nxt_d = prefetch_dn(dc + 1)
            elif s + 1 < nsup:
                nxt = prefetch_up(0)
            for th in range(nth):
                yb = R["py_ring"].next()
                ps_y = R["ps_y"][yb]
                tt = t0 + th * 512
                rb = R["xr_ring"].next()
                xr = R["xr"][rb]
                v_xr = P.dma("gpsimd", xr[:], x_in_v[:, dc, tt:tt + 512], waits=R["xr_free"][rb], sig=R["s_io"])
                for fc in range(FC):
                    w = []
                    if fc == 0:
                        w = [(R["s_act"], conv_v), (R["s_dve"], h_done)] + R["py_free"][yb]
                    v_mm = P.op("tensor",
                                lambda e, wdb=wdb, ps_y=ps_y, fc=fc, th=th: e.matmul(
                                    ps_y[:], wdb[:, fc, :], R["hT"][:, fc, th * 512:(th + 1) * 512],
                                    start=(fc == 0), stop=(fc == FC - 1)),
                                waits=w, sig=(R["s_pe"] if fc == FC - 1 else None))
                last_dn_mm = v_mm
                v_o = P.op("vector",
                           lambda e, xr=xr, ps_y=ps_y: e.scalar_tensor_tensor(
                               out=xr[:], in0=ps_y[:], scalar=0.5, in1=xr[:], op0=ALU.mult, op1=ALU.add),
                           waits=[(R["s_pe"], v_mm), (R["s_io"], v_xr)], sig=R["s_dve"])
                R["py_free"][yb] = [(R["s_dve"], v_o)]
                v_st = P.dma("gpsimd", x_out_v[:, dc, tt:tt + 512], xr[:], waits=[(R["s_dve"], v_o)], sig=R["s_out"])
                R["xr_free"][rb] = [(R["s_out"], v_st)]
            R["wd_free"][db] = [(R["s_pe"], last_dn_mm)]
        R["h_free"] = [(R["s_pe"], last_dn_mm)]
        if xn_ready_next is not None:
            xn_ready = xn_ready_next
    R["gain_free"] = [(R["s_dve"], R["s_dve"].n)]
    return [(R["s_out"], R["s_out"].n)]


def alloc_ffn_resources(P, TS):
    R = {}
    R["s_io"] = P.sem("s_io")
    R["s_out"] = P.sem("s_out")
    R["s_act"] = P.sem("s_act")
    R["s_dve"] = P.sem("s_dve")
    R["s_pe"] = P.sem("s_pe")
    R["s_stg"] = [P.sem(f"s_stg{i}") for i in range(3)]
    R["gain"] = P.sbuf("gain", [128, KC], F32)
    R["gain_free"] = []
    R["ones"] = P.sbuf("ones", [128, 128], BF16)
    R["xa"] = [P.sbuf(f"xa{i}", [128, KC, 256], F32) for i in range(1)]
    R["sq"] = [P.sbuf(f"sq{i}", [128, KC, 256], BF16) for i in range(1)]
    R["xa_ring"] = Ring(R["xa"])
    R["xa_free"] = [[] for _ in R["xa"]]
    R["sq_free"] = [[] for _ in R["sq"]]
    R["rstd"] = P.sbuf("rstd", [128, 256], F32)
    R["rstd_free"] = []
    R["sd"] = P.sbuf("sd", [128, 256], F32)
    R["epsb"] = P.sbuf("epsb", [128, 1], F32)
    R["xnT2"] = [P.sbuf(f"xnT{i}", [128, KC, TS], BF16) for i in range(2)]
    R["xn2_free"] = [[], []]
    R["xnT"] = R["xnT2"][0]
    R["xn_free"] = []
    R["hT"] = P.sbuf("hT", [128, FC, TS], BF16)
    R["h_free"] = []
    R["stg"] = [P.sbuf(f"stg{i}", [128, KC * 128], F32) for i in range(3)]
    R["stg_ring"] = Ring(R["stg"])
    R["stg_free"] = [[] for _ in R["stg"]]
    R["wbf_g"] = [P.sbuf(f"wbfg{i}", [128, KC, 128], BF16) for i in range(2)]
    R["wbf_u"] = [P.sbuf(f"wbfu{i}", [128, KC, 128], BF16) for i in range(2)]
    R["w_ring"] = Ring(R["wbf_g"])
    R["wbf_free"] = [[] for _ in range(2)]
    R["wbf_d"] = [P.sbuf(f"wbfd{i}", [128, FC, 128], BF16) for i in range(2)]
    R["wd_ring"] = Ring(R["wbf_d"])
    R["wd_free"] = [[] for _ in range(2)]
    R["sg"] = [P.sbuf(f"sg{i}", [128, 512], F32) for i in range(2)]
    R["sg_free"] = [[] for _ in range(2)]
    R["xr"] = [P.sbuf(f"xr{i}", [128, 512], F32) for i in range(2)]
    R["xr_ring"] = Ring(R["xr"])
    R["xr_free"] = [[] for _ in range(2)]
    R["psA"] = P.psum("psA", [128, 512], F32)
    R["psA_free"] = []
    R["ps_g"] = [P.psum(f"psg{i}", [128, 512], F32) for i in range(2)]
    R["ps_u"] = [P.psum(f"psu{i}", [128, 512], F32) for i in range(2)]
    R["pu_ring"] = Ring(R["ps_g"])
    R["pu_free"] = [[] for _ in range(2)]
    R["ps_y"] = [P.psum(f"psy{i}", [128, 512], F32) for i in range(2)]
    R["py_ring"] = Ring(R["ps_y"])
    R["py_free"] = [[] for _ in range(2)]
    P.op("vector", lambda e: e.memset(R["epsb"][:], EPS), sig=R["s_dve"])
    v = P.op("vector", lambda e: e.memset(R["ones"][:], 1.0 / D), sig=R["s_dve"])
    R["psA_free"] = [(R["s_dve"], v)]
    return R


def phase_ffn(G, pfx, x_in, x_out, gain, wg, wu, wd, TS=512, nt=NT):
    def body(P):
        R = alloc_ffn_resources(P, TS)
        fin = emit_ffn(P, R, x_in, x_out, gain, wg, wu, wd, TS, nt)
        P.wait("gpsimd", fin)
    run_phase(G, pfx, body)


def tile_w_up(w):
    return np.ascontiguousarray(
        w.reshape(KC, 128, FC, 128).transpose(2, 1, 0, 3).reshape(FC, 128, KC * 128))


def tile_w_down(w):
    return np.ascontiguousarray(
        w.reshape(FC, 128, KC, 128).transpose(2, 1, 0, 3).reshape(KC, 128, FC * 128))


def tile_gain(g):
    return np.ascontiguousarray(g.reshape(KC, 128).T)


def alloc_common(P, TS):
    R = {}
    for k in ("s_io", "s_out", "s_act", "s_dve", "s_pe", "s_pool"):
        R[k] = P.sem(k)
    R["s_stg"] = [P.sem(f"s_stg{i}") for i in range(3)]
    R["stg"] = [P.sbuf(f"stg{i}", [128, KC * 128], F32) for i in range(3)]
    R["stg_ring"] = Ring(R["stg"])
    R["stg_free"] = [[] for _ in R["stg"]]
    R["wbf"] = [P.sbuf(f"wbf{i}", [128, KC, 128], BF16) for i in range(2)]
    R["w_ring"] = Ring(R["wbf"])
    R["wbf_free"] = [[] for _ in range(2)]
    return R


def alloc_stageA(P, R, TS):
    R["gain"] = P.sbuf("gain", [128, KC], F32)
    R["gain_free"] = []
    R["ones"] = P.sbuf("ones", [128, 128], BF16)
    R["xa"] = [P.sbuf("xa0", [128, KC, 256], F32)]
    R["sq"] = [P.sbuf("sq0", [128, KC, 256], BF16)]
    R["xa_ring"] = Ring(R["xa"])
    R["xa_free"] = [[]]
    R["sq_free"] = [[]]
    R["rstd"] = P.sbuf("rstd", [128, 256], F32)
    R["rstd_free"] = []
    R["sd"] = P.sbuf("sd", [128, 256], F32)
    R["epsb"] = P.sbuf("epsb", [128, 1], F32)
    R["xnT"] = P.sbuf("xnT", [128, KC, TS], BF16)
    R["xn_free"] = []
    R["psA"] = P.psum("psA", [128, 512], F32)
    P.op("vector", lambda e: e.memset(R["epsb"][:], EPS), sig=R["s_dve"])
    v = P.op("vector", lambda e: e.memset(R["ones"][:], 1.0 / D), sig=R["s_dve"])
    R["psA_free"] = [(R["s_dve"], v)]


def emit_wload(P, R, wsrc_ap):
    wb = R["w_ring"].next()
    sb = R["stg_ring"].next()
    stg = R["stg"][sb]
    wdst = R["wbf"][wb]
    v_l = P.dma("sync", stg[:, 0:KC * 128], wsrc_ap, waits=R["stg_free"][sb], sig=R["s_stg"][sb])
    v_c = P.op("scalar",
               lambda e, stg=stg, wdst=wdst: e.activation(
                   out=wdst[:].rearrange("p k f -> p (k f)"), in_=stg[:, 0:KC * 128], func=AF.Copy),
               waits=[(R["s_stg"][sb], v_l)] + R["wbf_free"][wb], sig=R["s_act"])
    R["stg_free"][sb] = [(R["s_act"], v_c)]
    return wdst, wb, [(R["s_act"], v_c)]


def phase_normproj(G, pfx, x_in, gain, w, u, nch, nt=NT):
    nth = nt // 512

    def body(P):
        R = alloc_common(P, nt)
        alloc_stageA(P, R, nt)
        ps = [P.psum(f"pp{i}", [128, 512], F32) for i in range(4)]
        ps_free = [[] for _ in range(4)]
        osb = [P.sbuf(f"osb{i}", [128, 512], F32) for i in range(4)]
        osb_free = [[] for _ in range(4)]
        x_in_v = x_in.rearrange("(kc p) t -> p kc t", p=128)
        v_gain = P.dma("gpsimd", R["gain"][:], gain, sig=R["s_io"])
        cnt = 0
        nxt_w = emit_wload(P, R, w[0])
        for ch in range(nch):
            wt, wb, w_ready = nxt_w
            if ch + 1 < nch:
                nxt_w = emit_wload(P, R, w[ch + 1])
            for th in range(nth):
                if ch == 0:
                    xn_ready = emit_stageA(P, R, x_in_v, th * 512, 512, v_gain, dst_off=th * 512)
                pb = cnt % 4
                cnt += 1
                for kc in range(KC):
                    wts = (w_ready + xn_ready + ps_free[pb]) if kc == 0 else []
                    v_mm = P.op("tensor",
                                lambda e, wt=wt, pb=pb, kc=kc, th=th: e.matmul(
                                    ps[pb][:], wt[:, kc, :], R["xnT"][:, kc, th * 512:(th + 1) * 512],
                                    start=(kc == 0), stop=(kc == KC - 1)),
                                waits=wts, sig=(R["s_pe"] if kc == KC - 1 else None))
                if pb % 2 == 0:
                    v_c = P.op("vector", lambda e, pb=pb: e.tensor_copy(out=osb[pb][:], in_=ps[pb][:]),
                               waits=[(R["s_pe"], v_mm)] + osb_free[pb], sig=R["s_dve"])
                    cw = [(R["s_dve"], v_c)]
                else:
                    v_c = P.op("scalar", lambda e, pb=pb: e.activation(out=osb[pb][:], in_=ps[pb][:], func=AF.Copy),
                               waits=[(R["s_pe"], v_mm)] + osb_free[pb], sig=R["s_act"])
                    cw = [(R["s_act"], v_c)]
                ps_free[pb] = cw
                v_st = P.dma("gpsimd", u[ch][:, th * 512:(th + 1) * 512], osb[pb][:], waits=cw, sig=R["s_out"])
                osb_free[pb] = [(R["s_out"], v_st)]
            R["wbf_free"][wb] = [(R["s_pe"], v_mm)]
        P.wait("gpsimd", [(R["s_out"], R["s_out"].n)])
    run_phase(G, pfx, body)


def phase_outproj(G, pfx, x_in, r_in, w, x_out, nt=NT, scale=1.0):
    nth = nt // 512

    def body(P):
        R = alloc_common(P, nt)
        rT = P.sbuf("rT", [128, KC, nt], BF16)
        ps = [P.psum(f"pp{i}", [128, 512], F32) for i in range(4)]
        ps_free = [[] for _ in range(4)]
        xr = [P.sbuf(f"xr{i}", [128, 512], F32) for i in range(4)]
        xr_free = [[] for _ in range(4)]
        x_in_v = x_in.rearrange("(kc p) t -> p kc t", p=128)
        x_out_v = x_out.rearrange("(kc p) t -> p kc t", p=128)
        s_r = P.sem("s_ld0")
        nxt_w = emit_wload(P, R, w[0])
        for kc in range(KC):
            v_r = P.dma("sync", rT[:, kc, :], r_in[kc], sig=s_r)
        r_ready = [(s_r, v_r)]
        total = KC * nth

        def load_x(idx):
            pb = idx % 4
            dc, th = divmod(idx, nth)
            return P.dma("gpsimd", xr[pb][:], x_in_v[:, dc, th * 512:(th + 1) * 512], waits=xr_free[pb], sig=sem_x[pb])

        sem_x = [P.sem(n) for n in ("s_ld1", "s_tr", "s_cc", "s_pool")]
        AHEAD = 2
        v_xs = {i: load_x(i) for i in range(min(AHEAD, total))}
        cnt = 0
        for dc in range(KC):
            wt, wb, w_ready = nxt_w
            if dc + 1 < KC:
                nxt_w = emit_wload(P, R, w[dc + 1])
            for th in range(nth):
                pb = cnt % 4
                idx = cnt
                cnt += 1
                if idx + AHEAD < total:
                    v_xs[idx + AHEAD] = load_x(idx + AHEAD)
                v_x = v_xs.pop(idx)
                for kc in range(KC):
                    wts = (w_ready + r_ready + ps_free[pb]) if kc == 0 else []
                    v_mm = P.op("tensor",
                                lambda e, wt=wt, pb=pb, kc=kc, th=th: e.matmul(
                                    ps[pb][:], wt[:, kc, :], rT[:, kc, th * 512:(th + 1) * 512],
                                    start=(kc == 0), stop=(kc == KC - 1)),
                                waits=wts, sig=(R["s_pe"] if kc == KC - 1 else None))
                v_o = P.op("vector",
                           lambda e, pb=pb: e.scalar_tensor_tensor(
                               out=xr[pb][:], in0=ps[pb][:], scalar=float(scale), in1=xr[pb][:],
                               op0=ALU.mult, op1=ALU.add),
                           waits=[(R["s_pe"], v_mm), (sem_x[pb], v_x)], sig=R["s_dve"])
                ps_free[pb] = [(R["s_dve"], v_o)]
                v_st = P.dma("gpsimd", x_out_v[:, dc, th * 512:(th + 1) * 512], xr[pb][:],
                             waits=[(R["s_dve"], v_o)], sig=R["s_out"])
                xr_free[pb] = [(R["s_out"], v_st)]
            R["wbf_free"][wb] = [(R["s_pe"], v_mm)]
        P.wait("gpsimd", [(R["s_out"], R["s_out"].n)])
    run_phase(G, pfx, body)


def tile_w(w):
    n = w.shape[1] // 128
    return np.ascontiguousarray(w.reshape(KC, 128, n, 128).transpose(2, 1, 0, 3).reshape(n, 128, KC * 128))


CW = 31
NA = 8


def phase_even_conv(G, pfx, u, tail, flagf, cw, cvec, out, nt=NT):
    nth = nt // 512

    def body(P):
        R = {k: P.sem(k) for k in ("s_io", "s_out", "s_act", "s_dve", "s_pe")}
        s_ld = [P.sem("s_ld0"), P.sem("s_ld1")]
        P.auto = {"vector": R["s_dve"], "scalar": R["s_act"]}
        Y = P.sbuf("Y", [128, NA, nt], F32)
        ve = [P.sbuf(f"ve{i}", [128, 32 + nt], F32) for i in range(2)]
        ge = [P.sbuf(f"ge{i}", [128, 32 + nt], F32) for i in range(2)]
        ao = P.sbuf("ao", [128, NA, nt], BF16)
        cw_sb = P.sbuf("cw_sb", [128, NA, CW], F32)
        cv_sb = P.sbuf("cv_sb", [128, 3, NA], F32)
        ones = P.sbuf("ones", [128, 128], BF16)
        epsb = P.sbuf("epsb", [128, 1], F32)
        ybf = [P.sbuf(f"ybf{i}", [128, 512], BF16) for i in range(2)]
        ysq = [P.sbuf(f"ysq{i}", [128, 512], BF16) for i in range(2)]
        mu = P.sbuf("mu", [128, 512], F32)
        tt = P.sbuf("tt", [128, 512], F32)
        psS = P.psum("psS", [128, 512], F32)
        psQ = P.psum("psQ", [128, 512], F32)
        P.op("vector", lambda e: e.memset(epsb[:], EPS))
        P.op("vector", lambda e: e.memset(ones[:], 1.0 / (NA * 128)))
        fl_sb = P.sbuf("fl_sb", [128, 1], F32)
        P.dma("sync", cw_sb[:], cw, sig=R["s_io"])
        P.dma("sync", fl_sb[:], flagf, sig=R["s_io"])
        v_c = P.dma("sync", cv_sb[:], cvec, sig=R["s_io"])
        c_ready = [(R["s_io"], v_c)]
        buf_free = [[], []]
        for c in range(NA):
            b = c % 2
            P.dma("sync", ve[b][:, 0:32], tail[c], waits=buf_free[b], sig=s_ld[b])
            P.dma("sync", ve[b][:, 32:32 + nt], u[c], sig=s_ld[b])
            P.dma("sync", ge[b][:, 0:32], tail[NA + c], sig=s_ld[b])
            v_l = P.dma("sync", ge[b][:, 32:32 + nt], u[NA + c], sig=s_ld[b])
            v_s = P.op("scalar", lambda e, b=b: e.activation(out=ge[b][:], in_=ge[b][:], func=AF.Sigmoid),
                       waits=[(s_ld[b], v_l)])
            P.op("vector", lambda e, b=b: e.tensor_tensor(out=ve[b][:], in0=ve[b][:], in1=ge[b][:], op=ALU.mult),
                 waits=[(R["s_act"], v_s)] + c_ready)
            P.op("vector", lambda e, b=b: e.tensor_scalar_mul(out=ve[b][:, 0:32], in0=ve[b][:, 0:32], scalar1=fl_sb[:, 0:1]))
            P.op("vector", lambda e, b=b, c=c: e.tensor_scalar(
                out=Y[:, c, :], in0=ve[b][:, 2:2 + nt], scalar1=cw_sb[:, c, 0:1], scalar2=cv_sb[:, 0, c:c + 1],
                op0=ALU.mult, op1=ALU.add))
            for j in range(1, CW):
                v_y = P.op("vector", lambda e, b=b, c=c, j=j: e.scalar_tensor_tensor(
                    out=Y[:, c, :], in0=ve[b][:, 2 + j:2 + j + nt], scalar=cw_sb[:, c, j:j + 1], in1=Y[:, c, :],
                    op0=ALU.mult, op1=ALU.add))
            buf_free[b] = [(R["s_dve"], v_y)]
        y_ready = [(R["s_dve"], v_y)]
        ps_free = []
        yb_free = [[], []]
        cnt = 0
        for th in range(nth):
            sl = slice(th * 512, (th + 1) * 512)
            for c in range(NA):
                k = cnt % 2
                cnt += 1
                P.op("scalar", lambda e, k=k, c=c, sl=sl: e.activation(out=ybf[k][:], in_=Y[:, c, sl], func=AF.Copy),
                     waits=y_ready + yb_free[k])
                v_a = P.op("scalar", lambda e, k=k, c=c, sl=sl: e.activation(out=ysq[k][:], in_=Y[:, c, sl], func=AF.Square))
                P.op("tensor", lambda e, k=k, c=c: e.matmul(psS[:], ones[:], ybf[k][:], start=(c == 0), stop=(c == NA - 1)),
                     waits=[(R["s_act"], v_a)] + (ps_free if c == 0 else []))
                v_m = P.op("tensor", lambda e, k=k, c=c: e.matmul(psQ[:], ones[:], ysq[k][:], start=(c == 0), stop=(c == NA - 1)),
                           sig=R["s_pe"])
                yb_free[k] = [(R["s_pe"], v_m)]
            v_mu = P.op("scalar", lambda e: e.activation(out=mu[:], in_=psS[:], func=AF.Copy), waits=[(R["s_pe"], v_m)])
            P.op("vector", lambda e: e.tensor_tensor(out=tt[:], in0=mu[:], in1=mu[:], op=ALU.mult), waits=[(R["s_act"], v_mu)])
            v_v = P.op("vector", lambda e: e.tensor_tensor(out=tt[:], in0=psQ[:], in1=tt[:], op=ALU.subtract))
            ps_free = [(R["s_dve"], v_v)]
            v_sd = P.op("scalar", lambda e: e.activation(out=tt[:], in_=tt[:], func=AF.Sqrt, bias=epsb[:], scale=1.0),
                        waits=[(R["s_dve"], v_v)])
            P.op("vector", lambda e: e.reciprocal(out=tt[:], in_=tt[:]), waits=[(R["s_act"], v_sd)])
            for c in range(NA):
                P.op("vector", lambda e, c=c, sl=sl: e.tensor_tensor(out=Y[:, c, sl], in0=Y[:, c, sl], in1=mu[:], op=ALU.subtract))
                v_z = P.op("vector", lambda e, c=c, sl=sl: e.tensor_tensor(out=Y[:, c, sl], in0=Y[:, c, sl], in1=tt[:], op=ALU.mult))
                v_o = P.op("scalar", lambda e, c=c, sl=sl: e.activation(
                    out=ao[:, c, sl], in_=Y[:, c, sl], func=AF.Silu, scale=cv_sb[:, 1, c:c + 1], bias=cv_sb[:, 2, c:c + 1]),
                    waits=[(R["s_dve"], v_z)])
        for c in range(NA):
            P.dma("sync", out[c], ao[:, c, :], waits=[(R["s_act"], v_o)], sig=R["s_out"])
        P.wait("sync", [(R["s_out"], R["s_out"].n)])
    run_phase(G, pfx, body)


NH = 8
HD = 128


def phase_even_qk(G, pfx, uq, uk, uv, cs, gqk, prot, qT, kT, vT, nt=NT):
    nth = nt // 512

    def body(P):
        R = {k: P.sem(k) for k in ("s_io", "s_out", "s_act", "s_dve", "s_pe")}
        s_ld = [P.sem("s_ld0"), P.sem("s_ld1")]
        P.auto = {"vector": R["s_dve"], "scalar": R["s_act"]}
        raw = [P.sbuf(f"raw{i}", [128, nt], F32) for i in range(2)]
        sq = P.sbuf("sq", [128, nt], BF16)
        sd = P.sbuf("sd", [128, nt], F32)
        qn = P.sbuf("qn", [128, nt], BF16)
        t1 = P.sbuf("t1", [128, nt], F32)
        t2 = P.sbuf("t2", [128, nt], F32)
        ob = [P.sbuf(f"ob{i}", [128, nt], BF16) for i in range(2)]
        cs_sb = P.sbuf("cs_sb", [128, 2, nt], F32)
        g_sb = P.sbuf("g_sb", [128, 2], F32)
        prot_sb = P.sbuf("prot_sb", [128, 128], BF16)
        ones = P.sbuf("ones", [128, 128], BF16)
        epsb = P.sbuf("epsb", [128, 1], F32)
        psB = P.psum("psB", [128, nt], F32)
        psR = P.psum("psR", [128, nt], F32)
        P.op("vector", lambda e: e.memset(epsb[:], EPS))
        P.op("vector", lambda e: e.memset(ones[:], 1.0 / HD))
        P.dma("sync", cs_sb[:, 0, :], cs[0], sig=R["s_io"])
        P.dma("sync", cs_sb[:, 1, :], cs[1], sig=R["s_io"])
        P.dma("sync", prot_sb[:], prot, sig=R["s_io"])
        v_g = P.dma("sync", g_sb[:], gqk, sig=R["s_io"])
        v_gs = P.op("vector", lambda e: e.tensor_scalar_mul(out=g_sb[:, 0:1], in0=g_sb[:, 0:1], scalar1=float(HD ** -0.5)),
                    waits=[(R["s_io"], v_g)])
        raw_free = [[], []]
        ob_free = [[], []]
        psB_free, psR_free = [], []
        units = []
        for h in range(NH):
            units.append((uq[h], qT[h], 0, 0))
            units.append((uk[h], kT[h], 1, 0))
        all_src = [u_[0] for u_ in units] + [uv[h] for h in range(NH)]

        def load_unit(i):
            return P.dma("sync", raw[i % 2][:], all_src[i], waits=raw_free[i % 2], sig=s_ld[i % 2])

        v_l_next = load_unit(0)
        for i, (src, dst, gi, poff) in enumerate(units):
            b = i % 2
            v_l = v_l_next
            v_l_next = load_unit(i + 1)
            v_sq = P.op("scalar", lambda e, b=b: e.activation(out=sq[:], in_=raw[b][:], func=AF.Square),
                        waits=[(s_ld[b], v_l)])
            for th in range(nth):
                sl = slice(th * 512, (th + 1) * 512)
                v_m = P.op("tensor", lambda e, sl=sl: e.matmul(psB[:, sl], ones[:], sq[:, sl], start=True, stop=True),
                           waits=([(R["s_act"], v_sq)] + psB_free) if th == 0 else [],
                           sig=R["s_pe"] if th == nth - 1 else None)
            v_sd = P.op("scalar", lambda e: e.activation(out=sd[:], in_=psB[:], func=AF.Sqrt, bias=epsb[:], scale=1.0),
                        waits=[(R["s_pe"], v_m)])
            psB_free = [(R["s_act"], v_sd)]
            P.op("vector", lambda e: e.reciprocal(out=sd[:], in_=sd[:]), waits=[(R["s_act"], v_sd)])
            v_qn = P.op("vector", lambda e, b=b, gi=gi: e.scalar_tensor_tensor(
                out=qn[:], in0=raw[b][:], scalar=g_sb[:, gi:gi + 1], in1=sd[:], op0=ALU.mult, op1=ALU.mult))
            raw_free[b] = [(R["s_dve"], v_qn)]
            for th in range(nth):
                sl = slice(th * 512, (th + 1) * 512)
                v_m = P.op("tensor", lambda e, sl=sl: e.matmul(psR[:, sl], prot_sb[:], qn[:, sl], start=True, stop=True),
                           waits=([(R["s_dve"], v_qn)] + psR_free) if th == 0 else [],
                           sig=R["s_pe"] if th == nth - 1 else None)
            P.op("vector", lambda e, poff=poff: e.tensor_tensor(out=t1[:], in0=qn[:], in1=cs_sb[:, 0, poff:poff + nt], op=ALU.mult))
            v_t2 = P.op("vector", lambda e, poff=poff: e.tensor_tensor(out=t2[:], in0=psR[:], in1=cs_sb[:, 1, poff:poff + nt], op=ALU.mult),
                        waits=[(R["s_pe"], v_m)])
            psR_free = [(R["s_dve"], v_t2)]
            v_o = P.op("vector", lambda e, b=b: e.tensor_tensor(out=ob[b][:], in0=t1[:], in1=t2[:], op=ALU.add),
                       waits=ob_free[b])
            v_st = P.dma("sync", dst, ob[b][:], waits=[(R["s_dve"], v_o)], sig=R["s_out"])
            ob_free[b] = [(R["s_out"], v_st)]
        i0 = len(units)
        for j in range(NH):
            h = j
            b = (i0 + j) % 2
            v_l = v_l_next
            if i0 + j + 1 < len(all_src):
                v_l_next = load_unit(i0 + j + 1)
            v_c = P.op("scalar", lambda e, b=b: e.activation(out=ob[b][:], in_=raw[b][:], func=AF.Copy),
                       waits=[(s_ld[b], v_l)] + ob_free[b])
            raw_free[b] = [(R["s_act"], v_c)]
            v_st = P.dma("sync", vT[h], ob[b][:], waits=[(R["s_act"], v_c)], sig=R["s_out"])
            ob_free[b] = [(R["s_out"], v_st)]
        P.wait("sync", [(R["s_out"], R["s_out"].n)])
    run_phase(G, pfx, body)


def rope_tables(pos0, n):
    half = HD // 2
    inv = np.exp(-np.log(10000.0) * np.arange(half, dtype=np.float32) / half).astype(np.float32)
    pos = (pos0 + np.arange(n)).astype(np.float32)
    ang = pos[None, :] * inv[:, None]
    c = np.cos(ang).astype(np.float32)
    s_ = np.sin(ang).astype(np.float32)
    return np.ascontiguousarray(np.stack([np.concatenate([c, c], 0), np.concatenate([s_, s_], 0)], 0))


def rot_matrix():
    import ml_dtypes
    m = np.zeros((128, 128), np.float32)
    for dp in range(64):
        m[dp + 64, dp] = -1.0
        m[dp, dp + 64] = 1.0
    return m.astype(ml_dtypes.bfloat16)


DILS = (1, 4, 16)
NEG = -30000.0


def attn_units(nt):
    slots = {}
    units = []
    for d in DILS:
        nb = (2 * nt) // d // 128
        for c in range(d):
            for n in range(nb // 2 - 1, nb):
                slots[(d, c, n)] = len(slots)
            for n in range(nb // 2, nb):
                units.append((d, c, n))
    return units, slots


def phase_even_attn(G, pfx, qT, kT_own, vT_own, kT_prev, vT_prev, ident, maskb, fones, flagf, oT, nt=NT):
    units, slots = attn_units(nt)
    nslot = len(slots)

    def body(P):
        R = {k: P.sem(k) for k in ("s_io", "s_out", "s_act", "s_dve", "s_pe", "s_tr")}
        s_ld = [P.sem("s_ld0"), P.sem("s_ld1")]
        P.auto = {"vector": R["s_dve"], "scalar": R["s_act"]}
        k_sb = [P.sbuf(f"k_sb{i}", [128, 2 * nt], BF16) for i in range(2)]
        q_sb = [P.sbuf(f"q_sb{i}", [128, nt], BF16) for i in range(2)]
        v_sb = [P.sbuf(f"v_sb{i}", [128, 2 * nt], BF16) for i in range(2)]
        Vb = [P.sbuf(f"Vb{i}", [128, nslot, 128], BF16) for i in range(2)]
        acc = P.sbuf("acc", [128, 2, nt], F32)
        rden = P.sbuf("rden", [128, nt], F32)
        ob = [P.sbuf(f"ob{i}", [128, nt], BF16) for i in range(2)]
        pT = [P.sbuf(f"pT{i}", [128, 2, 128], BF16) for i in range(2)]
        id_sb = P.sbuf("id_sb", [128, 128], BF16)
        mb_sb = P.sbuf("mb_sb", [128, 2, 128], BF16)
        fo_sb = P.sbuf("fo_sb", [128, 128], BF16)
        on_sb = P.sbuf("on_sb", [128, 128], BF16)
        psT = [P.psum(f"psT{i}", [128, 8, 128], BF16) for i in range(2)]
        psS = [P.psum(f"psS{i}", [128, 2, 128], F32) for i in range(2)]
        psO = [P.psum(f"psO{i}", [128, 2, 128], F32) for i in range(2)]
        fl_sb = P.sbuf("fl_sb", [128, 1], F32)
        v_ms = P.op("vector", lambda e: e.memset(on_sb[:], 1.0))
        P.dma("sync", fl_sb[:], flagf, sig=R["s_io"])
        P.dma("sync", id_sb[:], ident, sig=R["s_io"])
        P.dma("sync", mb_sb[:], maskb, sig=R["s_io"])
        v_c = P.dma("sync", fo_sb[:], fones, sig=R["s_io"])
        c_ready = [(R["s_io"], v_c), (R["s_dve"], v_ms)]
        in_free = [[], []]
        vb_free = [[], []]
        ob_free = [[], []]
        psT_free = [[], []]
        psS_free = [[], []]
        psO_free = [[], []]
        pT_free = [[], []]
        tcnt = 0
        ucnt = 0

        def load_head(h):
            hb = h % 2
            P.dma("sync", k_sb[hb][:, 0:nt], kT_prev[h], waits=in_free[hb], sig=s_ld[hb])
            P.dma("sync", k_sb[hb][:, nt:2 * nt], kT_own[h], sig=s_ld[hb])
            P.dma("sync", q_sb[hb][:], qT[h], sig=s_ld[hb])
            P.dma("sync", v_sb[hb][:, 0:nt], vT_prev[h], sig=s_ld[hb])
            return P.dma("sync", v_sb[hb][:, nt:2 * nt], vT_own[h], sig=s_ld[hb])

        for h in range(NH):
            hb = h % 2
            if h == 0:
                v_l_next = load_head(0)
            v_l = v_l_next
            if h + 1 < NH:
                v_l_next = load_head(h + 1)
            v_fz = P.op("vector", lambda e, hb=hb: e.tensor_scalar_mul(
                out=v_sb[hb][:, 0:nt], in0=v_sb[hb][:, 0:nt], scalar1=fl_sb[:, 0:1]),
                waits=[(s_ld[hb], v_l)] + c_ready)
            h_ready = [(s_ld[hb], v_l), (R["s_dve"], v_fz)] + c_ready
            keys = list(slots.keys())
            v_ev = None
            for g0 in range(0, nslot, 8):
                grp = keys[g0:g0 + 8]
                tb = tcnt % 2
                tcnt += 1
                for j, (d, c, n) in enumerate(grp):
                    st = 128 * n * d + c
                    v_t = P.op("tensor", lambda e, tb=tb, j=j, st=st, d=d, hb=hb: e.transpose(
                        out=psT[tb][:, j, :], in_=v_sb[hb][:, st:st + 127 * d + 1:d], identity=id_sb[:]),
                        waits=(h_ready + psT_free[tb] + vb_free[hb]) if j == 0 else [],
                        sig=R["s_tr"] if j == len(grp) - 1 else None)
                v_ev = P.op("vector", lambda e, tb=tb, g0=g0, ng=len(grp), hb=hb: e.tensor_copy(
                    out=Vb[hb][:, g0:g0 + ng, :], in_=psT[tb][:, 0:ng, :]), waits=[(R["s_tr"], v_t)])
                psT_free[tb] = [(R["s_dve"], v_ev)]
            vb_ready = [(R["s_dve"], v_ev)]
            last_tr = [(R["s_tr"], v_t)]

            def emit_S(u):
                nonlocal ucnt
                d, c, n = u
                pb = ucnt % 2
                ucnt += 1
                qst = 128 * n * d + c - nt
                qsl = slice(qst, qst + 127 * d + 1, d)
                for kb in range(2):
                    kst = 128 * (n - 1 + kb) * d + c
                    P.op("tensor", lambda e, pb=pb, kb=kb, kst=kst, d=d, qsl=qsl, hb=hb: e.matmul(
                        psS[pb][:, kb, :], k_sb[hb][:, kst:kst + 127 * d + 1:d], q_sb[hb][:, qsl], start=True, stop=False),
                        waits=(h_ready + psS_free[pb]) if kb == 0 else [])
                    v_s = P.op("tensor", lambda e, pb=pb, kb=kb: e.matmul(
                        psS[pb][:, kb, :], id_sb[:], mb_sb[:, kb, :], start=False, stop=True),
                        sig=R["s_pe"] if kb == 1 else None)
                v_e = P.op("scalar", lambda e, pb=pb: e.activation(out=pT[pb][:], in_=psS[pb][:], func=AF.Exp),
                           waits=[(R["s_pe"], v_s)] + pT_free[pb])
                psS_free[pb] = [(R["s_act"], v_e)]
                return (u, pb, qsl, v_e)

            def emit_ND(st_):
                (d, c, n), pb, qsl, v_e = st_
                prev_in_prev_half = (128 * (n - 1) * d + c) < nt
                for kb in range(2):
                    sl_ = slots[(d, c, n - 1 + kb)]
                    P.op("tensor", lambda e, pb=pb, kb=kb, sl_=sl_, hb=hb: e.matmul(
                        psO[pb][:, 0, :], Vb[hb][:, sl_, :], pT[pb][:, kb, :], start=(kb == 0), stop=(kb == 1)),
                        waits=([(R["s_act"], v_e)] + vb_ready + psO_free[pb]) if kb == 0 else [])
                for kb in range(2):
                    on = fo_sb if (kb == 0 and prev_in_prev_half) else on_sb
                    v_n = P.op("tensor", lambda e, pb=pb, kb=kb, on=on: e.matmul(
                        psO[pb][:, 1, :], on[:], pT[pb][:, kb, :], start=(kb == 0), stop=(kb == 1)),
                        sig=R["s_pe"] if kb == 1 else None)
                pT_free[pb] = [(R["s_pe"], v_n)]
                if d == 1:
                    v_a = P.op("vector", lambda e, pb=pb, qsl=qsl: e.tensor_copy(out=acc[:, :, qsl], in_=psO[pb][:]),
                               waits=[(R["s_pe"], v_n)])
                else:
                    v_a = P.op("vector", lambda e, pb=pb, qsl=qsl: e.tensor_tensor(
                        out=acc[:, :, qsl], in0=psO[pb][:], in1=acc[:, :, qsl], op=ALU.add), waits=[(R["s_pe"], v_n)])
                psO_free[pb] = [(R["s_dve"], v_a)]
                return v_n

            pend = None
            v_n = None
            for u in units:
                st_ = emit_S(u)
                if pend is not None:
                    v_n = emit_ND(pend)
                pend = st_
            v_n = emit_ND(pend)
            in_free[hb] = [(R["s_pe"], v_n)] + last_tr
            vb_free[hb] = [(R["s_pe"], v_n)]
            P.op("vector", lambda e: e.reciprocal(out=rden[:], in_=acc[:, 1, :]))
            v_o = P.op("vector", lambda e, hb=hb: e.tensor_tensor(out=ob[hb][:], in0=acc[:, 0, :], in1=rden[:], op=ALU.mult),
                       waits=ob_free[hb])
            v_st = P.dma("sync", oT[h], ob[hb][:], waits=[(R["s_dve"], v_o)], sig=R["s_out"])
            ob_free[hb] = [(R["s_out"], v_st)]
        P.wait("sync", [(R["s_out"], R["s_out"].n)])
    run_phase(G, pfx, body)


def attn_consts(flag):
    import ml_dtypes
    k = np.arange(128)[:, None]
    q = np.arange(128)[None, :]
    mb = np.zeros((128, 2, 128), np.float32)
    mb[:, 0, :] = np.where(k >= q, 0.0, NEG)
    mb[:, 1, :] = np.where(k <= q, 0.0, NEG)
    return {"ident": np.eye(128, dtype=np.float32).astype(ml_dtypes.bfloat16),
            "maskb": mb.astype(ml_dtypes.bfloat16),
            "fones": np.full((128, 128), float(flag), np.float32).astype(ml_dtypes.bfloat16)}


HH = 16
CH = 64
HG_BCAST = True


def phase_hgrn(G, pfx, u, lbl, gng, ident, cmask, S0, flagf, oT, S_end, state_only, nt=NT):
    ncs = nt // CH
    full = not state_only
    LAG = 2

    def body(P):
        R = {k: P.sem(k) for k in ("s_io", "s_out", "s_act", "s_dve", "s_pe", "s_ld0", "s_ld1")}
        s_ldb = [R["s_ld0"], R["s_ld1"]]
        P.auto = {"vector": R["s_dve"], "scalar": R["s_act"]}
        nin = 2
        fz = [P.sbuf(f"fz{i}", [128, nt], F32) for i in range(nin)]
        iz = [P.sbuf(f"iz{i}", [128, nt], F32) for i in range(nin)]
        if full:
            qz = [P.sbuf(f"qz{i}", [128, nt], F32) for i in range(nin)]
            gz = [P.sbuf(f"gz{i}", [128, nt], F32) for i in range(nin)]
            e1 = P.sbuf("e1", [128, ncs, CH], F32)
            e2 = P.sbuf("e2", [128, ncs, CH], F32)
            eb = P.sbuf("eb", [128, nt], F32)
            Qt = P.sbuf("Qt", [128, nt], BF16)
            Kt = P.sbuf("Kt", [128, nt], BF16)
            Qh = [P.sbuf(f"Qh{i}", [128, nt], BF16) for i in range(2)]
            Am = P.sbuf("Am", [64, ncs, CH], BF16)
            Sb_all = P.sbuf("Sb_all", [128, ncs, 128], BF16)
            osq = P.sbuf("osq", [128, 512], BF16)
            sd = P.sbuf("sd", [128, 512], F32)
            on = P.sbuf("on", [128, 512], F32)
            ob = [P.sbuf(f"ob{i}", [128, nt], BF16) for i in range(2)]
        kk = P.sbuf("kk", [128, nt], F32)
        bb = P.sbuf("bb", [128, ncs, CH], F32)
        e3 = P.sbuf("e3", [128, ncs, CH], F32)
        el = [P.sbuf(f"el{i}", [128, ncs], F32) for i in range(2)]
        rmask = P.sbuf("rmask", [128, nt], F32)
        Kh = P.sbuf("Kh", [128, nt], BF16)
        ibf = P.sbuf("ibf", [128, nt], BF16)
        Vtok = P.sbuf("Vtok", [64, ncs, 128], BF16)
        Ktok = P.sbuf("Ktok", [64, ncs, 128], BF16)
        S = [P.sbuf(f"S{i}", [128, 128], F32) for i in range(2)]
        lb_sb = P.sbuf("lb_sb", [128, 2, HH], F32)
        lb = P.sbuf("lb", [128, HH], F32)
        oml = P.sbuf("oml", [128, HH], F32)
        gn_sb = P.sbuf("gn_sb", [128, HH], F32)
        id_sb = P.sbuf("id_sb", [128, 128], BF16)
        cm_sb = P.sbuf("cm_sb", [64, 64], F32)
        ones = P.sbuf("ones", [128, 128], BF16)
        epsb = P.sbuf("epsb", [128, 1], F32)
        fl_sb = P.sbuf("fl_sb", [128, 1], F32)
        psT = [P.psum(f"psT{i}", [64, 8, 128], BF16) for i in range(2)]
        NKB = 2
        psK_l = [P.psum(f"psK{i}", [128, 128], F32) for i in range(NKB)]
        psK_at = lambda kb: psK_l[kb][:]
        if full:
            psA = P.psum("psA", [64, 8, CH], F32)
            psO = [P.psum(f"psO{i}", [128, 512], F32) for i in range(2)]
            psN = P.psum("psN", [128, 512], F32)

        P.op("vector", lambda e: e.memset(epsb[:], EPS))
        P.op("vector", lambda e: e.memset(ones[:], 1.0 / 128))
        P.op("vector", lambda e: e.memset(rmask[:], 1.0))
        P.op("vector", lambda e: e.memset(rmask[:, 0:nt - CH + 1:CH], 0.0))
        P.dma("sync", lb_sb[:], lbl, sig=R["s_io"])
        P.dma("sync", fl_sb[:], flagf, sig=R["s_io"])
        P.dma("sync", gn_sb[:], gng, sig=R["s_io"])
        P.dma("sync", id_sb[:], ident, sig=R["s_io"])
        v_c = P.dma("sync", cm_sb[:], cmask, sig=R["s_io"])
        c_io = [(R["s_io"], v_c)]
        P.op("vector", lambda e: e.tensor_tensor(out=lb[:], in0=lb_sb[:, 1, :], in1=lb_sb[:, 0, :], op=ALU.subtract), waits=c_io)
        v_a = P.op("scalar", lambda e: e.activation(out=lb[:], in_=lb[:], func=AF.Sigmoid), waits=[(R["s_dve"], R["s_dve"].n)])
        v_c2 = P.op("vector", lambda e: e.tensor_scalar(out=oml[:], in0=lb[:], scalar1=-1.0, scalar2=1.0, op0=ALU.mult, op1=ALU.add),
                    waits=[(R["s_act"], v_a)])
        c_ready = c_io + [(R["s_dve"], v_c2)]

        in_free = [[] for _ in range(nin)]
        ob_free = [[], []]
        psT_free = [[], []]
        psA_free = []
        psO_free = [[], []]
        psK_free = [[] for _ in range(2)]
        psN_free = []
        s0_free = []
        copy_wait = [[], []]
        pe_prev = []
        tr_prev = []
        tcnt = 0
        ocnt = 0
        kcnt = 0

        def issue_loads(h):
            b = h % nin
            v = P.dma("sync", fz[b][:], u[HH + h], waits=in_free[b], sig=s_ldb[b])
            v = P.dma("sync", iz[b][:], u[2 * HH + h], sig=s_ldb[b])
            if full:
                P.dma("sync", qz[b][:], u[h], sig=s_ldb[b])
                v = P.dma("sync", gz[b][:], u[3 * HH + h], sig=s_ldb[b])
            return [(s_ldb[b], v)]

        def prologue(h, ld, info):
            b = h % nin
            fzb, izb = fz[b], iz[b]
            elb = el[h % 2]
            a1 = P.op("scalar", lambda e, fzb=fzb: e.activation(out=fzb[:], in_=fzb[:], func=AF.Sigmoid), waits=ld)
            yield
            if full:
                qzb, gzb = qz[b], gz[b]
                Qhb = Qh[h % 2]
                P.op("scalar", lambda e, qzb=qzb: e.activation(out=qzb[:], in_=qzb[:], func=AF.Silu))
                yield
                P.op("scalar", lambda e, gzb=gzb: e.activation(out=gzb[:], in_=gzb[:], func=AF.Silu))
                yield
            P.op("scalar", lambda e, izb=izb: e.activation(out=ibf[:], in_=izb[:], func=AF.Copy), waits=info["tr_prev"])
            yield
            P.op("vector", lambda e, h=h, fzb=fzb: e.tensor_scalar(out=fzb[:], in0=fzb[:], scalar1=oml[:, h:h + 1], scalar2=lb[:, h:h + 1],
                                                                   op0=ALU.mult, op1=ALU.add), waits=[(R["s_act"], a1)])
            yield
            d2 = P.op("vector", lambda e, fzb=fzb: e.tensor_scalar(out=kk[:], in0=fzb[:], scalar1=-1.0, scalar2=1.0, op0=ALU.mult, op1=ALU.add))
            yield
            a5 = P.op("scalar", lambda e, fzb=fzb: e.activation(out=fzb[:], in_=fzb[:], func=AF.Ln), waits=[(R["s_dve"], d2)])
            yield
            bb2 = bb[:].rearrange("p c t -> p (c t)")
            d3 = P.op("vector", lambda e, bb2=bb2, fzb=fzb: e.tensor_tensor_scan(out=bb2, data0=rmask[:], data1=fzb[:], initial=0.0,
                                                                               op0=ALU.mult, op1=ALU.add), waits=[(R["s_act"], a5)])
            yield
            if full:
                d4 = P.op("vector", lambda e: e.tensor_tensor(out=e1[:], in0=bb[:], in1=bb[:, :, 31:32].broadcast_to([128, ncs, CH]),
                                                              op=ALU.subtract))
                yield
            d5 = P.op("vector", lambda e: e.tensor_tensor(out=e3[:], in0=bb[:], in1=bb[:, :, CH - 1:CH].broadcast_to([128, ncs, CH]),
                                                          op=ALU.subtract))
            yield
            if full:
                P.op("scalar", lambda e, bb2=bb2: e.activation(out=eb[:], in_=bb2, func=AF.Exp), waits=[(R["s_dve"], d3)])
                yield
            P.op("scalar", lambda e, elb=elb: e.activation(out=elb[:], in_=bb[:, :, CH - 1], func=AF.Exp),
                 waits=[(R["s_dve"], d3)] + info["el_free"][h % 2])
            yield
            if full:
                P.op("scalar", lambda e: e.activation(out=e2[:], in_=e1[:], func=AF.Exp, scale=-1.0), waits=[(R["s_dve"], d4)])
                yield
                P.op("scalar", lambda e: e.activation(out=e1[:], in_=e1[:], func=AF.Exp))
                yield
            v_e = P.op("scalar", lambda e: e.activation(out=e3[:], in_=e3[:], func=AF.Exp, scale=-1.0), waits=[(R["s_dve"], d5)])
            yield
            e32 = e3[:].rearrange("p c t -> p (c t)")
            wpro = [(R["s_act"], v_e)] + info["tr_prev"] + info["a_prev"]
            if full:
                e12 = e1[:].rearrange("p c t -> p (c t)")
                e22 = e2[:].rearrange("p c t -> p (c t)")
                P.op("vector", lambda e, e12=e12, qzb=qzb: e.tensor_tensor(out=Qt[:], in0=qzb[:], in1=e12, op=ALU.mult), waits=wpro)
                yield
                P.op("vector", lambda e, e22=e22: e.tensor_tensor(out=Kt[:], in0=kk[:], in1=e22, op=ALU.mult))
                yield
                P.op("vector", lambda e, qzb=qzb, Qhb=Qhb: e.tensor_tensor(out=Qhb[:], in0=qzb[:], in1=eb[:], op=ALU.mult),
                     waits=info["qh_free"][h % 2])
                yield
            v_pro = P.op("vector", lambda e, e32=e32: e.tensor_tensor(out=Kh[:], in0=kk[:], in1=e32, op=ALU.mult), waits=wpro)
            info["pro"] = [(R["s_dve"], v_pro)]
            yield

        info = {"tr_prev": [], "a_prev": [], "el_free": [[], []], "qh_free": [[], []], "pro": None}
        ld_next = issue_loads(0)
        for _ in prologue(0, ld_next + c_ready, info):
            pass
        for h in range(HH):
            b = h % nin
            if h + 1 < HH:
                ld_next = issue_loads(h + 1)
            if full:
                gzb = gz[b]
                Qhb = Qh[h % 2]
            elb = el[h % 2]
            pro = info["pro"]
            if S0 is not None:
                v_l = P.dma("sync", S[0][:], S0[h], waits=s0_free + copy_wait[0], sig=R["s_io"])
                v_s0 = P.op("vector", lambda e: e.tensor_scalar_mul(out=S[0][:], in0=S[0][:], scalar1=fl_sb[:, 0:1]),
                            waits=[(R["s_io"], v_l)] + c_ready)
            else:
                v_s0 = P.op("vector", lambda e: e.memset(S[0][:], 0.0), waits=s0_free + copy_wait[0])
            s0_ready = [(R["s_dve"], v_s0)]
            for (src, dst) in ((ibf, Vtok), (Kh, Ktok)):
                for g0 in range(0, ncs, 8):
                    tb = tcnt % 2
                    tcnt += 1
                    for j in range(8):
                        c = g0 + j
                        v_t = P.op("tensor", lambda e, tb=tb, j=j, c=c, src=src: e.transpose(
                            out=psT[tb][:, j, :], in_=src[:, c * CH:(c + 1) * CH], identity=id_sb[:]),
                            waits=(pro + psT_free[tb]) if j == 0 else [], sig=R["s_pe"] if j == 7 else None)
                    v_ev = P.op("vector", lambda e, tb=tb, g0=g0, dst=dst: e.tensor_copy(out=dst[:, g0:g0 + 8, :], in_=psT[tb][:]),
                                waits=[(R["s_pe"], v_t)])
                    psT_free[tb] = [(R["s_dve"], v_ev)]
            tok_ready = [(R["s_dve"], v_ev)]
            info["tr_prev"] = [(R["s_pe"], v_t)]
            if full:
                for g0 in range(0, ncs, 8):
                    for j in range(8):
                        c = g0 + j
                        v_m = P.op("tensor", lambda e, j=j, c=c: e.matmul(
                            psA[:, j, :], Kt[:, c * CH:(c + 1) * CH], Qt[:, c * CH:(c + 1) * CH], start=True, stop=True),
                            waits=(pro + psA_free) if j == 0 else [], sig=R["s_pe"] if j == 7 else None)
                    for j in range(8):
                        c = g0 + j
                        v_am = P.op("vector", lambda e, j=j, c=c: e.tensor_tensor(out=Am[:, c, :], in0=psA[:, j, :], in1=cm_sb[:], op=ALU.mult),
                                    waits=[(R["s_pe"], v_m)] if j == 0 else [])
                    psA_free = [(R["s_dve"], v_am)]
                am_ready = [(R["s_dve"], v_am)]
                info["a_prev"] = [(R["s_pe"], v_m)]
                v_cp = P.op("scalar", lambda e: e.activation(out=Sb_all[:, 0, :], in_=S[0][:], func=AF.Copy),
                            waits=s0_ready + pe_prev)
                copy_wait[0] = [(R["s_act"], v_cp)]
                cp_val = {0: v_cp}
            hb = h % 2
            v_ob = None
            gen = prologue(h + 1, ld_next + c_ready, info) if h + 1 < HH else iter(())
            for c in range(ncs + LAG):
                next(gen, None)
                if c < ncs:
                    kb = kcnt % NKB
                    kcnt += 1
                    v_k = P.op("tensor", lambda e, kb=kb, c=c: e.matmul(psK_at(kb), Ktok[:, c, :], Vtok[:, c, :], start=True, stop=True),
                               waits=psK_free[kb] + (tok_ready if c == 0 else []), sig=R["s_pe"])
                    src_s, dst_s = S[c % 2], S[(c + 1) % 2]
                    v_s = P.op("vector", lambda e, kb=kb, c=c, src_s=src_s, dst_s=dst_s, elb=elb: e.scalar_tensor_tensor(
                        out=dst_s[:], in0=src_s[:], scalar=elb[:, c:c + 1], in1=psK_at(kb), op0=ALU.mult, op1=ALU.add),
                        waits=[(R["s_pe"], v_k)] + (s0_ready if c == 0 else []) + copy_wait[(c + 1) % 2])
                    psK_free[kb] = [(R["s_dve"], v_s)]
                    if full and c + 1 < ncs:
                        v_cp = P.op("scalar", lambda e, c=c, dst_s=dst_s: e.activation(out=Sb_all[:, c + 1, :], in_=dst_s[:], func=AF.Copy),
                                    waits=[(R["s_dve"], v_s)])
                        copy_wait[(c + 1) % 2] = [(R["s_act"], v_cp)]
                        cp_val[c + 1] = v_cp
                cc = c - LAG
                if full and cc >= 0:
                    j = cc % 8
                    if j == 0:
                        pb = ocnt % 2
                        ocnt += 1
                    cs_ = slice(cc * CH, (cc + 1) * CH)
                    osl = slice(j * CH, (j + 1) * CH)
                    P.op("tensor", lambda e, pb=pb, cc=cc, osl=osl: e.matmul(psO[pb][:, osl], Vtok[:, cc, :], Am[:, cc, :], start=True, stop=False),
                         waits=(am_ready + tok_ready + psO_free[pb]) if j == 0 else [])
                    v_o = P.op("tensor", lambda e, pb=pb, cc=cc, cs_=cs_, osl=osl, Qhb=Qhb: e.matmul(psO[pb][:, osl], Sb_all[:, cc, :], Qhb[:, cs_], start=False, stop=True),
                               waits=[(R["s_act"], cp_val[cc])], sig=R["s_pe"])
                    if j == 7:
                        tsl = slice((cc - 7) * CH, (cc + 1) * CH)
                        v_q = P.op("scalar", lambda e, pb=pb: e.activation(out=osq[:], in_=psO[pb][:], func=AF.Square),
                                   waits=[(R["s_pe"], v_o)])
                        v_n = P.op("tensor", lambda e: e.matmul(psN[:], ones[:], osq[:], start=True, stop=True),
                                   waits=[(R["s_act"], v_q)] + psN_free, sig=R["s_pe"])
                        v_d = P.op("scalar", lambda e: e.activation(out=sd[:], in_=psN[:], func=AF.Sqrt, bias=epsb[:], scale=1.0),
                                   waits=[(R["s_pe"], v_n)])
                        psN_free = [(R["s_act"], v_d)]
                        P.op("vector", lambda e: e.reciprocal(out=sd[:], in_=sd[:]), waits=[(R["s_act"], v_d)])
                        v_on = P.op("vector", lambda e, pb=pb, h=h: e.scalar_tensor_tensor(
                            out=on[:], in0=psO[pb][:], scalar=gn_sb[:, h:h + 1], in1=sd[:], op0=ALU.mult, op1=ALU.mult))
                        psO_free[pb] = [(R["s_dve"], v_on)]
                        v_ob = P.op("vector", lambda e, hb=hb, tsl=tsl, gzb=gzb: e.tensor_tensor(out=ob[hb][:, tsl], in0=on[:], in1=gzb[:, tsl], op=ALU.mult),
                                    waits=ob_free[hb] if cc == 7 else [])
            for _ in gen:
                pass
            s_fin = S[ncs % 2]
            info["el_free"][h % 2] = [(R["s_dve"], v_s)]
            if full:
                info["qh_free"][h % 2] = [(R["s_pe"], v_o)]
            if full:
                v_st = P.dma("sync", oT[h], ob[hb][:], waits=[(R["s_dve"], v_ob)], sig=R["s_out"])
                ob_free[hb] = [(R["s_out"], v_st)]
                pe_prev = [(R["s_pe"], v_n)]
            else:
                pe_prev = [(R["s_pe"], v_k)]
            if S_end is not None:
                v_ss = P.dma("sync", S_end[h], s_fin[:], waits=[(R["s_dve"], v_s)], sig=R["s_out"])
                s0_free = [(R["s_out"], v_ss)]
            else:
                s0_free = [(R["s_dve"], v_s)]
            in_free[b] = [(R["s_dve"], R["s_dve"].n), (R["s_act"], R["s_act"].n)] + pe_prev
        P.wait("sync", [(R["s_out"], R["s_out"].n)])
    run_phase(G, pfx, body)


def hgrn_consts():
    import ml_dtypes
    s = np.arange(64)[:, None]
    t = np.arange(64)[None, :]
    return {"ident": np.eye(128, dtype=np.float32).astype(ml_dtypes.bfloat16),
            "cmask": (s <= t).astype(np.float32)}


RG_PAIRS = [[0, 1], [2, 3], [4, 5], [6, 7]]


def phase_copy(G, pfx, pairs):
    def body(P):
        s_out = P.sem("s_out")
        for (dst, src) in pairs:
            P.dma("sync", dst, src, sig=s_out)
        P.wait("sync", [(s_out, s_out.n)])
    run_phase(G, pfx, body)


def phase_allgather(G, pfx, pairs):
    def body(P):
        s_cc = P.sem("s_cc")
        for (src, dst) in pairs:
            v = P.op("gpsimd", lambda e, src=src, dst=dst: e.collective_compute(
                "AllGather", ALU.bypass, replica_groups=RG_PAIRS, ins=[src], outs=[dst]), sig=s_cc, inc=1)
            P.wait("gpsimd", [(s_cc, v)])
    run_phase(G, pfx, body)


def build_fused(nt=NT):
    nc = bass.Bass("TRN2", target_bir_lowering=False)

    def din(name, shape, dt=F32):
        return nc.dram_tensor(name, list(shape), dt, kind="ExternalInput").ap()

    def dint(name, shape, dt=F32):
        return nc.dram_tensor(name, list(shape), dt).ap()

    x_in = din("xT", [D, nt])
    ffn_w = {}
    for l in range(2):
        for f in (1, 2):
            ffn_w[(l, f)] = (din(f"g{f}{l}", [128, KC]), din(f"wg{f}{l}", [FC, 128, KC * 128]),
                             din(f"wu{f}{l}", [FC, 128, KC * 128]), din(f"wd{f}{l}", [KC, 128, FC * 128]))
    gm = [din("gm0", [128, KC]), din("gm1", [128, KC])]
    w_ei = din("w_ei", [40, 128, KC * 128])
    w_eo = din("w_eo", [KC, 128, KC * 128])
    w_oi = din("w_oi", [64, 128, KC * 128])
    w_oo = din("w_oo", [KC, 128, KC * 128])
    cw = din("cw", [128, NA, CW])
    cvec = din("cvec", [128, 3, NA])
    cs = din("cs", [2, 128, nt])
    gqk = din("gqk", [128, 2])
    prot = din("prot", [128, 128], BF16)
    ident = din("ident", [128, 128], BF16)
    maskb = din("maskb", [128, 2, 128], BF16)
    fones = din("fones", [128, 128], BF16)
    flagf = din("flagf", [128, 1])
    lbl = din("lbl", [128, 2, HH])
    gng = din("gng", [128, HH])
    cmask = din("cmask", [64, 64])
    y_out = nc.dram_tensor("yT", [D, nt], F32, kind="ExternalOutput").ap()

    xa = dint("xa", [D, nt])
    xb = dint("xb", [D, nt])
    u = dint("u", [64, 128, nt])
    rT = dint("rT", [KC, 128, nt], BF16)
    qT = dint("qT", [NH, 128, nt], BF16)
    HP = NH // 2
    k_own = [dint(f"k_own{i}", [HP * 128, nt], BF16) for i in range(2)]
    v_own = [dint(f"v_own{i}", [HP * 128, nt], BF16) for i in range(2)]
    k_all = [dint(f"k_all{i}", [2 * HP * 128, nt], BF16) for i in range(2)]
    v_all = [dint(f"v_all{i}", [2 * HP * 128, nt], BF16) for i in range(2)]
    t_own = dint("t_own", [2 * NA * 128, 32])
    t_all = dint("t_all", [2 * 2 * NA * 128, 32])
    s_own = dint("s_own", [HH * 128, 128])
    s_all = dint("s_all", [2 * HH * 128, 128])

    def ch(ap2d, n):
        return ap2d.rearrange("(c p) f -> c p f", p=128)

    with ExitStack() as ges:
        G = Glob(nc, ges)
        phase_ffn(G, "f10", x_in, xa, *ffn_w[(0, 1)], nt=nt)
        phase_normproj(G, "np0", xa, gm[0], w_ei, u, 40, nt)
        phase_copy(G, "tl", [(ch(t_own, 2 * NA)[c], u[c][:, nt - 32:nt]) for c in range(2 * NA)])
        k_own_h = [ch(k_own[h // HP], HP)[h % HP] for h in range(NH)]
        v_own_h = [ch(v_own[h // HP], HP)[h % HP] for h in range(NH)]
        k_prev_h = [ch(k_all[h // HP], 2 * HP)[h % HP] for h in range(NH)]
        v_prev_h = [ch(v_all[h // HP], 2 * HP)[h % HP] for h in range(NH)]
        phase_even_qk(G, "qk", u[16:24], u[24:32], u[32:40], cs, gqk, prot, qT, k_own_h, v_own_h, nt)
        phase_allgather(G, "ag0", [(t_own, t_all)] + [(k_own[i], k_all[i]) for i in range(2)]
                        + [(v_own[i], v_all[i]) for i in range(2)])
        phase_even_conv(G, "cv", u, ch(t_all, 4 * NA)[0:2 * NA], flagf, cw, cvec, rT[0:NA], nt)
        phase_even_attn(G, "at", qT, k_own_h, v_own_h, k_prev_h, v_prev_h,
                        ident, maskb, fones, flagf, rT[NA:KC], nt)
        phase_outproj(G, "op0", xa, rT, w_eo, xb, nt)
        phase_ffn(G, "f20", xb, xa, *ffn_w[(0, 2)], nt=nt)
        phase_ffn(G, "f11", xa, xb, *ffn_w[(1, 1)], nt=nt)
        phase_normproj(G, "np1", xb, gm[1], w_oi, u, 64, nt)
        phase_hgrn(G, "h1", u, lbl, gng, ident, cmask, None, flagf, None, ch(s_own, HH), True, nt)
        phase_allgather(G, "ag1", [(s_own, s_all)])
        phase_hgrn(G, "h2", u, lbl, gng, ident, cmask, ch(s_all, 2 * HH)[0:HH], flagf, rT, None, False, nt)
        phase_outproj(G, "op1", xb, rT, w_oo, xa, nt)
        phase_ffn(G, "f21", xa, y_out, *ffn_w[(1, 2)], nt=nt)
    return nc


_PROGS = {}


def _prog(key, builder):
    if key not in _PROGS:
        _PROGS[key] = builder()
    return _PROGS[key]


def _vec_pk(v, n):
    return np.ascontiguousarray(v.reshape(n, 128).T)


def kernel(x, norm_ffn1, ffn1_wg, ffn1_wu, ffn1_wd, norm_mix, norm_ffn2, ffn2_wg,
           ffn2_wu, ffn2_wd, ev_w_in, ev_conv_w, ev_conv_b, ev_cn_g, ev_cn_b,
           ev_qn_g, ev_kn_g, ev_w_out, od_w_in, od_lb_logits, od_gn_g, od_w_out):
    import ml_dtypes
    f32 = np.float32
    x = np.asarray(x, f32)
    A = lambda a: np.asarray(a, f32)
    shared = {}
    fw = {1: (norm_ffn1, ffn1_wg, ffn1_wu, ffn1_wd), 2: (norm_ffn2, ffn2_wg, ffn2_wu, ffn2_wd)}
    for l in range(2):
        for f in (1, 2):
            g, wg, wu, wd = fw[f]
            shared[f"g{f}{l}"] = tile_gain(A(g)[l])
            shared[f"wg{f}{l}"] = tile_w_up(A(wg)[l])
            shared[f"wu{f}{l}"] = tile_w_up(A(wu)[l])
            shared[f"wd{f}{l}"] = tile_w_down(A(wd)[l])
    shared["gm0"] = tile_gain(A(norm_mix)[0])
    shared["gm1"] = tile_gain(A(norm_mix)[1])
    shared["w_ei"] = tile_w(A(ev_w_in)[0])
    shared["w_eo"] = tile_w(A(ev_w_out)[0])
    shared["w_oi"] = tile_w(A(od_w_in)[0])
    shared["w_oo"] = tile_w(A(od_w_out)[0])
    shared["cw"] = np.ascontiguousarray(A(ev_conv_w)[0].T.reshape(NA, 128, CW).transpose(1, 0, 2))
    shared["cvec"] = np.ascontiguousarray(np.stack([_vec_pk(A(v)[0], NA) for v in (ev_conv_b, ev_cn_g, ev_cn_b)], 1))
    shared["gqk"] = np.ascontiguousarray(np.stack([A(ev_qn_g)[0], A(ev_kn_g)[0]], 1))
    shared["prot"] = rot_matrix()
    ac = attn_consts(1.0)
    shared["ident"] = ac["ident"]
    shared["maskb"] = ac["maskb"]
    shared["lbl"] = np.ascontiguousarray(A(od_lb_logits).reshape(2, HH, 128).transpose(2, 0, 1))
    shared["gng"] = _vec_pk(A(od_gn_g)[0], HH)
    shared["cmask"] = hgrn_consts()["cmask"]

    maps = []
    for c in range(NCORES):
        half = c % 2
        m = dict(shared)
        m["xT"] = np.ascontiguousarray(x[c // 2, half * NT:(half + 1) * NT, :].T)
        m["cs"] = rope_tables(half * NT, NT)
        m["fones"] = np.full((128, 128), float(half), f32).astype(ml_dtypes.bfloat16)
        m["flagf"] = np.full((128, 1), float(half), f32)
        maps.append(m)

    nc = _prog("fused", lambda: build_fused(NT))
    res = run_bass_kernel_spmd(nc, maps, core_ids=list(range(NCORES)))
    out = np.empty_like(x)
    for c in range(NCORES):
        out[c // 2, (c % 2) * NT:(c % 2 + 1) * NT, :] = res.results[c]["yT"].T
    return out
```
